# Optimizing a Trainium2 kernel written in Bass

```python
import jax, jax.numpy as jnp
from jax import lax
import numpy as np

D_MODEL = 1024
BATCH = 16
SEQ = 2048
DEPTH = 1

POOL_WINDOWS = (2, 4, 8, 16)
POOL_WIDTH = D_MODEL // 2
POOL_GROUP = POOL_WIDTH // len(POOL_WINDOWS)
N_HEADS = 8
QK_NOPE = 64
QK_ROPE = 32
QK_HEAD = QK_NOPE + QK_ROPE
V_HEAD = 64
ATTN_WIDTH = N_HEADS * V_HEAD
Q_LORA = D_MODEL // 4
KV_LORA = D_MODEL // 8
MIX_WIDTH = POOL_WIDTH + ATTN_WIDTH
IN_SPLITS = (POOL_WIDTH, POOL_WIDTH, Q_LORA, KV_LORA, QK_ROPE, ATTN_WIDTH)
IN_WIDTH = sum(IN_SPLITS)
ROPE_THETA = 10000.0
Q_BLOCK = 128
EPS = 1e-6

kernel_name = "hybrid_pool_mla_adaln_encoder_layer"


def rmsnorm(x, g):
    xf = x.astype(jnp.float32)
    r = lax.rsqrt(jnp.mean(xf * xf, axis=-1, keepdims=True) + EPS)
    return (xf * r).astype(x.dtype) * g


def rope_tables(positions):
    inv = ROPE_THETA ** (-jnp.arange(0, QK_ROPE, 2, dtype=jnp.float32) / QK_ROPE)
    ang = positions.astype(jnp.float32)[..., None] * inv
    return jnp.cos(ang)[:, :, None, :], jnp.sin(ang)[:, :, None, :]


def apply_rope(x, cos, sin):
    x1, x2 = jnp.split(x, 2, axis=-1)
    cos = cos.astype(x.dtype)
    sin = sin.astype(x.dtype)
    return jnp.concatenate([x1 * cos - x2 * sin, x2 * cos + x1 * sin], axis=-1)


def multiscale_pool(u, pool_w, pool_scale):
    B, S, P = u.shape
    uf = u.astype(jnp.float32)
    cs = jnp.concatenate([jnp.zeros((B, 1, P), jnp.float32), jnp.cumsum(uf, axis=1)], axis=1)
    t = jnp.arange(S)
    outs = []
    for gi, w in enumerate(POOL_WINDOWS):
        lo = jnp.clip(t - w // 2, 0, S)
        hi = jnp.clip(t + (w - w // 2), 0, S)
        csg = cs[..., gi * POOL_GROUP:(gi + 1) * POOL_GROUP]
        cnt = (hi - lo).astype(jnp.float32)[None, :, None]
        mean = (jnp.take(csg, hi, axis=1) - jnp.take(csg, lo, axis=1)) / cnt
        diff = (mean - uf[..., gi * POOL_GROUP:(gi + 1) * POOL_GROUP]).astype(u.dtype)
        outs.append(jnp.einsum('bsc,cd->bsd', diff, pool_w[gi]))
    return jnp.concatenate(outs, axis=-1) * pool_scale


def latent_attention(q_lat, kv_lat, k_rope_raw, cos, sin, g_q_lat, w_uq, g_kv_lat, w_ukv, g_qnorm, g_knorm):
    B, S, _ = q_lat.shape
    q = jnp.einsum('bsr,rd->bsd', rmsnorm(q_lat, g_q_lat), w_uq).reshape(B, S, N_HEADS, QK_HEAD)
    kv = jnp.einsum('bsr,rd->bsd', rmsnorm(kv_lat, g_kv_lat), w_ukv).reshape(B, S, N_HEADS, QK_NOPE + V_HEAD)
    k_nope, v = kv[..., :QK_NOPE], kv[..., QK_NOPE:]
    k_rope = jnp.broadcast_to(k_rope_raw[:, :, None, :], (B, S, N_HEADS, QK_ROPE))
    k = jnp.concatenate([k_nope, k_rope], axis=-1)
    q = rmsnorm(q, g_qnorm)
    k = rmsnorm(k, g_knorm)
    q = jnp.concatenate([q[..., :QK_NOPE], apply_rope(q[..., QK_NOPE:], cos, sin)], axis=-1)
    k = jnp.concatenate([k[..., :QK_NOPE], apply_rope(k[..., QK_NOPE:], cos, sin)], axis=-1)
    scale = QK_HEAD ** -0.5
    nblk = S // Q_BLOCK
    qb = q.reshape(B, nblk, Q_BLOCK, N_HEADS, QK_HEAD).transpose(1, 0, 2, 3, 4)

    def attend(qblk):
        s = jnp.einsum('bqhd,bkhd->bhqk', qblk, k, preferred_element_type=jnp.float32) * scale
        p = jax.nn.softmax(s, axis=-1)
        return jnp.einsum('bhqk,bkhd->bqhd', p.astype(v.dtype), v)

    o = lax.map(attend, qb)
    return o.transpose(1, 0, 2, 3, 4).reshape(B, S, ATTN_WIDTH)


def setup_inputs(seed: int = 0) -> dict:
    key = jax.random.key(seed)
    ks = jax.random.split(key, 20)
    L, D = DEPTH, D_MODEL
    nrm = lambda k, shape, fan: jax.random.normal(k, shape, jnp.float32) * fan ** -0.5
    gain = lambda k, shape: 1.0 + 0.05 * jax.random.normal(k, shape, jnp.float32)
    return {
        "x": jax.random.normal(ks[0], (BATCH, SEQ, D), jnp.float32),
        "c": jax.random.normal(ks[1], (BATCH, D), jnp.float32),
        "positions": jnp.tile(jnp.arange(SEQ, dtype=jnp.int32)[None, :], (BATCH, 1)),
        "ada_w": nrm(ks[2], (L, D, 3 * D), D) * 0.5,
        "ada_b": 0.02 * jax.random.normal(ks[3], (L, 3 * D), jnp.float32),
        "norm_g": gain(ks[4], (L, D)),
        "w_in": nrm(ks[5], (L, D, IN_WIDTH), D),
        "pool_w": nrm(ks[6], (L, len(POOL_WINDOWS), POOL_GROUP, POOL_GROUP), POOL_GROUP),
        "pool_scale": gain(ks[7], (L, POOL_WIDTH)),
        "g_q_lat": gain(ks[8], (L, Q_LORA)),
        "w_uq": nrm(ks[9], (L, Q_LORA, N_HEADS * QK_HEAD), Q_LORA),
        "g_kv_lat": gain(ks[10], (L, KV_LORA)),
        "w_ukv": nrm(ks[11], (L, KV_LORA, N_HEADS * (QK_NOPE + V_HEAD)), KV_LORA),
        "g_qnorm": gain(ks[12], (L, QK_HEAD)),
        "g_knorm": gain(ks[13], (L, QK_HEAD)),
        "w_out": nrm(ks[14], (L, MIX_WIDTH, D), MIX_WIDTH),
    }


def reference(x, c, positions, ada_w, ada_b, norm_g, w_in, pool_w, pool_scale, g_q_lat, w_uq,
              g_kv_lat, w_ukv, g_qnorm, g_knorm, w_out):
    cos, sin = rope_tables(positions)
    offs = list(np.cumsum(IN_SPLITS)[:-1])
    h = x
    c_act = jax.nn.silu(c)
    for l in range(DEPTH):
        mod = jnp.einsum('bd,de->be', c_act, ada_w[l]) + ada_b[l]
        shift, scale, gate = jnp.split(mod, 3, axis=-1)
        xn = rmsnorm(h, norm_g[l]) * (1.0 + scale[:, None, :]) + shift[:, None, :]
        proj = jnp.einsum('bsd,de->bse', xn, w_in[l])
        u_pool, g_pool, q_lat, kv_lat, k_rope_raw, g_attn = jnp.split(proj, offs, axis=-1)
        y_pool = multiscale_pool(u_pool, pool_w[l], pool_scale[l]) * jax.nn.silu(g_pool)
        y_attn = latent_attention(q_lat, kv_lat, k_rope_raw, cos, sin, g_q_lat[l], w_uq[l],
                                  g_kv_lat[l], w_ukv[l], g_qnorm[l], g_knorm[l]) * jax.nn.silu(g_attn)
        y = jnp.einsum('bsm,md->bsd', jnp.concatenate([y_pool, y_attn], axis=-1), w_out[l])
        h = h + gate[:, None, :] * y
    return h
```

```python
import math
from contextlib import ExitStack

import numpy as np
import ml_dtypes

import concourse.bass as bass
import concourse.mybir as mybir
from concourse.bass_utils import run_bass_kernel_spmd

F32 = mybir.dt.float32
BF16 = mybir.dt.bfloat16
I32 = mybir.dt.int32
U8 = mybir.dt.uint8
AF = mybir.ActivationFunctionType
ALU = mybir.AluOpType
AX = mybir.AxisListType

PE, ACT, DVE, POOL, SP = "pe", "act", "dve", "pool", "sp"

D = 1024
KC = 8
NH = 8
QKH = 96
EPS = 1e-6
N_CORES = 8
BATCH = 16
SEQ = 2048
NCOL = 2112
POOL_W = (2, 4, 8, 16)
FILL_QK = 0


class Ins:
    __slots__ = ("eng", "fn", "deps", "needs_inc", "count", "is_dma", "dsem", "dval", "dprev", "pos")

    def __init__(self, eng, fn, is_dma):
        self.eng = eng
        self.fn = fn
        self.deps = []
        self.needs_inc = False
        self.count = 0
        self.is_dma = is_dma
        self.dsem = None
        self.dval = 0
        self.dprev = 0
        self.pos = 0


class Sched:
    def __init__(self):
        self.streams = {PE: [], ACT: [], DVE: [], POOL: [], SP: []}
        self.last_writer = {}
        self.readers = {}
        self.n = 0
        self.cur_barrier = None
        self.dmas_since_barrier = []

    def add(self, eng, _opname, *args, reads=(), writes=(), dma=False, **kw):
        if not getattr(self, "enabled", True):
            return None
        fn = (lambda e: getattr(e, _opname)(*args, **kw))
        pk = [k for k in reads if isinstance(k, str) and k.startswith("ps")]
        if pk:
            writes = list(writes) + [k for k in pk if k not in writes]
        ins = Ins(eng, fn, dma)
        ins.pos = self.n
        self.n += 1
        deps = {}
        for k in reads:
            w = self.last_writer.get(k)
            if w is not None:
                deps[id(w)] = w
        for k in writes:
            w = self.last_writer.get(k)
            if w is not None:
                deps[id(w)] = w
            for r in self.readers.get(k, ()):
                deps[id(r)] = r
        for k in reads:
            self.readers.setdefault(k, []).append(ins)
        for k in writes:
            self.last_writer[k] = ins
            self.readers[k] = []
        if self.cur_barrier is not None:
            deps[id(self.cur_barrier)] = self.cur_barrier
        for d in deps.values():
            if d is ins:
                continue
            if (not d.is_dma) and (not dma) and d.eng == eng and eng == PE:
                continue
            ins.deps.append(d)
        self.streams[eng].append(ins)
        if dma:
            self.dmas_since_barrier.append(ins)
        return ins

    def barrier(self, fn):
        if not getattr(self, "enabled", True):
            return None
        ins = Ins(DVE, fn, False)
        ins.pos = self.n
        self.n += 1
        for e, st in self.streams.items():
            for prev in reversed(st):
                if not prev.is_dma:
                    ins.deps.append(prev)
                    break
        ins.deps.extend(self.dmas_since_barrier)
        if self.cur_barrier is not None:
            ins.deps.append(self.cur_barrier)
        self.dmas_since_barrier = []
        self.streams[DVE].append(ins)
        self.cur_barrier = ins
        self.last_writer = {}
        self.readers = {}
        return ins

    def emit(self, sems, dma_sems):
        for st in self.streams.values():
            for ins in st:
                for d in ins.deps:
                    d.needs_inc = True
        for e, st in self.streams.items():
            c = 0
            for ins in st:
                if ins.is_dma:
                    continue
                if ins.needs_inc:
                    c += 1
                    ins.count = c
        alld = [i for st in self.streams.values() for i in st if i.is_dma]
        alld.sort(key=lambda i: i.pos)
        cum = [0] * len(dma_sems)
        n_sw = 4
        n_hw = len(dma_sems) - n_sw
        jc = {True: 0, False: 0}
        for ins in alld:
            sw = ins.eng == POOL
            if sw:
                s = n_hw + jc[True] % n_sw
            else:
                s = jc[False] % n_hw
            jc[sw] += 1
            ins.dsem = s
            ins.dprev = cum[s]
            cum[s] += 16
            ins.dval = cum[s]

        def run_stream(e, eng):
            waited = {}
            for ins in self.streams[e]:
                reqs = {}
                for d in ins.deps:
                    if d.is_dma:
                        key = ("d", d.dsem)
                        v = d.dval
                    else:
                        key = ("e", d.eng)
                        v = d.count
                    if v > reqs.get(key, 0):
                        reqs[key] = v
                if ins.is_dma and ins.dprev > 0:
                    key = ("d", ins.dsem)
                    if ins.dprev > reqs.get(key, 0):
                        reqs[key] = ins.dprev
                for key, v in reqs.items():
                    if waited.get(key, 0) >= v:
                        continue
                    waited[key] = v
                    sem = dma_sems[key[1]] if key[0] == "d" else sems[key[1]]
                    eng.wait_ge(sem, v)
                bi = ins.fn(eng)
                if ins.is_dma:
                    bi.then_inc(dma_sems[ins.dsem], 16)
                elif ins.needs_inc:
                    bi.then_inc(sems[e], 1)
            last = {}
            for ins in self.streams[e]:
                if ins.is_dma:
                    last[ins.dsem] = max(last.get(ins.dsem, 0), ins.dval)
            for s, v in last.items():
                if waited.get(("d", s), 0) < v:
                    eng.wait_ge(dma_sems[s], v)

        return run_stream


class Region:
    def __init__(self, base_ap, nbytes):
        self.base = base_ap
        self.nbytes = nbytes
        self.off = 0

    def reset(self, off=0):
        self.off = off

    def alloc(self, shape_free, dt):
        esz = {F32: 4, BF16: 2, I32: 4}[dt]
        n = 1
        for s in shape_free:
            n *= s
        nb = (n * esz + 31) // 32 * 32
        assert self.off + nb <= self.nbytes, (self.off, nb, self.nbytes)
        v = self.base[:, self.off:self.off + n * esz].bitcast(dt)
        self.off += nb
        if len(shape_free) == 2:
            v = v.rearrange("p (a b) -> p a b", a=shape_free[0])
        elif len(shape_free) == 3:
            v = v.rearrange("p (a b c) -> p a b c", a=shape_free[0], b=shape_free[1])
        return v


def band_consts(S):
    out = {}
    t = np.arange(S)
    for gi, w in enumerate(POOL_W):
        lo = np.clip(t - w // 2, 0, S)
        hi = np.clip(t + (w - w // 2), 0, S)
        B = np.zeros((S, S), np.float32)
        for tt in range(S):
            B[lo[tt]:hi[tt], tt] = 1.0 / float(hi[tt] - lo[tt])
            B[tt, tt] -= 1.0
        out[gi] = dict(
            first=B[0:128, 0:128], mid=B[128:256, 128:256], last=B[S - 128:S, S - 128:S],
            left=B[0:128, 128:136], right=B[256:384, 248:256])
    return out


def host_consts(S):
    bands = band_consts(S)
    bt = np.zeros((128, 4, 400), np.float32)
    for g in range(4):
        bt[:, g, 0:128] = bands[g]["first"]
        bt[:, g, 128:256] = bands[g]["mid"]
        bt[:, g, 256:384] = bands[g]["last"]
        bt[:, g, 384:392] = bands[g]["left"]
        bt[:, g, 392:400] = bands[g]["right"]
    inv = (10000.0 ** (-np.arange(0, 32, 2, dtype=np.float32) / 32.0)).astype(np.float32)
    col = np.zeros((128, 6), np.float32)
    for i in range(32):
        col[64 + i, 0] = inv[i % 16]
        col[64 + i, 1] = -1.0 if i < 16 else 1.0
    for p in range(128):
        col[p, 4] = inv[(p % 32) % 16]
        col[p, 5] = -1.0 if (p % 32) < 16 else 1.0
    col[:, 2] = EPS
    col[:, 3] = -0.5 * math.log(96.0)
    return dict(
        ident=np.eye(128, dtype=np.float32).astype(ml_dtypes.bfloat16),
        bands=bt.astype(ml_dtypes.bfloat16),
        colc=col,
    )


def host_layout(inp, S, NB, core):
    f = lambda a: np.ascontiguousarray(np.asarray(a), dtype=np.float32)
    b0 = core * NB
    x = f(inp["x"])[b0:b0 + NB]
    c = f(inp["c"])[b0:b0 + NB]
    pos = np.ascontiguousarray(np.asarray(inp["positions"]), dtype=np.int32)[b0:b0 + NB]
    w_in = f(inp["w_in"])[0]
    cols = np.concatenate([
        np.arange(0, 1408),
        np.arange(512, 576), np.arange(1408, 1424), np.arange(1424, 1440),
        np.arange(512, 576), np.arange(1424, 1440), np.arange(1408, 1424),
        np.arange(1440, 1952)])
    assert cols.size == NCOL
    w_uq = f(inp["w_uq"])[0]
    uq_cols = []
    for h in range(NH):
        uq_cols.append(np.arange(h * 96, h * 96 + 96))
    for h in range(NH):
        uq_cols.append(np.concatenate([np.arange(h * 96, h * 96 + 64), np.arange(h * 96 + 80, h * 96 + 96),
                                       np.arange(h * 96 + 64, h * 96 + 80)]))
    uq_cols = np.concatenate(uq_cols)
    w_ukv = f(inp["w_ukv"])[0]
    kv_cols = np.concatenate([np.concatenate([np.arange(h * 128 + 64, h * 128 + 128) for h in range(NH)]),
                              np.concatenate([np.arange(h * 128, h * 128 + 64) for h in range(NH)])])
    gq = f(inp["g_qnorm"])[0]
    gk = f(inp["g_knorm"])[0]
    vec = np.zeros((128, 64), np.float32)
    vec[:, 0:8] = f(inp["norm_g"])[0].reshape(8, 128).T
    vec[:, 8:32] = f(inp["ada_b"])[0].reshape(24, 128).T
    vec[:, 32:36] = f(inp["pool_scale"])[0].reshape(4, 128).T
    vec[:, 36:38] = f(inp["g_q_lat"])[0].reshape(2, 128).T
    vec[:, 38] = f(inp["g_kv_lat"])[0]
    vec[0:96, 39] = gq
    vec[0:64, 40] = gq[0:64]
    vec[64:80, 40] = gq[80:96]
    vec[80:96, 40] = gq[64:80]
    vec[64:96, 41] = gk[64:96]
    vec[64:80, 42] = gk[80:96]
    vec[80:96, 42] = gk[64:80]
    rows = np.zeros((1, 192), np.float32)
    rows[0, 0:96] = gq
    rows[0, 96:192] = gk
    return dict(
        x=x, cT=np.ascontiguousarray(c.T.reshape(KC, 128, NB).transpose(1, 0, 2)), pos=pos,
        ada_w=f(inp["ada_w"])[0], w_in=np.ascontiguousarray(w_in[:, cols]),
        w_uq=np.ascontiguousarray(w_uq[:, uq_cols]), w_ukv=np.ascontiguousarray(w_ukv[:, kv_cols]),
        w_out=f(inp["w_out"])[0], pool_w=np.ascontiguousarray(f(inp["pool_w"])[0].transpose(1, 0, 2)),
        vec=vec, rows=rows, gkn=np.ascontiguousarray(np.broadcast_to(gk[None, 0:64], (128, 64))),
        adab_gate=f(inp["ada_b"])[0][None, 2048:3072].copy(),
    )


def build(S=SEQ, NB=2):
    TT = S // 128
    NBLK = S // 512
    QW = min(1024, S)
    NQP = S // QW
    NJ = QW // 512
    nc = bass.Bass("TRN2", target_bir_lowering=False)

    def din(name, shape, dt=F32):
        return nc.dram_tensor(name, list(shape), dt, kind="ExternalInput").ap()

    x_d = din("x", [NB, S, D])
    cT_d = din("cT", [128, KC, NB])
    pos_d = din("pos", [NB, S], I32)
    adaw_d = din("ada_w", [D, 3 * D])
    win_d = din("w_in", [D, NCOL])
    wuq_d = din("w_uq", [256, 1536])
    wukv_d = din("w_ukv", [128, 1024])
    wout_d = din("w_out", [D, D])
    poolw_d = din("pool_w", [128, 4, 128])
    vec_d = din("vec", [128, 64])
    rows_d = din("rows", [1, 192])
    gkn_d = din("gkn", [128, 64])
    adabg_d = din("adab_gate", [1, 1024])
    ident_d = din("ident", [128, 128], BF16)
    bands_d = din("bands", [128, 4, 400], BF16)
    colc_d = din("colc", [128, 6])
    out_d = nc.dram_tensor("out", [NB, S, D], F32, kind="ExternalOutput").ap()
    wins_d = nc.dram_tensor("win_s", [128, KC * NCOL], BF16).ap()
    wouts_d = nc.dram_tensor("wout_s", [128, KC * D], BF16).ap()
    tabs_d = nc.dram_tensor("tab_s", [NB, 2, 32, S], BF16).ap()

    Sd = Sched()
    A = Sd.add
    import os as _os
    _stop = _os.environ.get("KSTOP", "")

    def ckpt(name):
        if name == _stop:
            Sd.enabled = False
    es = ExitStack()
    with es:
        XB = 122 * 1024
        PB = 84 * 1024
        Xt = es.enter_context(nc.sbuf_tensor("Xreg", [128, XB], U8))
        Pt = es.enter_context(nc.sbuf_tensor("Preg", [128, PB], U8))
        ps = es.enter_context(nc.psum_tensor("ps", [128, 4096], F32))
        sems = {e: es.enter_context(nc.semaphore("s_" + e)) for e in (PE, ACT, DVE, POOL, SP)}
        dsems = [es.enter_context(nc.semaphore("dq%d" % i)) for i in range(24)]
        X = Region(Xt[:], XB)
        P = Region(Pt[:], PB)

        def bank(i, n=512, off=0):
            return ps[:, i * 512 + off:i * 512 + off + n]

        wuq = P.alloc([2, 1536], BF16)
        wukv = P.alloc([1024], BF16)
        wukvg = P.alloc([512], BF16)
        poolw = P.alloc([4, 128], BF16)
        ident = P.alloc([128], BF16)
        bands = P.alloc([4, 400], BF16)
        ones_bf = P.alloc([128], BF16)
        ones_f = P.alloc([128], F32)
        vec = P.alloc([64], F32)
        colc = P.alloc([6], F32)
        rows = P.alloc([192], F32)
        negB = P.alloc([1], F32)
        cact = P.alloc([KC, NB], F32)
        mod = P.alloc([24, NB], F32)
        gs = P.alloc([KC, NB], F32)
        COS = P.alloc([S], BF16)
        SINS = P.alloc([S], BF16)
        ypT = P.alloc([4, S], BF16)
        sga = P.alloc([4, S], BF16)
        qln = P.alloc([2, S], BF16)
        kvn = P.alloc([S], BF16)
        KR = P.alloc([S], BF16)
        krstat = P.alloc([TT], F32)
        kss = P.alloc([TT, NH], F32)
        sck = P.alloc([TT, NH], F32)
        gate_bc = [P.alloc([1024], F32) for _ in range(NB)]
        rstat = P.alloc([TT, 2], F32)
        small = P.alloc([16], F32)
        P_end = P.off

        norm_g = vec[:, 0:8]
        adab = vec[:, 8:32]
        pscale = vec[:, 32:36]
        gql = vec[:, 36:38]
        gkvl = vec[:, 38:39]
        gq96 = vec[:, 39:40]
        gqs96 = vec[:, 40:41]
        gkr = vec[:, 41:42]
        gkrs = vec[:, 42:43]
        invf = colc[:, 0:1]
        sgn = colc[:, 1:2]
        epsc = colc[:, 2:3]
        lnsc = colc[:, 3:4]
        invf4 = colc[:, 4:5]
        sgn4 = colc[:, 5:6]

        rr = [0]

        def evac_eng():
            rr[0] += 1
            return ACT if rr[0] % 2 else DVE

        def copy_on(eng_name, out, in_, reads, writes):
            if eng_name == ACT:
                A(ACT, "activation", out=out, in_=in_, func=AF.Copy, reads=reads, writes=writes)
            else:
                A(eng_name, "tensor_copy", out=out, in_=in_, reads=reads, writes=writes)

        X.reset()
        A(SP, "dma_start", out=vec, in_=vec_d, writes=["vec"], dma=True)
        A(SP, "dma_start", out=colc, in_=colc_d, writes=["colc"], dma=True)
        A(SP, "dma_start", out=rows[0:1, :], in_=rows_d, writes=["rows"], dma=True)
        A(SP, "dma_start", out=ident, in_=ident_d, writes=["ident"], dma=True)
        A(SP, "dma_start", out=bands, in_=bands_d, writes=["bands"], dma=True)
        A(SP, "dma_start", out=cact, in_=cT_d, writes=["cact"], dma=True)
        NQ4 = S // 512
        posi_all = X.alloc([NB, 512], I32)
        for b in range(NB):
            for q4 in range(NQ4):
                A(SP, "dma_start", out=posi_all[32 * q4:32 * q4 + 32, b, :],
                  in_=pos_d[b:b + 1, q4 * 512:(q4 + 1) * 512].partition_broadcast(32), writes=[("posi", b)], dma=True)
        A(POOL, "memset", ones_bf, 1.0, writes=["ones_bf"])
        A(POOL, "memset", ones_f, 1.0, writes=["ones_f"])
        A(ACT, "activation", out=cact, in_=cact, func=AF.Silu, reads=["cact"], writes=["cact"])

        A(DVE, "tensor_reduce", out=small[0:1, 0:2], in_=rows[0:1, :].rearrange("p (a b) -> p a b", a=2),
                                         axis=AX.X, op=ALU.max, apply_absolute_value=True,
          reads=["rows"], writes=["small"])
        A(DVE, "scalar_tensor_tensor", out=small[0:1, 2:3], in0=small[0:1, 0:1], scalar=-math.sqrt(96.0),
                                                in1=small[0:1, 1:2], op0=ALU.mult, op1=ALU.mult,
          reads=["small"], writes=["small2"])
        A(PE, "matmul", bank(7, 1), lhsT=ones_f[0:1, :], rhs=small[0:1, 2:3], start=True, stop=True,
          reads=["small2", "ones_f"], writes=["ps7"])
        A(DVE, "tensor_copy", out=negB, in_=bank(7, 1), reads=["ps7"], writes=["negB"])

        ckpt("s1")
        stg_uq = X.alloc([2, 1536], F32)
        stg_kv = X.alloc([1024], F32)
        stg_pw = X.alloc([4, 128], F32)
        gkn = X.alloc([64], F32)
        A(SP, "dma_start", out=stg_uq, in_=wuq_d.rearrange("(k p) c -> p k c", p=128), writes=["stg_uq"], dma=True)
        A(SP, "dma_start", out=stg_kv, in_=wukv_d, writes=["stg_kv"], dma=True)
        A(SP, "dma_start", out=stg_pw, in_=poolw_d, writes=["stg_pw"], dma=True)
        A(SP, "dma_start", out=gkn, in_=gkn_d, writes=["gkn"], dma=True)
        A(DVE, "tensor_copy", out=wuq, in_=stg_uq, reads=["stg_uq"], writes=["wuq"])
        A(DVE, "tensor_copy", out=wukv, in_=stg_kv, reads=["stg_kv"], writes=["wukv"])
        A(DVE, "tensor_copy", out=poolw, in_=stg_pw, reads=["stg_pw"], writes=["poolw"])
        A(DVE, "tensor_tensor", out=wukvg.rearrange("p (h d) -> p h d", h=NH),
                                         in0=stg_kv[:, 512:1024].rearrange("p (h d) -> p h d", h=NH),
                                         in1=gkn.unsqueeze(1).to_broadcast([128, NH, 64]), op=ALU.mult,
          reads=["stg_kv", "gkn"], writes=["wukvg"])

        ckpt("s2")
        CW = 264
        NSTG = 2
        stg_w = [X.alloc([KC, CW], F32) for _ in range(NSTG)]
        stg_wb = [X.alloc([KC, CW], BF16) for _ in range(NSTG)]
        wins_v = wins_d.rearrange("p (k c) -> p k c", k=KC)
        wouts_v = wouts_d.rearrange("p (k c) -> p k c", k=KC)
        jobs = [(win_d, wins_v, ci * CW, CW) for ci in range(NCOL // CW)]
        jobs += [(wout_d, wouts_v, ci * 256, 256) for ci in range(4)]
        for j, (src, dst, c0, cw) in enumerate(jobs):
            sf = stg_w[j % NSTG]
            sbf = stg_wb[j % NSTG]
            A(SP, "dma_start", out=sf[:, :, 0:cw], in_=src[:, c0:c0 + cw].rearrange("(k p) c -> p k c", p=128),
              writes=[("stg_w", j % NSTG)], dma=True)
            copy_on(POOL, sbf[:, :, 0:cw], sf[:, :, 0:cw], [("stg_w", j % NSTG)], [("stg_wb", j % NSTG)])
            A(POOL, "dma_start", out=dst[:, :, c0:c0 + cw], in_=sbf[:, :, 0:cw],
              reads=[("stg_wb", j % NSTG)], dma=True)

        late_dve = []
        stg_ada = [X.alloc([KC, 512], F32) for _ in range(2)]
        cbc = [X.alloc([KC, 128], F32) for _ in range(NB)]
        adabg = X.alloc([1024], F32)
        A(SP, "dma_start", out=adabg, in_=adabg_d.partition_broadcast(128), writes=["adabg"], dma=True)
        for b in range(NB):
            A(DVE, "tensor_copy", out=cbc[b], in_=cact[:, :, b:b + 1].to_broadcast([128, KC, 128]),
              reads=["cact"], writes=[("cbc", b)])
        for ec in range(6):
            st = stg_ada[ec % 2]
            A(ACT, "dma_start", out=st, in_=adaw_d[:, ec * 512:(ec + 1) * 512].rearrange("(k p) c -> p k c", p=128),
              writes=[("stg_ada", ec % 2)], dma=True)
            for mt in range(4):
                m = ec * 4 + mt
                for kc in range(KC):
                    A(PE, "matmul",
                        bank(0, NB, m * NB), lhsT=st[:, kc, mt * 128:(mt + 1) * 128], rhs=cact[:, kc, :],
                        start=(kc == 0), stop=(kc == KC - 1),
                      reads=[("stg_ada", ec % 2), "cact"], writes=["ps0"])
            if ec >= 4:
                for b in range(NB):
                    bk = 1 + b * 2 + (ec - 4)
                    for kc in range(KC):
                        A(PE, "matmul",
                            bank(bk), lhsT=cbc[b][:, kc, :], rhs=st[:, kc, :], start=(kc == 0), stop=(kc == KC - 1),
                          reads=[("stg_ada", ec % 2), ("cbc", b)], writes=["ps%d" % bk])
                    late_dve.append((lambda b=b, bk=bk, ec=ec: A(DVE, "tensor_tensor",
                        out=gate_bc[b][:, (ec - 4) * 512:(ec - 3) * 512], in0=bank(bk),
                        in1=adabg[:, (ec - 4) * 512:(ec - 3) * 512], op=ALU.add,
                      reads=["ps%d" % bk, "adabg"], writes=[("gate_bc", b, ec)])))
        late_dve.append(lambda: A(DVE, "tensor_tensor", out=mod, in0=bank(0, 24 * NB).rearrange("p (m b) -> p m b", b=NB),
                                         in1=adab.unsqueeze(2).to_broadcast([128, 24, NB]), op=ALU.add,
          reads=["ps0", "vec"], writes=["mod"]))
        late_dve.append(lambda: A(DVE, "scalar_tensor_tensor", out=gs, in0=mod[:, 8:16, :], scalar=1.0,
                                                in1=norm_g.unsqueeze(2).to_broadcast([128, KC, NB]),
                                                op0=ALU.add, op1=ALU.mult,
          reads=["mod", "vec"], writes=["gs"]))

        R = slice(64, 96)
        PQ = slice(0, 32 * NQ4)
        tmpc = [(X.alloc([512], F32), X.alloc([512], F32), X.alloc([512], I32), X.alloc([512], BF16)) for _ in range(2)]
        for b in reversed(range(NB)):
            posi = posi_all[PQ, b, :]
            chains = []
            for ci, (which, shift) in enumerate((("cos", math.pi / 2), ("sin", 0.0))):
                ang, angk, angi, tout = tmpc[ci]
                ang, angk, angi, tout = ang[PQ, :], angk[PQ, :], angi[PQ, :], tout[PQ, :]
                ka, kk_, ki, ko = ("ang", ci), ("angk", ci), ("angi", ci), ("tout", ci)
                ops = [
                    (DVE, "tensor_copy", dict(out=ang, in_=posi), [("posi", b)], [ka]),
                    (DVE, "tensor_scalar", dict(out=ang, in0=ang, scalar1=invf4[PQ, :], scalar2=shift,
                                                op0=ALU.mult, op1=ALU.add), [ka, "colc"], [ka]),
                    (DVE, "tensor_scalar", dict(out=angi, in0=ang, scalar1=1.0 / (2 * math.pi), scalar2=None,
                                                op0=ALU.mult), [ka], [ki]),
                    (DVE, "tensor_copy", dict(out=angk, in_=angi), [ki], [kk_]),
                    (DVE, "scalar_tensor_tensor", dict(out=ang, in0=angk, scalar=-2 * math.pi, in1=ang,
                                                       op0=ALU.mult, op1=ALU.add), [kk_, ka], [ka]),
                    (DVE, "tensor_scalar", dict(out=angk, in0=ang, scalar1=math.pi, scalar2=None, op0=ALU.is_gt), [ka], [kk_]),
                    (DVE, "scalar_tensor_tensor", dict(out=ang, in0=angk, scalar=-2 * math.pi, in1=ang,
                                                       op0=ALU.mult, op1=ALU.add), [kk_, ka], [ka]),
                    (DVE, "tensor_scalar", dict(out=angk, in0=ang, scalar1=-math.pi, scalar2=None, op0=ALU.is_lt), [ka], [kk_]),
                    (DVE, "scalar_tensor_tensor", dict(out=ang, in0=angk, scalar=2 * math.pi, in1=ang,
                                                       op0=ALU.mult, op1=ALU.add), [kk_, ka], [ka]),
                ]
                if which == "cos":
                    ops.append((ACT, "activation", dict(out=tout, in_=ang, func=AF.Sin), [ka], [ko]))
                else:
                    ops.append((ACT, "activation", dict(out=ang, in_=ang, func=AF.Sin), [ka], [ka]))
                    ops.append((DVE, "tensor_scalar", dict(out=tout, in0=ang, scalar1=sgn4[PQ, :], scalar2=None,
                                                           op0=ALU.mult), [ka, "colc"], [ko]))
                chains.append(ops)
            for k_ in range(max(len(c) for c in chains)):
                for c in chains:
                    if k_ < len(c):
                        eng_, op_, kw_, rd_, wr_ = c[k_]
                        A(eng_, op_, reads=rd_, writes=wr_, **kw_)
            for ci, dst in enumerate((COS, SINS)):
                tout = tmpc[ci][3]
                for q4 in range(NQ4):
                    if b > 0:
                        A(SP, "dma_start", out=tabs_d[b, ci, :, q4 * 512:(q4 + 1) * 512], in_=tout[32 * q4:32 * q4 + 32, :],
                          reads=[("tout", ci)], dma=True)
                    else:
                        A(SP, "dma_start", out=dst[R, q4 * 512:(q4 + 1) * 512], in_=tout[32 * q4:32 * q4 + 32, :],
                          reads=[("tout", ci)], writes=[("tab", ci, q4)], dma=True)

        for f_ in late_dve:
            f_()
        Sd.barrier(lambda e: e.memset(small[:, 8:9], 0.0))

        ckpt("setup")
        for b in range(NB):
            X.reset()
            Wb = X.alloc([KC, NCOL], BF16)
            xnT = X.alloc([KC, S], BF16)
            mark = X.off
            xt = [X.alloc([1024], F32) for _ in range(8)]
            junk = X.alloc([1024], BF16)
            xr = [X.alloc([1024], BF16) for _ in range(2)]
            X.reset(mark)
            u_tok = X.alloc([TT, 512], BF16)
            sgp = X.alloc([4, S], BF16)
            diffT = [X.alloc([512], BF16) for _ in range(2)]
            krg = X.alloc([512], F32)
            krsg = X.alloc([512], F32)
            krsq = X.alloc([512], BF16)
            kt1 = X.alloc([512], F32)
            qlg = X.alloc([2, 512], BF16)
            qlsq = X.alloc([2, 512], BF16)
            kvg = X.alloc([512], BF16)
            kvsq = X.alloc([512], BF16)
            lnr = X.alloc([512], F32)
            Rq = X.alloc([512], BF16)

            A(POOL, "memset", rstat, 0.0, writes=["rstat0"])

            R = slice(64, 96)
            if b > 0:
                A(SP, "dma_start", out=COS[R, :], in_=tabs_d[b, 0], writes=["COS"], dma=True)
                A(SP, "dma_start", out=SINS[R, :], in_=tabs_d[b, 1], writes=["SINS"], dma=True)
            ckpt("tables")
            _ska = set(_os.environ.get("KSKIPA", "").split(","))
            for tg in range(TT // 4):
                for j in range(4):
                    tt = tg * 4 + j
                    xb = xt[tt % 8]
                    A(SP, "dma_start", out=xb, in_=x_d[b, tt * 128:(tt + 1) * 128, :],
                      writes=[("xt", tt % 8)], dma=True)
                    if tt % 2 == 1 and tt // 2 < KC:
                        A(SP, "dma_start", out=Wb[:, tt // 2, :], in_=wins_v[:, tt // 2, :], writes=[("Wb", tt // 2)], dma=True)
                    A(ACT, "activation", out=junk, in_=xb, func=AF.Square, accum_out=rstat[:, tt, 0:1],
                      reads=[("xt", tt % 8), "rstat0"], writes=["junk", ("rs", tt)])
                g4 = slice(tg * 4, tg * 4 + 4)
                A(ACT, "activation", out=rstat[:, g4, 1:2], in_=rstat[:, g4, 0:1], func=AF.Ln, bias=epsc, scale=1.0 / D,
                  reads=[("rs", tg * 4 + j_) for j_ in range(4)] + ["colc"], writes=[("rl", tg)])
                A(ACT, "activation", out=rstat[:, g4, 0:1], in_=rstat[:, g4, 1:2], func=AF.Exp, scale=-0.5,
                  reads=[("rl", tg)], writes=[("rr", tg)])
                for j in range(4):
                    tt = tg * 4 + j
                    xb = xt[tt % 8]
                    if "xr" not in _ska: A(DVE, "tensor_scalar", out=xr[tt % 2], in0=xb, scalar1=rstat[:, tt, 0:1],
                                                                   scalar2=None, op0=ALU.mult,
                      reads=[("xt", tt % 8), ("rr", tg)], writes=[("xr", tt % 2)])
                    pst = bank(j).bitcast(BF16)
                    for kc in range(KC):
                        if "tr" not in _ska: A(PE, "transpose", out=pst[:, kc * 128:(kc + 1) * 128],
                                                                           in_=xr[tt % 2][:, kc * 128:(kc + 1) * 128],
                                                                           identity=ident,
                          reads=[("xr", tt % 2), "ident"], writes=["ps%d" % j])
                if tg == TT // 4 - 1:
                    for kc in range(TT // 2, KC):
                        A(SP, "dma_start", out=Wb[:, kc, :], in_=wins_v[:, kc, :], writes=[("Wb", kc)], dma=True)
                src_all = ps[:, 0:2048].bitcast(BF16).rearrange("p (j k t) -> p j k t", j=4, k=KC)
                for kc in range(KC):
                    dst = xnT[:, kc, tg * 512:(tg + 1) * 512].rearrange("p (j t) -> p j t", j=4)
                    A(ACT, "activation", out=dst[:, 0:2, :], in_=src_all[:, 0:2, kc, :], func=AF.Identity,
                      bias=mod[:, kc, b:b + 1], scale=gs[:, kc, b:b + 1],
                      reads=["ps0", "ps1", "gs", "mod"], writes=[("xnT", kc, tg, 0)])
                    A(DVE, "tensor_scalar", out=dst[:, 2:4, :], in0=src_all[:, 2:4, kc, :], scalar1=gs[:, kc, b:b + 1],
                      scalar2=mod[:, kc, b:b + 1], op0=ALU.mult, op1=ALU.add,
                      reads=["ps2", "ps3", "gs", "mod"], writes=[("xnT", kc, tg, 1)])

            Sd.barrier(lambda e: e.memset(small[:, 8:9], 0.0))

            ckpt("A")
            xn_keys = lambda tg: [("xnT", kc, tg, hh) for kc in range(KC) for hh in range(2)]
            for tt in range(TT):
                bk = tt % 2
                for kc in range(KC):
                    A(PE, "matmul", bank(bk), lhsT=xnT[:, kc, tt * 128:(tt + 1) * 128],
                                                                  rhs=Wb[:, kc, 0:512], start=(kc == 0), stop=(kc == KC - 1),
                      reads=xn_keys(tt // 4) + [("Wb", kc)], writes=["ps%d" % bk])
                copy_on(evac_eng(), u_tok[:, tt, :], bank(bk), ["ps%d" % bk], [("u_tok", tt)])

            ckpt("Bi")
            def fm_tile(c0, m, blk, bk):
                for kc in range(KC):
                    A(PE, "matmul", bank(bk)[0:m, :], lhsT=Wb[:, kc, c0:c0 + m],
                                                    rhs=xnT[:, kc, blk * 512:(blk + 1) * 512],
                                                    start=(kc == 0), stop=(kc == KC - 1),
                      reads=xn_keys(blk) + [("Wb", kc)], writes=["ps%d" % bk])

            for blk in range(NBLK):
                cs = slice(blk * 512, (blk + 1) * 512)
                nb_ = [2]

                def nxt():
                    nb_[0] = 2 + (nb_[0] - 1) % 6
                    return nb_[0]
                for g in range(4):
                    bk = nxt()
                    fm_tile(512 + g * 128, 128, blk, bk)
                    A(ACT, "activation", out=sgp[:, g, cs], in_=bank(bk), func=AF.Silu,
                      reads=["ps%d" % bk], writes=[("sgp", g, blk)])
                for g in range(4):
                    bk = nxt()
                    fm_tile(1600 + g * 128, 128, blk, bk)
                    A(ACT, "activation", out=sga[:, g, cs], in_=bank(bk), func=AF.Silu,
                      reads=["ps%d" % bk], writes=[("sga", g, blk)])
                for j in range(2):
                    bk = nxt()
                    fm_tile(1024 + j * 128, 128, blk, bk)
                    A(DVE, "tensor_scalar", out=qlg[:, j, :], in0=bank(bk), scalar1=gql[:, j:j + 1],
                                                                 scalar2=None, op0=ALU.mult,
                      reads=["ps%d" % bk, "vec"], writes=[("qlg", j)])
                    A(ACT, "activation", out=qlsq[:, j, :], in_=bank(bk), func=AF.Square,
                      reads=["ps%d" % bk], writes=[("qlsq", j)])
                bk = nxt()
                for j in range(2):
                    A(PE, "matmul", bank(bk), lhsT=ones_bf, rhs=qlsq[:, j, :], start=(j == 0), stop=(j == 1),
                      reads=[("qlsq", j), "ones_bf"], writes=["ps%d" % bk])
                A(ACT, "activation", out=lnr, in_=bank(bk), func=AF.Ln, bias=epsc, scale=1.0 / 256,
                  reads=["ps%d" % bk, "colc"], writes=["lnr"])
                A(ACT, "activation", out=Rq, in_=lnr, func=AF.Exp, scale=-0.5, reads=["lnr"], writes=["Rq"])
                for j in range(2):
                    A(DVE, "tensor_tensor", out=qln[:, j, cs], in0=qlg[:, j, :], in1=Rq, op=ALU.mult,
                      reads=[("qlg", j), "Rq"], writes=[("qln", j, blk)])
                bk = nxt()
                fm_tile(1280, 128, blk, bk)
                A(DVE, "tensor_scalar", out=kvg, in0=bank(bk), scalar1=gkvl, scalar2=None, op0=ALU.mult,
                  reads=["ps%d" % bk, "vec"], writes=["kvg"])
                A(ACT, "activation", out=kvsq, in_=bank(bk), func=AF.Square, reads=["ps%d" % bk], writes=["kvsq"])
                bk = nxt()
                A(PE, "matmul", bank(bk), lhsT=ones_bf, rhs=kvsq, start=True, stop=True,
                  reads=["kvsq", "ones_bf"], writes=["ps%d" % bk])
                A(ACT, "activation", out=lnr, in_=bank(bk), func=AF.Ln, bias=epsc, scale=1.0 / 128,
                  reads=["ps%d" % bk, "colc"], writes=["lnr"])
                A(ACT, "activation", out=Rq, in_=lnr, func=AF.Exp, scale=-0.5, reads=["lnr"], writes=["Rq"])
                A(DVE, "tensor_tensor", out=kvn[:, cs], in0=kvg, in1=Rq, op=ALU.mult,
                  reads=["kvg", "Rq"], writes=[("kvn", blk)])
                bk = nxt()
                fm_tile(1408, 96, blk, bk)
                A(ACT, "activation", out=krg[R, :], in_=bank(bk)[R, :], func=AF.Copy, scale=gkr[R, :],
                  reads=["ps%d" % bk, "vec"], writes=["krg"])
                A(ACT, "activation", out=krsq[R, :], in_=bank(bk)[R, :], func=AF.Square,
                  reads=["ps%d" % bk], writes=["krsq"])
                bk = nxt()
                fm_tile(1504, 96, blk, bk)
                A(ACT, "activation", out=krsg[R, :], in_=bank(bk)[R, :], func=AF.Copy, scale=gkrs[R, :],
                  reads=["ps%d" % bk, "vec"], writes=["krsg"])
                A(POOL, "tensor_tensor", out=kt1[R, :], in0=krg[R, :], in1=COS[R, cs], op=ALU.mult,
                  reads=["krg", "COS"], writes=["kt1"])
                A(POOL, "tensor_tensor", out=krsg[R, :], in0=krsg[R, :], in1=SINS[R, cs], op=ALU.mult,
                  reads=["krsg", "SINS"], writes=["krsg"])
                A(POOL, "tensor_tensor", out=KR[R, cs], in0=kt1[R, :], in1=krsg[R, :], op=ALU.add,
                  reads=["kt1", "krsg"], writes=[("KR", blk)])
                for j in range(4):
                    tt = blk * 4 + j
                    A(PE, "matmul", bank(1, 1, tt), lhsT=krsq[R, j * 128:(j + 1) * 128], rhs=ones_bf[R, 0:1],
                                                    start=True, stop=True,
                      reads=["krsq", "ones_bf"], writes=["ps1"])
            A(DVE, "tensor_copy", out=krstat, in_=bank(1, TT), reads=["ps1"], writes=["krstat"])

            ckpt("Bii")
            items = [(g, blk) for g in range(4) for blk in range(NBLK)]

            def band(idx):
                g, blk = items[idx]
                bk = 2 + idx % 2
                for j in range(4):
                    tt = blk * 4 + j
                    coff = 0 if tt == 0 else (256 if tt == TT - 1 else 128)
                    has_l = tt > 0
                    has_r = tt < TT - 1
                    A(PE, "matmul", bank(bk, 128, j * 128), lhsT=u_tok[:, tt, g * 128:(g + 1) * 128], rhs=bands[:, g, coff:coff + 128],
                      start=True, stop=not (has_l or has_r), reads=[("u_tok", tt), "bands"], writes=["ps%d" % bk])
                    if has_l:
                        A(PE, "matmul", bank(bk, 8, j * 128), lhsT=u_tok[:, tt - 1, g * 128:(g + 1) * 128], rhs=bands[:, g, 384:392],
                          start=False, stop=not has_r, reads=[("u_tok", tt - 1), "bands"], writes=["ps%d" % bk])
                    if has_r:
                        A(PE, "matmul", bank(bk, 8, j * 128 + 120), lhsT=u_tok[:, tt + 1, g * 128:(g + 1) * 128], rhs=bands[:, g, 392:400],
                          start=False, stop=True, reads=[("u_tok", tt + 1), "bands"], writes=["ps%d" % bk])

            band(0)
            for idx, (g, blk) in enumerate(items):
                cs = slice(blk * 512, (blk + 1) * 512)
                bk = 2 + idx % 2
                bk2 = 4 + idx % 2
                dT = diffT[idx % 2]
                if idx + 1 < len(items):
                    band(idx + 1)
                copy_on(evac_eng(), dT, bank(bk), ["ps%d" % bk], [("diffT", idx % 2)])
                A(PE, "matmul", bank(bk2), lhsT=poolw[:, g, :], rhs=dT, start=True, stop=True,
                  reads=[("diffT", idx % 2), "poolw"], writes=["ps%d" % bk2])
                A(DVE, "scalar_tensor_tensor", out=ypT[:, g, cs], in0=bank(bk2), scalar=pscale[:, g:g + 1], in1=sgp[:, g, cs],
                  op0=ALU.mult, op1=ALU.mult, reads=["ps%d" % bk2, "vec", ("sgp", g, blk)], writes=[("ypT", g, blk)])

            Sd.barrier(lambda e: e.memset(small[:, 8:9], 0.0))

            ckpt("Biii")
            X.reset()
            Wo = X.alloc([KC, D], BF16)
            vaug_off = X.off
            Vaug = X.alloc([TT, NH, 128], BF16)
            QT = [X.alloc([S], BF16) for _ in range(2)]
            KT = [X.alloc([S], BF16) for _ in range(2)]
            PT = [X.alloc([QW], BF16) for _ in range(3)]
            Osb = X.alloc([QW], F32)
            rec = X.alloc([QW], F32)
            tg_ = X.alloc([QW], F32)
            zsb = X.alloc([512], F32)
            zssb = X.alloc([512], F32)
            zsq = X.alloc([512], BF16)
            lnF = X.alloc([512], F32)
            Fq = X.alloc([512], F32)
            qn = X.alloc([512], F32)
            t1 = X.alloc([512], F32)
            t2 = X.alloc([512], F32)
            ksqs = [X.alloc([512], F32) for _ in range(2)]
            yaT = X.alloc([4, S], BF16)
            att_end = X.off
            X.reset(vaug_off)
            xo = [X.alloc([1024], F32) for _ in range(4)]
            to = [X.alloc([1024], F32) for _ in range(2)]
            X.reset(att_end)

            for kc in range(KC):
                A(SP, "dma_start", out=Wo[:, kc, :], in_=wouts_v[:, kc, :], reads=["wscratch"], writes=[("Wo", kc)], dma=True)

            ckpt("KV")
            def produce_stages(h, blk):
                cs = slice(blk * 512, (blk + 1) * 512)
                q_ = QT[h % 2]
                k_ = KT[h % 2]
                qk = ("QT", h % 2)
                kk = ("KT", h % 2)
                hp = h // 2
                rs = slice(0, 64) if h % 2 == 0 else slice(64, 128)

                def st0():
                    for j in range(2):
                        A(PE, "matmul", bank(6)[0:96, :], lhsT=wuq[:, j, h * 96:(h + 1) * 96], rhs=qln[:, j, cs],
                          start=(j == 0), stop=(j == 1), reads=[("qln", j, blk), "wuq"], writes=["ps6"])
                    for j in range(2):
                        A(PE, "matmul", bank(7)[0:96, :], lhsT=wuq[:, j, 768 + h * 96:768 + (h + 1) * 96], rhs=qln[:, j, cs],
                          start=(j == 0), stop=(j == 1), reads=[("qln", j, blk), "wuq"], writes=["ps7"])
                    A(DVE, "tensor_copy", out=zsb[0:96, :], in_=bank(6)[0:96, :], reads=["ps6"], writes=["zsb"])
                    A(DVE, "tensor_copy", out=zssb[R, :], in_=bank(7)[R, :], reads=["ps7"], writes=["zssb"])
                    A(DVE, "tensor_tensor", out=zsq[0:96, :], in0=zsb[0:96, :], in1=zsb[0:96, :], op=ALU.mult,
                      reads=["zsb"], writes=["zsq"])

                def st1():
                    A(PE, "matmul", bank(6)[0:96, :], lhsT=ones_bf[0:96, 0:96], rhs=zsq[0:96, :], start=True, stop=True,
                      reads=["zsq", "ones_bf"], writes=["ps6"])
                    A(PE, "matmul", bank(7), lhsT=wukvg[:, hp * 128:(hp + 1) * 128], rhs=kvn[:, cs], start=True, stop=True,
                      reads=[("kvn", blk), "wukvg"], writes=["ps7"])
                    A(DVE, "tensor_copy", out=k_[0:64, cs], in_=bank(7)[rs, :], reads=["ps7"], writes=[kk])
                    A(SP, "dma_start", out=k_[R, cs], in_=KR[R, cs], reads=[("KR", blk), kk], writes=[kk], dma=True)

                def st2():
                    A(ACT, "activation", out=lnF[0:96, :], in_=bank(6)[0:96, :], func=AF.Ln, bias=epsc[0:96, :], scale=1.0 / QKH,
                      reads=["ps6", "colc"], writes=["lnF"])
                    A(ACT, "activation", out=Fq[0:96, :], in_=lnF[0:96, :], func=AF.Exp, scale=-0.5, reads=["lnF"], writes=["Fq"])

                def st3():
                    A(DVE, "scalar_tensor_tensor", out=qn[0:96, :], in0=zsb[0:96, :], scalar=gq96[0:96, :], in1=Fq[0:96, :],
                      op0=ALU.mult, op1=ALU.mult, reads=["zsb", "Fq", "vec"], writes=["qn"])
                    A(DVE, "scalar_tensor_tensor", out=t2[R, :], in0=zssb[R, :], scalar=gqs96[R, :], in1=Fq[R, :],
                      op0=ALU.mult, op1=ALU.mult, reads=["zssb", "Fq", "vec"], writes=["t2"])
                    A(POOL, "tensor_copy", out=q_[0:64, cs], in_=qn[0:64, :], reads=["qn"], writes=[qk])
                    A(POOL, "tensor_tensor", out=t1[R, :], in0=qn[R, :], in1=COS[R, cs], op=ALU.mult,
                      reads=["qn", "COS"], writes=["t1"])
                    A(POOL, "tensor_tensor", out=t2[R, :], in0=t2[R, :], in1=SINS[R, cs], op=ALU.mult,
                      reads=["t2", "SINS"], writes=["t2"])
                    A(POOL, "tensor_tensor", out=q_[R, cs], in0=t1[R, :], in1=t2[R, :], op=ALU.add,
                      reads=["t1", "t2", qk], writes=[qk])
                return [st0, st1, st2, st3]

            NG = NQP * TT
            groups = [(h, qp, kt) for h in range(NH) for qp in range(NQP) for kt in range(TT)]
            gap = max(4, (NG - 1) // NBLK)
            assert NBLK * 4 <= NG, (NBLK, NG)

            def emit_qk(i):
                h, qp, kt = groups[i]
                q_ = QT[h % 2]
                k_ = KT[h % 2]
                sb_ = i % 2
                sbk = [0, 2][sb_]
                for j in ([0] * FILL_QK + list(range(NJ))):
                    q0 = qp * QW + j * 512
                    A(PE, "matmul", bank(sbk + j), lhsT=k_[0:96, kt * 128:(kt + 1) * 128], rhs=q_[0:96, q0:q0 + 512],
                      start=True, stop=True, reads=[("QT", h % 2), ("KT", h % 2)], writes=["ps%d" % (sbk + j)])

            p0_list = [st for blk in range(NBLK) for st in produce_stages(0, blk)]
            for tt in range(TT):
                for k_ in range(len(p0_list)):
                    if (k_ * TT) // len(p0_list) == tt:
                        p0_list[k_]()
                ba, bb = (tt % 3) * 2, (tt % 3) * 2 + 1
                ksq = ksqs[tt % 2]
                A(PE, "matmul", bank(ba), lhsT=kvn[:, tt * 128:(tt + 1) * 128], rhs=wukv[:, 0:512], start=True, stop=True,
                  reads=[("kvn", tt // 4), "wukv"], writes=["ps%d" % ba])
                A(PE, "matmul", bank(bb), lhsT=kvn[:, tt * 128:(tt + 1) * 128], rhs=wukv[:, 512:1024], start=True, stop=True,
                  reads=[("kvn", tt // 4), "wukv"], writes=["ps%d" % bb])
                A(POOL, "memset", Vaug[:, tt], 1.0, writes=[("Vones", tt)])
                vsrc = bank(ba).rearrange("p (h d) -> p h d", h=NH)
                A(ACT, "activation", out=Vaug[:, tt, 0:NH:2, 0:64], in_=vsrc[:, 0:NH:2, :], func=AF.Copy,
                  reads=["ps%d" % ba, ("Vones", tt)], writes=[("V", tt, 0)])
                A(ACT, "activation", out=Vaug[:, tt, 1:NH:2, 64:128], in_=vsrc[:, 1:NH:2, :], func=AF.Copy,
                  reads=["ps%d" % ba, ("Vones", tt)], writes=[("V", tt, 1)])
                A(ACT, "activation", out=ksq, in_=bank(bb), func=AF.Square, reads=["ps%d" % bb], writes=[("ksq", tt % 2)])
                A(DVE, "tensor_reduce", out=kss[:, tt, :], in_=ksq.rearrange("p (h d) -> p h d", h=NH),
                                                        axis=AX.X, op=ALU.add, reads=[("ksq", tt % 2)], writes=[("kss", tt)])
            A(DVE, "tensor_tensor", out=kss, in0=kss, in1=krstat.unsqueeze(2).to_broadcast([128, TT, NH]), op=ALU.add,
              reads=[("kss", tt) for tt in range(TT)] + ["krstat"], writes=["kss_all"])
            A(ACT, "activation", out=kss, in_=kss, func=AF.Ln, bias=epsc, scale=1.0 / QKH, reads=["kss_all", "colc"], writes=["kss_ln"])
            A(ACT, "activation", out=sck, in_=kss, func=AF.Exp, bias=lnsc, scale=-0.5, reads=["kss_ln", "colc"], writes=["sck"])

            emit_qk(0)
            if len(groups) > 1:
                emit_qk(1)
            pending = []
            for i, (h, qp, kt) in enumerate(groups):
                sb_ = i % 2
                sbk = [0, 2][sb_]
                pb_ = i % 3
                A(ACT, "activation", out=PT[pb_], in_=ps[:, sbk * 512:sbk * 512 + QW], func=AF.Exp, bias=negB,
                  scale=sck[:, kt, h:h + 1], reads=["ps%d" % (sbk + j_) for j_ in range(NJ)] + ["sck", "negB"], writes=[("PT", pb_)])
                if i + 2 < len(groups):
                    emit_qk(i + 2)
                for j in range(NJ):
                    A(PE, "matmul", bank(4 + j), lhsT=Vaug[:, kt, h, :], rhs=PT[pb_][:, j * 512:(j + 1) * 512],
                      start=(kt == 0), stop=(kt == TT - 1), reads=[("PT", pb_), ("V", kt, h % 2)], writes=["ps%d" % (4 + j)])
                if kt == TT - 1:
                    while pending:
                        pending.pop(0)[1]()
                    for j in range(NJ):
                        A(DVE, "tensor_copy", out=Osb[:, j * 512:(j + 1) * 512], in_=bank(4 + j), reads=["ps%d" % (4 + j)],
                          writes=[("Osb", j)])
                    cq0 = qp * QW
                    nr = slice(0, 64) if h % 2 == 0 else slice(64, 128)
                    dr = slice(64, 128) if h % 2 == 0 else slice(0, 64)
                    NCH = QW // 256

                    def mk_rec(c, nr=nr, dr=dr):
                        def f():
                            cc = slice(c * 256, (c + 1) * 256)
                            A(DVE, "reciprocal", out=rec[nr, cc], in_=Osb[dr, cc], reads=[("Osb", c // 2)], writes=[("rec", c)])
                        return f

                    def mk_tg(c, nr=nr, h=h, cq0=cq0):
                        def f():
                            cc = slice(c * 256, (c + 1) * 256)
                            A(POOL, "tensor_tensor", out=tg_[nr, cc], in0=rec[nr, cc], in1=sga[nr, h // 2, cq0 + c * 256:cq0 + (c + 1) * 256],
                              op=ALU.mult, reads=[("rec", c)] + [("sga", h // 2, bl) for bl in range(NBLK)], writes=[("tg", c)])
                        return f

                    def mk_fin(c, nr=nr, h=h, qp=qp, cq0=cq0):
                        def f():
                            cc = slice(c * 256, (c + 1) * 256)
                            A(DVE, "tensor_tensor", out=yaT[nr, h // 2, cq0 + c * 256:cq0 + (c + 1) * 256], in0=Osb[nr, cc], in1=tg_[nr, cc],
                              op=ALU.mult, reads=[("Osb", c // 2), ("tg", c)], writes=[("yaT", h, qp, c)])
                        return f
                    offs = [3, 4, 5, 10, 11, 12]
                    for c in range(NCH + 1):
                        fs = []
                        if c < NCH:
                            fs += [mk_rec(c), mk_tg(c)]
                        if c >= 1:
                            fs.append(mk_fin(c - 1))
                        for f in fs:
                            pending.append((i + offs[min(c, len(offs) - 1)], f))
                while pending and pending[0][0] <= i:
                    pending.pop(0)[1]()
                gi = qp * TT + kt
                if h + 1 < NH:
                    for blk in range(NBLK):
                        for si in range(4):
                            so = (0, 3, 5, 6)[si] if gap >= 7 else si
                            if min(blk * gap + so, NG - 3) == gi:
                                produce_stages(h + 1, blk)[si]()

            while pending:
                pending.pop(0)[1]()
            ckpt("ATT")
            Sd.barrier(lambda e: e.memset(small[:, 8:9], 0.0))
            for tt in range(TT):
                xb = xo[tt % 4]
                tb = to[tt % 2]
                A(SP, "dma_start", out=xb, in_=x_d[b, tt * 128:(tt + 1) * 128, :],
                  writes=[("xo", tt % 4)], dma=True)
                for n in range(2):
                    bk = (tt % 2) * 2 + n
                    for c in range(KC):
                        src = ypT[:, c, tt * 128:(tt + 1) * 128] if c < 4 else yaT[:, c - 4, tt * 128:(tt + 1) * 128]
                        rk = [("ypT", c, tt // 4)] if c < 4 else [("yaT", 2 * (c - 4) + hh_, tt * 128 // QW, (tt * 128 % QW) // 256) for hh_ in range(2)]
                        A(PE, "matmul", bank(bk), lhsT=src, rhs=Wo[:, c, n * 512:(n + 1) * 512],
                                                                          start=(c == 0), stop=(c == KC - 1),
                          reads=rk + [("Wo", c)], writes=["ps%d" % bk])
                    A(DVE, "tensor_tensor", out=tb[:, n * 512:(n + 1) * 512], in0=bank(bk),
                                                                        in1=gate_bc[b][:, n * 512:(n + 1) * 512], op=ALU.mult,
                      reads=["ps%d" % bk, ("gate_bc", b, 4), ("gate_bc", b, 5)], writes=[("to", tt % 2, n)])
                A(POOL, "tensor_tensor", out=tb, in0=tb, in1=xb, op=ALU.add,
                  reads=[("to", tt % 2, 0), ("to", tt % 2, 1), ("xo", tt % 4)], writes=[("to", tt % 2, 0), ("to", tt % 2, 1)])
                A(POOL, "dma_start", out=out_d[b, tt * 128:(tt + 1) * 128, :], in_=tb,
                  reads=[("to", tt % 2, 0), ("to", tt % 2, 1)], dma=True)

            Sd.barrier(lambda e: e.memset(small[:, 8:9], 0.0))

        run = Sd.emit(sems, dsems)
        with nc.Block() as block:
            @block.sync
            def _(e):
                run(SP, e)

            @block.tensor
            def _(e):
                run(PE, e)

            @block.scalar
            def _(e):
                run(ACT, e)

            @block.vector
            def _(e):
                run(DVE, e)

            @block.gpsimd
            def _(e):
                run(POOL, e)
    return nc


def make_in_maps(inputs, S, NB, n_cores):
    hc = host_consts(S)
    maps = []
    for core in range(n_cores):
        m = host_layout(inputs, S, NB, core)
        m.update(hc)
        maps.append(m)
    return maps


def kernel(**inputs):
    NB = BATCH // N_CORES
    nc = build(SEQ, NB)
    in_maps = make_in_maps(inputs, SEQ, NB, N_CORES)
    res = run_bass_kernel_spmd(nc, in_maps, core_ids=list(range(N_CORES)))
    out = np.concatenate([np.asarray(r["out"]) for r in res.results], axis=0)
    return out.astype(np.float32)
```

```python
import math
from contextlib import ExitStack

import numpy as np
import ml_dtypes

import concourse.bass as bass
import concourse.mybir as mybir
from concourse.bass_utils import run_bass_kernel_spmd

F32 = mybir.dt.float32
BF16 = mybir.dt.bfloat16
I32 = mybir.dt.int32
U8 = mybir.dt.uint8
AF = mybir.ActivationFunctionType
ALU = mybir.AluOpType
AX = mybir.AxisListType

PE, ACT, DVE, POOL, SP = "pe", "act", "dve", "pool", "sp"

D = 1024
KC = 8
NH = 8
QKH = 96
EPS = 1e-6
N_CORES = 8
BATCH = 16
SEQ = 2048
NCOL = 2112
POOL_W = (2, 4, 8, 16)
FILL_QK = 0


class Ins:
    __slots__ = ("eng", "fn", "deps", "needs_inc", "count", "is_dma", "dsem", "dval", "dprev", "pos")

    def __init__(self, eng, fn, is_dma):
        self.eng = eng
        self.fn = fn
        self.deps = []
        self.needs_inc = False
        self.count = 0
        self.is_dma = is_dma
        self.dsem = None
        self.dval = 0
        self.dprev = 0
        self.pos = 0


class Sched:
    def __init__(self):
        self.streams = {PE: [], ACT: [], DVE: [], POOL: [], SP: []}
        self.last_writer = {}
        self.readers = {}
        self.n = 0
        self.cur_barrier = None
        self.dmas_since_barrier = []

    def add(self, eng, _opname, *args, reads=(), writes=(), dma=False, **kw):
        if not getattr(self, "enabled", True):
            return None
        fn = (lambda e: getattr(e, _opname)(*args, **kw))
        pk = [k for k in reads if isinstance(k, str) and k.startswith("ps")]
        if pk:
            writes = list(writes) + [k for k in pk if k not in writes]
        ins = Ins(eng, fn, dma)
        ins.pos = self.n
        self.n += 1
        deps = {}
        for k in reads:
            w = self.last_writer.get(k)
            if w is not None:
                deps[id(w)] = w
        for k in writes:
            w = self.last_writer.get(k)
            if w is not None:
                deps[id(w)] = w
            for r in self.readers.get(k, ()):
                deps[id(r)] = r
        for k in reads:
            self.readers.setdefault(k, []).append(ins)
        for k in writes:
            self.last_writer[k] = ins
            self.readers[k] = []
        if self.cur_barrier is not None:
            deps[id(self.cur_barrier)] = self.cur_barrier
        for d in deps.values():
            if d is ins:
                continue
            if (not d.is_dma) and (not dma) and d.eng == eng and eng == PE:
                continue
            ins.deps.append(d)
        self.streams[eng].append(ins)
        if dma:
            self.dmas_since_barrier.append(ins)
        return ins

    def barrier(self, fn):
        if not getattr(self, "enabled", True):
            return None
        ins = Ins(DVE, fn, False)
        ins.pos = self.n
        self.n += 1
        for e, st in self.streams.items():
            for prev in reversed(st):
                if not prev.is_dma:
                    ins.deps.append(prev)
                    break
        ins.deps.extend(self.dmas_since_barrier)
        if self.cur_barrier is not None:
            ins.deps.append(self.cur_barrier)
        self.dmas_since_barrier = []
        self.streams[DVE].append(ins)
        self.cur_barrier = ins
        self.last_writer = {}
        self.readers = {}
        return ins

    def emit(self, sems, dma_sems):
        for st in self.streams.values():
            for ins in st:
                for d in ins.deps:
                    d.needs_inc = True
        for e, st in self.streams.items():
            c = 0
            for ins in st:
                if ins.is_dma:
                    continue
                if ins.needs_inc:
                    c += 1
                    ins.count = c
        alld = [i for st in self.streams.values() for i in st if i.is_dma]
        alld.sort(key=lambda i: i.pos)
        cum = [0] * len(dma_sems)
        n_sw = 4
        n_hw = len(dma_sems) - n_sw
        jc = {True: 0, False: 0}
        for ins in alld:
            sw = ins.eng == POOL
            if sw:
                s = n_hw + jc[True] % n_sw
            else:
                s = jc[False] % n_hw
            jc[sw] += 1
            ins.dsem = s
            ins.dprev = cum[s]
            cum[s] += 16
            ins.dval = cum[s]

        def run_stream(e, eng):
            waited = {}
            for ins in self.streams[e]:
                reqs = {}
                for d in ins.deps:
                    if d.is_dma:
                        key = ("d", d.dsem)
                        v = d.dval
                    else:
                        key = ("e", d.eng)
                        v = d.count
                    if v > reqs.get(key, 0):
                        reqs[key] = v
                if ins.is_dma and ins.dprev > 0:
                    key = ("d", ins.dsem)
                    if ins.dprev > reqs.get(key, 0):
                        reqs[key] = ins.dprev
                for key, v in reqs.items():
                    if waited.get(key, 0) >= v:
                        continue
                    waited[key] = v
                    sem = dma_sems[key[1]] if key[0] == "d" else sems[key[1]]
                    eng.wait_ge(sem, v)
                bi = ins.fn(eng)
                if ins.is_dma:
                    bi.then_inc(dma_sems[ins.dsem], 16)
                elif ins.needs_inc:
                    bi.then_inc(sems[e], 1)
            last = {}
            for ins in self.streams[e]:
                if ins.is_dma:
                    last[ins.dsem] = max(last.get(ins.dsem, 0), ins.dval)
            for s, v in last.items():
                if waited.get(("d", s), 0) < v:
                    eng.wait_ge(dma_sems[s], v)

        return run_stream


class Region:
    def __init__(self, base_ap, nbytes):
        self.base = base_ap
        self.nbytes = nbytes
        self.off = 0

    def reset(self, off=0):
        self.off = off

    def alloc(self, shape_free, dt):
        esz = {F32: 4, BF16: 2, I32: 4}[dt]
        n = 1
        for s in shape_free:
            n *= s
        nb = (n * esz + 31) // 32 * 32
        assert self.off + nb <= self.nbytes, (self.off, nb, self.nbytes)
        v = self.base[:, self.off:self.off + n * esz].bitcast(dt)
        self.off += nb
        if len(shape_free) == 2:
            v = v.rearrange("p (a b) -> p a b", a=shape_free[0])
        elif len(shape_free) == 3:
            v = v.rearrange("p (a b c) -> p a b c", a=shape_free[0], b=shape_free[1])
        return v


def band_consts(S):
    out = {}
    t = np.arange(S)
    for gi, w in enumerate(POOL_W):
        lo = np.clip(t - w // 2, 0, S)
        hi = np.clip(t + (w - w // 2), 0, S)
        B = np.zeros((S, S), np.float32)
        for tt in range(S):
            B[lo[tt]:hi[tt], tt] = 1.0 / float(hi[tt] - lo[tt])
            B[tt, tt] -= 1.0
        out[gi] = dict(
            first=B[0:128, 0:128], mid=B[128:256, 128:256], last=B[S - 128:S, S - 128:S],
            left=B[0:128, 128:136], right=B[256:384, 248:256])
    return out


def host_consts(S):
    bands = band_consts(S)
    bt = np.zeros((128, 4, 400), np.float32)
    for g in range(4):
        bt[:, g, 0:128] = bands[g]["first"]
        bt[:, g, 128:256] = bands[g]["mid"]
        bt[:, g, 256:384] = bands[g]["last"]
        bt[:, g, 384:392] = bands[g]["left"]
        bt[:, g, 392:400] = bands[g]["right"]
    inv = (10000.0 ** (-np.arange(0, 32, 2, dtype=np.float32) / 32.0)).astype(np.float32)
    col = np.zeros((128, 6), np.float32)
    for i in range(32):
        col[64 + i, 0] = inv[i % 16]
        col[64 + i, 1] = -1.0 if i < 16 else 1.0
    for p in range(128):
        col[p, 4] = inv[(p % 32) % 16]
        col[p, 5] = -1.0 if (p % 32) < 16 else 1.0
    col[:, 2] = EPS
    col[:, 3] = -0.5 * math.log(96.0)
    return dict(
        ident=np.eye(128, dtype=np.float32).astype(ml_dtypes.bfloat16),
        bands=bt.astype(ml_dtypes.bfloat16),
        colc=col,
    )


def host_layout(inp, S, NB, core):
    f = lambda a: np.ascontiguousarray(np.asarray(a), dtype=np.float32)
    b0 = core * NB
    x = f(inp["x"])[b0:b0 + NB]
    c = f(inp["c"])[b0:b0 + NB]
    pos = np.ascontiguousarray(np.asarray(inp["positions"]), dtype=np.int32)[b0:b0 + NB]
    w_in = f(inp["w_in"])[0]
    cols = np.concatenate([
        np.arange(0, 1408),
        np.arange(512, 576), np.arange(1408, 1424), np.arange(1424, 1440),
        np.arange(512, 576), np.arange(1424, 1440), np.arange(1408, 1424),
        np.arange(1440, 1952)])
    assert cols.size == NCOL
    w_uq = f(inp["w_uq"])[0]
    uq_cols = []
    for h in range(NH):
        uq_cols.append(np.arange(h * 96, h * 96 + 96))
    for h in range(NH):
        uq_cols.append(np.concatenate([np.arange(h * 96, h * 96 + 64), np.arange(h * 96 + 80, h * 96 + 96),
                                       np.arange(h * 96 + 64, h * 96 + 80)]))
    uq_cols = np.concatenate(uq_cols)
    w_ukv = f(inp["w_ukv"])[0]
    kv_cols = np.concatenate([np.concatenate([np.arange(h * 128 + 64, h * 128 + 128) for h in range(NH)]),
                              np.concatenate([np.arange(h * 128, h * 128 + 64) for h in range(NH)])])
    gq = f(inp["g_qnorm"])[0]
    gk = f(inp["g_knorm"])[0]
    vec = np.zeros((128, 64), np.float32)
    vec[:, 0:8] = f(inp["norm_g"])[0].reshape(8, 128).T
    vec[:, 8:32] = f(inp["ada_b"])[0].reshape(24, 128).T
    vec[:, 32:36] = f(inp["pool_scale"])[0].reshape(4, 128).T
    vec[:, 36:38] = f(inp["g_q_lat"])[0].reshape(2, 128).T
    vec[:, 38] = f(inp["g_kv_lat"])[0]
    vec[0:96, 39] = gq
    vec[0:64, 40] = gq[0:64]
    vec[64:80, 40] = gq[80:96]
    vec[80:96, 40] = gq[64:80]
    vec[64:96, 41] = gk[64:96]
    vec[64:80, 42] = gk[80:96]
    vec[80:96, 42] = gk[64:80]
    rows = np.zeros((1, 192), np.float32)
    rows[0, 0:96] = gq
    rows[0, 96:192] = gk
    return dict(
        x=x, cT=np.ascontiguousarray(c.T.reshape(KC, 128, NB).transpose(1, 0, 2)), pos=pos,
        ada_w=f(inp["ada_w"])[0], w_in=np.ascontiguousarray(w_in[:, cols]),
        w_uq=np.ascontiguousarray(w_uq[:, uq_cols]), w_ukv=np.ascontiguousarray(w_ukv[:, kv_cols]),
        w_out=f(inp["w_out"])[0], pool_w=np.ascontiguousarray(f(inp["pool_w"])[0].transpose(1, 0, 2)),
        vec=vec, rows=rows, gkn=np.ascontiguousarray(np.broadcast_to(gk[None, 0:64], (128, 64))),
        adab_gate=f(inp["ada_b"])[0][None, 2048:3072].copy(),
    )


def build(S=SEQ, NB=2):
    TT = S // 128
    NBLK = S // 512
    QW = min(1024, S)
    NQP = S // QW
    NJ = QW // 512
    nc = bass.Bass("TRN2", target_bir_lowering=False)

    def din(name, shape, dt=F32):
        return nc.dram_tensor(name, list(shape), dt, kind="ExternalInput").ap()

    x_d = din("x", [NB, S, D])
    cT_d = din("cT", [128, KC, NB])
    pos_d = din("pos", [NB, S], I32)
    adaw_d = din("ada_w", [D, 3 * D])
    win_d = din("w_in", [D, NCOL])
    wuq_d = din("w_uq", [256, 1536])
    wukv_d = din("w_ukv", [128, 1024])
    wout_d = din("w_out", [D, D])
    poolw_d = din("pool_w", [128, 4, 128])
    vec_d = din("vec", [128, 64])
    rows_d = din("rows", [1, 192])
    gkn_d = din("gkn", [128, 64])
    adabg_d = din("adab_gate", [1, 1024])
    ident_d = din("ident", [128, 128], BF16)
    bands_d = din("bands", [128, 4, 400], BF16)
    colc_d = din("colc", [128, 6])
    out_d = nc.dram_tensor("out", [NB, S, D], F32, kind="ExternalOutput").ap()
    wins_d = nc.dram_tensor("win_s", [128, KC * NCOL], BF16).ap()
    wouts_d = nc.dram_tensor("wout_s", [128, KC * D], BF16).ap()
    tabs_d = nc.dram_tensor("tab_s", [NB, 2, 32, S], BF16).ap()

    Sd = Sched()
    A = Sd.add
    import os as _os
    _stop = _os.environ.get("KSTOP", "")

    def ckpt(name):
        if name == _stop:
            Sd.enabled = False
    es = ExitStack()
    with es:
        XB = 122 * 1024
        PB = 84 * 1024
        Xt = es.enter_context(nc.sbuf_tensor("Xreg", [128, XB], U8))
        Pt = es.enter_context(nc.sbuf_tensor("Preg", [128, PB], U8))
        ps = es.enter_context(nc.psum_tensor("ps", [128, 4096], F32))
        sems = {e: es.enter_context(nc.semaphore("s_" + e)) for e in (PE, ACT, DVE, POOL, SP)}
        dsems = [es.enter_context(nc.semaphore("dq%d" % i)) for i in range(24)]
        X = Region(Xt[:], XB)
        P = Region(Pt[:], PB)

        def bank(i, n=512, off=0):
            return ps[:, i * 512 + off:i * 512 + off + n]

        wuq = P.alloc([2, 1536], BF16)
        wukv = P.alloc([1024], BF16)
        wukvg = P.alloc([512], BF16)
        poolw = P.alloc([4, 128], BF16)
        ident = P.alloc([128], BF16)
        bands = P.alloc([4, 400], BF16)
        ones_bf = P.alloc([128], BF16)
        ones_f = P.alloc([128], F32)
        vec = P.alloc([64], F32)
        colc = P.alloc([6], F32)
        rows = P.alloc([192], F32)
        negB = P.alloc([1], F32)
        cact = P.alloc([KC, NB], F32)
        mod = P.alloc([24, NB], F32)
        gs = P.alloc([KC, NB], F32)
        COS = P.alloc([S], BF16)
        SINS = P.alloc([S], BF16)
        ypT = P.alloc([4, S], BF16)
        sga = P.alloc([4, S], BF16)
        qln = P.alloc([2, S], BF16)
        kvn = P.alloc([S], BF16)
        KR = P.alloc([S], BF16)
        krstat = P.alloc([TT], F32)
        kss = P.alloc([TT, NH], F32)
        sck = P.alloc([TT, NH], F32)
        gate_bc = [P.alloc([1024], F32) for _ in range(NB)]
        rstat = P.alloc([TT, 2], F32)
        small = P.alloc([16], F32)
        P_end = P.off

        norm_g = vec[:, 0:8]
        adab = vec[:, 8:32]
        pscale = vec[:, 32:36]
        gql = vec[:, 36:38]
        gkvl = vec[:, 38:39]
        gq96 = vec[:, 39:40]
        gqs96 = vec[:, 40:41]
        gkr = vec[:, 41:42]
        gkrs = vec[:, 42:43]
        invf = colc[:, 0:1]
        sgn = colc[:, 1:2]
        epsc = colc[:, 2:3]
        lnsc = colc[:, 3:4]
        invf4 = colc[:, 4:5]
        sgn4 = colc[:, 5:6]

        rr = [0]

        def evac_eng():
            rr[0] += 1
            return ACT if rr[0] % 2 else DVE

        def copy_on(eng_name, out, in_, reads, writes):
            if eng_name == ACT:
                A(ACT, "activation", out=out, in_=in_, func=AF.Copy, reads=reads, writes=writes)
            else:
                A(eng_name, "tensor_copy", out=out, in_=in_, reads=reads, writes=writes)

        X.reset()
        A(SP, "dma_start", out=vec, in_=vec_d, writes=["vec"], dma=True)
        A(SP, "dma_start", out=colc, in_=colc_d, writes=["colc"], dma=True)
        A(SP, "dma_start", out=rows[0:1, :], in_=rows_d, writes=["rows"], dma=True)
        A(SP, "dma_start", out=ident, in_=ident_d, writes=["ident"], dma=True)
        A(SP, "dma_start", out=bands, in_=bands_d, writes=["bands"], dma=True)
        A(SP, "dma_start", out=cact, in_=cT_d, writes=["cact"], dma=True)
        NQ4 = S // 512
        posi_all = X.alloc([NB, 512], I32)
        for b in range(NB):
            for q4 in range(NQ4):
                A(SP, "dma_start", out=posi_all[32 * q4:32 * q4 + 32, b, :],
                  in_=pos_d[b:b + 1, q4 * 512:(q4 + 1) * 512].partition_broadcast(32), writes=[("posi", b)], dma=True)
        A(POOL, "memset", ones_bf, 1.0, writes=["ones_bf"])
        A(POOL, "memset", ones_f, 1.0, writes=["ones_f"])
        A(ACT, "activation", out=cact, in_=cact, func=AF.Silu, reads=["cact"], writes=["cact"])

        A(DVE, "tensor_reduce", out=small[0:1, 0:2], in_=rows[0:1, :].rearrange("p (a b) -> p a b", a=2),
                                         axis=AX.X, op=ALU.max, apply_absolute_value=True,
          reads=["rows"], writes=["small"])
        A(DVE, "scalar_tensor_tensor", out=small[0:1, 2:3], in0=small[0:1, 0:1], scalar=-math.sqrt(96.0),
                                                in1=small[0:1, 1:2], op0=ALU.mult, op1=ALU.mult,
          reads=["small"], writes=["small2"])
        A(PE, "matmul", bank(7, 1), lhsT=ones_f[0:1, :], rhs=small[0:1, 2:3], start=True, stop=True,
          reads=["small2", "ones_f"], writes=["ps7"])
        A(DVE, "tensor_copy", out=negB, in_=bank(7, 1), reads=["ps7"], writes=["negB"])

        ckpt("s1")
        stg_uq = X.alloc([2, 1536], F32)
        stg_kv = X.alloc([1024], F32)
        stg_pw = X.alloc([4, 128], F32)
        gkn = X.alloc([64], F32)
        A(SP, "dma_start", out=stg_uq, in_=wuq_d.rearrange("(k p) c -> p k c", p=128), writes=["stg_uq"], dma=True)
        A(SP, "dma_start", out=stg_kv, in_=wukv_d, writes=["stg_kv"], dma=True)
        A(SP, "dma_start", out=stg_pw, in_=poolw_d, writes=["stg_pw"], dma=True)
        A(SP, "dma_start", out=gkn, in_=gkn_d, writes=["gkn"], dma=True)
        A(DVE, "tensor_copy", out=wuq, in_=stg_uq, reads=["stg_uq"], writes=["wuq"])
        A(DVE, "tensor_copy", out=wukv, in_=stg_kv, reads=["stg_kv"], writes=["wukv"])
        A(DVE, "tensor_copy", out=poolw, in_=stg_pw, reads=["stg_pw"], writes=["poolw"])
        A(DVE, "tensor_tensor", out=wukvg.rearrange("p (h d) -> p h d", h=NH),
                                         in0=stg_kv[:, 512:1024].rearrange("p (h d) -> p h d", h=NH),
                                         in1=gkn.unsqueeze(1).to_broadcast([128, NH, 64]), op=ALU.mult,
          reads=["stg_kv", "gkn"], writes=["wukvg"])

        ckpt("s2")
        CW = 264
        NSTG = 2
        stg_w = [X.alloc([KC, CW], F32) for _ in range(NSTG)]
        stg_wb = [X.alloc([KC, CW], BF16) for _ in range(NSTG)]
        wins_v = wins_d.rearrange("p (k c) -> p k c", k=KC)
        wouts_v = wouts_d.rearrange("p (k c) -> p k c", k=KC)
        jobs = [(win_d, wins_v, ci * CW, CW) for ci in range(NCOL // CW)]
        jobs += [(wout_d, wouts_v, ci * 256, 256) for ci in range(4)]
        for j, (src, dst, c0, cw) in enumerate(jobs):
            sf = stg_w[j % NSTG]
            sbf = stg_wb[j % NSTG]
            A(SP, "dma_start", out=sf[:, :, 0:cw], in_=src[:, c0:c0 + cw].rearrange("(k p) c -> p k c", p=128),
              writes=[("stg_w", j % NSTG)], dma=True)
            copy_on(POOL, sbf[:, :, 0:cw], sf[:, :, 0:cw], [("stg_w", j % NSTG)], [("stg_wb", j % NSTG)])
            A(POOL, "dma_start", out=dst[:, :, c0:c0 + cw], in_=sbf[:, :, 0:cw],
              reads=[("stg_wb", j % NSTG)], dma=True)

        late_dve = []
        stg_ada = [X.alloc([KC, 512], F32) for _ in range(2)]
        cbc = [X.alloc([KC, 128], F32) for _ in range(NB)]
        adabg = X.alloc([1024], F32)
        A(SP, "dma_start", out=adabg, in_=adabg_d.partition_broadcast(128), writes=["adabg"], dma=True)
        for b in range(NB):
            A(DVE, "tensor_copy", out=cbc[b], in_=cact[:, :, b:b + 1].to_broadcast([128, KC, 128]),
              reads=["cact"], writes=[("cbc", b)])
        for ec in range(6):
            st = stg_ada[ec % 2]
            A(ACT, "dma_start", out=st, in_=adaw_d[:, ec * 512:(ec + 1) * 512].rearrange("(k p) c -> p k c", p=128),
              writes=[("stg_ada", ec % 2)], dma=True)
            for mt in range(4):
                m = ec * 4 + mt
                for kc in range(KC):
                    A(PE, "matmul",
                        bank(0, NB, m * NB), lhsT=st[:, kc, mt * 128:(mt + 1) * 128], rhs=cact[:, kc, :],
                        start=(kc == 0), stop=(kc == KC - 1),
                      reads=[("stg_ada", ec % 2), "cact"], writes=["ps0"])
            if ec >= 4:
                for b in range(NB):
                    bk = 1 + b * 2 + (ec - 4)
                    for kc in range(KC):
                        A(PE, "matmul",
                            bank(bk), lhsT=cbc[b][:, kc, :], rhs=st[:, kc, :], start=(kc == 0), stop=(kc == KC - 1),
                          reads=[("stg_ada", ec % 2), ("cbc", b)], writes=["ps%d" % bk])
                    late_dve.append((lambda b=b, bk=bk, ec=ec: A(DVE, "tensor_tensor",
                        out=gate_bc[b][:, (ec - 4) * 512:(ec - 3) * 512], in0=bank(bk),
                        in1=adabg[:, (ec - 4) * 512:(ec - 3) * 512], op=ALU.add,
                      reads=["ps%d" % bk, "adabg"], writes=[("gate_bc", b, ec)])))
        late_dve.append(lambda: A(DVE, "tensor_tensor", out=mod, in0=bank(0, 24 * NB).rearrange("p (m b) -> p m b", b=NB),
                                         in1=adab.unsqueeze(2).to_broadcast([128, 24, NB]), op=ALU.add,
          reads=["ps0", "vec"], writes=["mod"]))
        late_dve.append(lambda: A(DVE, "scalar_tensor_tensor", out=gs, in0=mod[:, 8:16, :], scalar=1.0,
                                                in1=norm_g.unsqueeze(2).to_broadcast([128, KC, NB]),
                                                op0=ALU.add, op1=ALU.mult,
          reads=["mod", "vec"], writes=["gs"]))

        R = slice(64, 96)
        PQ = slice(0, 32 * NQ4)
        tmpc = [(X.alloc([512], F32), X.alloc([512], F32), X.alloc([512], I32), X.alloc([512], BF16)) for _ in range(2)]
        for b in reversed(range(NB)):
            posi = posi_all[PQ, b, :]
            chains = []
            for ci, (which, shift) in enumerate((("cos", math.pi / 2), ("sin", 0.0))):
                ang, angk, angi, tout = tmpc[ci]
                ang, angk, angi, tout = ang[PQ, :], angk[PQ, :], angi[PQ, :], tout[PQ, :]
                ka, kk_, ki, ko = ("ang", ci), ("angk", ci), ("angi", ci), ("tout", ci)
                ops = [
                    (DVE, "tensor_copy", dict(out=ang, in_=posi), [("posi", b)], [ka]),
                    (DVE, "tensor_scalar", dict(out=ang, in0=ang, scalar1=invf4[PQ, :], scalar2=shift,
                                                op0=ALU.mult, op1=ALU.add), [ka, "colc"], [ka]),
                    (DVE, "tensor_scalar", dict(out=angi, in0=ang, scalar1=1.0 / (2 * math.pi), scalar2=None,
                                                op0=ALU.mult), [ka], [ki]),
                    (DVE, "tensor_copy", dict(out=angk, in_=angi), [ki], [kk_]),
                    (DVE, "scalar_tensor_tensor", dict(out=ang, in0=angk, scalar=-2 * math.pi, in1=ang,
                                                       op0=ALU.mult, op1=ALU.add), [kk_, ka], [ka]),
                    (DVE, "tensor_scalar", dict(out=angk, in0=ang, scalar1=math.pi, scalar2=None, op0=ALU.is_gt), [ka], [kk_]),
                    (DVE, "scalar_tensor_tensor", dict(out=ang, in0=angk, scalar=-2 * math.pi, in1=ang,
                                                       op0=ALU.mult, op1=ALU.add), [kk_, ka], [ka]),
                    (DVE, "tensor_scalar", dict(out=angk, in0=ang, scalar1=-math.pi, scalar2=None, op0=ALU.is_lt), [ka], [kk_]),
                    (DVE, "scalar_tensor_tensor", dict(out=ang, in0=angk, scalar=2 * math.pi, in1=ang,
                                                       op0=ALU.mult, op1=ALU.add), [kk_, ka], [ka]),
                ]
                if which == "cos":
                    ops.append((ACT, "activation", dict(out=tout, in_=ang, func=AF.Sin), [ka], [ko]))
                else:
                    ops.append((ACT, "activation", dict(out=ang, in_=ang, func=AF.Sin), [ka], [ka]))
                    ops.append((DVE, "tensor_scalar", dict(out=tout, in0=ang, scalar1=sgn4[PQ, :], scalar2=None,
                                                           op0=ALU.mult), [ka, "colc"], [ko]))
                chains.append(ops)
            for k_ in range(max(len(c) for c in chains)):
                for c in chains:
                    if k_ < len(c):
                        eng_, op_, kw_, rd_, wr_ = c[k_]
                        A(eng_, op_, reads=rd_, writes=wr_, **kw_)
            for ci, dst in enumerate((COS, SINS)):
                tout = tmpc[ci][3]
                for q4 in range(NQ4):
                    if b > 0:
                        A(SP, "dma_start", out=tabs_d[b, ci, :, q4 * 512:(q4 + 1) * 512], in_=tout[32 * q4:32 * q4 + 32, :],
                          reads=[("tout", ci)], dma=True)
                    else:
                        A(SP, "dma_start", out=dst[R, q4 * 512:(q4 + 1) * 512], in_=tout[32 * q4:32 * q4 + 32, :],
                          reads=[("tout", ci)], writes=[("tab", ci, q4)], dma=True)

        for f_ in late_dve:
            f_()
        Sd.barrier(lambda e: e.memset(small[:, 8:9], 0.0))

        ckpt("setup")
        for b in range(NB):
            X.reset()
            Wb = X.alloc([KC, NCOL], BF16)
            xnT = X.alloc([KC, S], BF16)
            mark = X.off
            xt = [X.alloc([1024], F32) for _ in range(8)]
            junk = X.alloc([1024], BF16)
            xr = [X.alloc([1024], BF16) for _ in range(2)]
            X.reset(mark)
            u_tok = X.alloc([TT, 512], BF16)
            sgp = X.alloc([4, S], BF16)
            diffT = [X.alloc([512], BF16) for _ in range(2)]
            krg = X.alloc([512], F32)
            krsg = X.alloc([512], F32)
            krsq = X.alloc([512], BF16)
            kt1 = X.alloc([512], F32)
            qlg = X.alloc([2, 512], BF16)
            qlsq = X.alloc([2, 512], BF16)
            kvg = X.alloc([512], BF16)
            kvsq = X.alloc([512], BF16)
            lnr = X.alloc([512], F32)
            Rq = X.alloc([512], BF16)

            A(POOL, "memset", rstat, 0.0, writes=["rstat0"])

            R = slice(64, 96)
            if b > 0:
                A(SP, "dma_start", out=COS[R, :], in_=tabs_d[b, 0], writes=["COS"], dma=True)
                A(SP, "dma_start", out=SINS[R, :], in_=tabs_d[b, 1], writes=["SINS"], dma=True)
            ckpt("tables")
            _ska = set(_os.environ.get("KSKIPA", "").split(","))
            for tg in range(TT // 4):
                for j in range(4):
                    tt = tg * 4 + j
                    xb = xt[tt % 8]
                    A(SP, "dma_start", out=xb, in_=x_d[b, tt * 128:(tt + 1) * 128, :],
                      writes=[("xt", tt % 8)], dma=True)
                    if tt % 2 == 1 and tt // 2 < KC:
                        A(SP, "dma_start", out=Wb[:, tt // 2, :], in_=wins_v[:, tt // 2, :], writes=[("Wb", tt // 2)], dma=True)
                    A(ACT, "activation", out=junk, in_=xb, func=AF.Square, accum_out=rstat[:, tt, 0:1],
                      reads=[("xt", tt % 8), "rstat0"], writes=["junk", ("rs", tt)])
                g4 = slice(tg * 4, tg * 4 + 4)
                A(ACT, "activation", out=rstat[:, g4, 1:2], in_=rstat[:, g4, 0:1], func=AF.Ln, bias=epsc, scale=1.0 / D,
                  reads=[("rs", tg * 4 + j_) for j_ in range(4)] + ["colc"], writes=[("rl", tg)])
                A(ACT, "activation", out=rstat[:, g4, 0:1], in_=rstat[:, g4, 1:2], func=AF.Exp, scale=-0.5,
                  reads=[("rl", tg)], writes=[("rr", tg)])
                for j in range(4):
                    tt = tg * 4 + j
                    xb = xt[tt % 8]
                    if "xr" not in _ska: A(DVE, "tensor_scalar", out=xr[tt % 2], in0=xb, scalar1=rstat[:, tt, 0:1],
                                                                   scalar2=None, op0=ALU.mult,
                      reads=[("xt", tt % 8), ("rr", tg)], writes=[("xr", tt % 2)])
                    pst = bank(j).bitcast(BF16)
                    for kc in range(KC):
                        if "tr" not in _ska: A(PE, "transpose", out=pst[:, kc * 128:(kc + 1) * 128],
                                                                           in_=xr[tt % 2][:, kc * 128:(kc + 1) * 128],
                                                                           identity=ident,
                          reads=[("xr", tt % 2), "ident"], writes=["ps%d" % j])
                if tg == TT // 4 - 1:
                    for kc in range(TT // 2, KC):
                        A(SP, "dma_start", out=Wb[:, kc, :], in_=wins_v[:, kc, :], writes=[("Wb", kc)], dma=True)
                src_all = ps[:, 0:2048].bitcast(BF16).rearrange("p (j k t) -> p j k t", j=4, k=KC)
                for kc in range(KC):
                    dst = xnT[:, kc, tg * 512:(tg + 1) * 512].rearrange("p (j t) -> p j t", j=4)
                    A(ACT, "activation", out=dst[:, 0:2, :], in_=src_all[:, 0:2, kc, :], func=AF.Identity,
                      bias=mod[:, kc, b:b + 1], scale=gs[:, kc, b:b + 1],
                      reads=["ps0", "ps1", "gs", "mod"], writes=[("xnT", kc, tg, 0)])
                    A(DVE, "tensor_scalar", out=dst[:, 2:4, :], in0=src_all[:, 2:4, kc, :], scalar1=gs[:, kc, b:b + 1],
                      scalar2=mod[:, kc, b:b + 1], op0=ALU.mult, op1=ALU.add,
                      reads=["ps2", "ps3", "gs", "mod"], writes=[("xnT", kc, tg, 1)])

            Sd.barrier(lambda e: e.memset(small[:, 8:9], 0.0))

            ckpt("A")
            xn_keys = lambda tg: [("xnT", kc, tg, hh) for kc in range(KC) for hh in range(2)]
            for tt in range(TT):
                bk = tt % 2
                for kc in range(KC):
                    A(PE, "matmul", bank(bk), lhsT=xnT[:, kc, tt * 128:(tt + 1) * 128],
                                                                  rhs=Wb[:, kc, 0:512], start=(kc == 0), stop=(kc == KC - 1),
                      reads=xn_keys(tt // 4) + [("Wb", kc)], writes=["ps%d" % bk])
                copy_on(evac_eng(), u_tok[:, tt, :], bank(bk), ["ps%d" % bk], [("u_tok", tt)])

            ckpt("Bi")
            def fm_tile(c0, m, blk, bk):
                for kc in range(KC):
                    A(PE, "matmul", bank(bk)[0:m, :], lhsT=Wb[:, kc, c0:c0 + m],
                                                    rhs=xnT[:, kc, blk * 512:(blk + 1) * 512],
                                                    start=(kc == 0), stop=(kc == KC - 1),
                      reads=xn_keys(blk) + [("Wb", kc)], writes=["ps%d" % bk])

            for blk in range(NBLK):
                cs = slice(blk * 512, (blk + 1) * 512)
                nb_ = [2]

                def nxt():
                    nb_[0] = 2 + (nb_[0] - 1) % 6
                    return nb_[0]
                for g in range(4):
                    bk = nxt()
                    fm_tile(512 + g * 128, 128, blk, bk)
                    A(ACT, "activation", out=sgp[:, g, cs], in_=bank(bk), func=AF.Silu,
                      reads=["ps%d" % bk], writes=[("sgp", g, blk)])
                for g in range(4):
                    bk = nxt()
                    fm_tile(1600 + g * 128, 128, blk, bk)
                    A(ACT, "activation", out=sga[:, g, cs], in_=bank(bk), func=AF.Silu,
                      reads=["ps%d" % bk], writes=[("sga", g, blk)])
                for j in range(2):
                    bk = nxt()
                    fm_tile(1024 + j * 128, 128, blk, bk)
                    A(DVE, "tensor_scalar", out=qlg[:, j, :], in0=bank(bk), scalar1=gql[:, j:j + 1],
                                                                 scalar2=None, op0=ALU.mult,
                      reads=["ps%d" % bk, "vec"], writes=[("qlg", j)])
                    A(ACT, "activation", out=qlsq[:, j, :], in_=bank(bk), func=AF.Square,
                      reads=["ps%d" % bk], writes=[("qlsq", j)])
                bk = nxt()
                for j in range(2):
                    A(PE, "matmul", bank(bk), lhsT=ones_bf, rhs=qlsq[:, j, :], start=(j == 0), stop=(j == 1),
                      reads=[("qlsq", j), "ones_bf"], writes=["ps%d" % bk])
                A(ACT, "activation", out=lnr, in_=bank(bk), func=AF.Ln, bias=epsc, scale=1.0 / 256,
                  reads=["ps%d" % bk, "colc"], writes=["lnr"])
                A(ACT, "activation", out=Rq, in_=lnr, func=AF.Exp, scale=-0.5, reads=["lnr"], writes=["Rq"])
                for j in range(2):
                    A(DVE, "tensor_tensor", out=qln[:, j, cs], in0=qlg[:, j, :], in1=Rq, op=ALU.mult,
                      reads=[("qlg", j), "Rq"], writes=[("qln", j, blk)])
                bk = nxt()
                fm_tile(1280, 128, blk, bk)
                A(DVE, "tensor_scalar", out=kvg, in0=bank(bk), scalar1=gkvl, scalar2=None, op0=ALU.mult,
                  reads=["ps%d" % bk, "vec"], writes=["kvg"])
                A(ACT, "activation", out=kvsq, in_=bank(bk), func=AF.Square, reads=["ps%d" % bk], writes=["kvsq"])
                bk = nxt()
                A(PE, "matmul", bank(bk), lhsT=ones_bf, rhs=kvsq, start=True, stop=True,
                  reads=["kvsq", "ones_bf"], writes=["ps%d" % bk])
                A(ACT, "activation", out=lnr, in_=bank(bk), func=AF.Ln, bias=epsc, scale=1.0 / 128,
                  reads=["ps%d" % bk, "colc"], writes=["lnr"])
                A(ACT, "activation", out=Rq, in_=lnr, func=AF.Exp, scale=-0.5, reads=["lnr"], writes=["Rq"])
                A(DVE, "tensor_tensor", out=kvn[:, cs], in0=kvg, in1=Rq, op=ALU.mult,
                  reads=["kvg", "Rq"], writes=[("kvn", blk)])
                bk = nxt()
                fm_tile(1408, 96, blk, bk)
                A(ACT, "activation", out=krg[R, :], in_=bank(bk)[R, :], func=AF.Copy, scale=gkr[R, :],
                  reads=["ps%d" % bk, "vec"], writes=["krg"])
                A(ACT, "activation", out=krsq[R, :], in_=bank(bk)[R, :], func=AF.Square,
                  reads=["ps%d" % bk], writes=["krsq"])
                bk = nxt()
                fm_tile(1504, 96, blk, bk)
                A(ACT, "activation", out=krsg[R, :], in_=bank(bk)[R, :], func=AF.Copy, scale=gkrs[R, :],
                  reads=["ps%d" % bk, "vec"], writes=["krsg"])
                A(POOL, "tensor_tensor", out=kt1[R, :], in0=krg[R, :], in1=COS[R, cs], op=ALU.mult,
                  reads=["krg", "COS"], writes=["kt1"])
                A(POOL, "tensor_tensor", out=krsg[R, :], in0=krsg[R, :], in1=SINS[R, cs], op=ALU.mult,
                  reads=["krsg", "SINS"], writes=["krsg"])
                A(POOL, "tensor_tensor", out=KR[R, cs], in0=kt1[R, :], in1=krsg[R, :], op=ALU.add,
                  reads=["kt1", "krsg"], writes=[("KR", blk)])
                for j in range(4):
                    tt = blk * 4 + j
                    A(PE, "matmul", bank(1, 1, tt), lhsT=krsq[R, j * 128:(j + 1) * 128], rhs=ones_bf[R, 0:1],
                                                    start=True, stop=True,
                      reads=["krsq", "ones_bf"], writes=["ps1"])
            A(DVE, "tensor_copy", out=krstat, in_=bank(1, TT), reads=["ps1"], writes=["krstat"])

            ckpt("Bii")
            items = [(g, blk) for g in range(4) for blk in range(NBLK)]

            def band(idx):
                g, blk = items[idx]
                bk = 2 + idx % 2
                for j in range(4):
                    tt = blk * 4 + j
                    coff = 0 if tt == 0 else (256 if tt == TT - 1 else 128)
                    has_l = tt > 0
                    has_r = tt < TT - 1
                    A(PE, "matmul", bank(bk, 128, j * 128), lhsT=u_tok[:, tt, g * 128:(g + 1) * 128], rhs=bands[:, g, coff:coff + 128],
                      start=True, stop=not (has_l or has_r), reads=[("u_tok", tt), "bands"], writes=["ps%d" % bk])
                    if has_l:
                        A(PE, "matmul", bank(bk, 8, j * 128), lhsT=u_tok[:, tt - 1, g * 128:(g + 1) * 128], rhs=bands[:, g, 384:392],
                          start=False, stop=not has_r, reads=[("u_tok", tt - 1), "bands"], writes=["ps%d" % bk])
                    if has_r:
                        A(PE, "matmul", bank(bk, 8, j * 128 + 120), lhsT=u_tok[:, tt + 1, g * 128:(g + 1) * 128], rhs=bands[:, g, 392:400],
                          start=False, stop=True, reads=[("u_tok", tt + 1), "bands"], writes=["ps%d" % bk])

            band(0)
            for idx, (g, blk) in enumerate(items):
                cs = slice(blk * 512, (blk + 1) * 512)
                bk = 2 + idx % 2
                bk2 = 4 + idx % 2
                dT = diffT[idx % 2]
                if idx + 1 < len(items):
                    band(idx + 1)
                copy_on(evac_eng(), dT, bank(bk), ["ps%d" % bk], [("diffT", idx % 2)])
                A(PE, "matmul", bank(bk2), lhsT=poolw[:, g, :], rhs=dT, start=True, stop=True,
                  reads=[("diffT", idx % 2), "poolw"], writes=["ps%d" % bk2])
                A(DVE, "scalar_tensor_tensor", out=ypT[:, g, cs], in0=bank(bk2), scalar=pscale[:, g:g + 1], in1=sgp[:, g, cs],
                  op0=ALU.mult, op1=ALU.mult, reads=["ps%d" % bk2, "vec", ("sgp", g, blk)], writes=[("ypT", g, blk)])

            Sd.barrier(lambda e: e.memset(small[:, 8:9], 0.0))

            ckpt("Biii")
            X.reset()
            Wo = X.alloc([KC, D], BF16)
            vaug_off = X.off
            Vaug = X.alloc([TT, NH, 128], BF16)
            QT = [X.alloc([S], BF16) for _ in range(2)]
            KT = [X.alloc([S], BF16) for _ in range(2)]
            PT = [X.alloc([QW], BF16) for _ in range(3)]
            Osb = X.alloc([QW], F32)
            rec = X.alloc([QW], F32)
            tg_ = X.alloc([QW], F32)
            zsb = X.alloc([512], F32)
            zssb = X.alloc([512], F32)
            zsq = X.alloc([512], BF16)
            lnF = X.alloc([512], F32)
            Fq = X.alloc([512], F32)
            qn = X.alloc([512], F32)
            t1 = X.alloc([512], F32)
            t2 = X.alloc([512], F32)
            ksqs = [X.alloc([512], F32) for _ in range(2)]
            yaT = X.alloc([4, S], BF16)
            att_end = X.off
            X.reset(vaug_off)
            xo = [X.alloc([1024], F32) for _ in range(4)]
            to = [X.alloc([1024], F32) for _ in range(2)]
            X.reset(att_end)

            for kc in range(KC):
                A(SP, "dma_start", out=Wo[:, kc, :], in_=wouts_v[:, kc, :], reads=["wscratch"], writes=[("Wo", kc)], dma=True)

            ckpt("KV")
            def produce_stages(h, blk):
                cs = slice(blk * 512, (blk + 1) * 512)
                q_ = QT[h % 2]
                k_ = KT[h % 2]
                qk = ("QT", h % 2)
                kk = ("KT", h % 2)
                hp = h // 2
                rs = slice(0, 64) if h % 2 == 0 else slice(64, 128)

                def st0():
                    for j in range(2):
                        A(PE, "matmul", bank(6)[0:96, :], lhsT=wuq[:, j, h * 96:(h + 1) * 96], rhs=qln[:, j, cs],
                          start=(j == 0), stop=(j == 1), reads=[("qln", j, blk), "wuq"], writes=["ps6"])
                    for j in range(2):
                        A(PE, "matmul", bank(7)[0:96, :], lhsT=wuq[:, j, 768 + h * 96:768 + (h + 1) * 96], rhs=qln[:, j, cs],
                          start=(j == 0), stop=(j == 1), reads=[("qln", j, blk), "wuq"], writes=["ps7"])
                    A(DVE, "tensor_copy", out=zsb[0:96, :], in_=bank(6)[0:96, :], reads=["ps6"], writes=["zsb"])
                    A(DVE, "tensor_copy", out=zssb[R, :], in_=bank(7)[R, :], reads=["ps7"], writes=["zssb"])
                    A(DVE, "tensor_tensor", out=zsq[0:96, :], in0=zsb[0:96, :], in1=zsb[0:96, :], op=ALU.mult,
                      reads=["zsb"], writes=["zsq"])

                def st1():
                    A(PE, "matmul", bank(6)[0:96, :], lhsT=ones_bf[0:96, 0:96], rhs=zsq[0:96, :], start=True, stop=True,
                      reads=["zsq", "ones_bf"], writes=["ps6"])
                    A(PE, "matmul", bank(7), lhsT=wukvg[:, hp * 128:(hp + 1) * 128], rhs=kvn[:, cs], start=True, stop=True,
                      reads=[("kvn", blk), "wukvg"], writes=["ps7"])
                    A(DVE, "tensor_copy", out=k_[0:64, cs], in_=bank(7)[rs, :], reads=["ps7"], writes=[kk])
                    A(SP, "dma_start", out=k_[R, cs], in_=KR[R, cs], reads=[("KR", blk), kk], writes=[kk], dma=True)

                def st2():
                    A(ACT, "activation", out=lnF[0:96, :], in_=bank(6)[0:96, :], func=AF.Ln, bias=epsc[0:96, :], scale=1.0 / QKH,
                      reads=["ps6", "colc"], writes=["lnF"])
                    A(ACT, "activation", out=Fq[0:96, :], in_=lnF[0:96, :], func=AF.Exp, scale=-0.5, reads=["lnF"], writes=["Fq"])

                def st3():
                    A(DVE, "scalar_tensor_tensor", out=qn[0:96, :], in0=zsb[0:96, :], scalar=gq96[0:96, :], in1=Fq[0:96, :],
                      op0=ALU.mult, op1=ALU.mult, reads=["zsb", "Fq", "vec"], writes=["qn"])
                    A(DVE, "scalar_tensor_tensor", out=t2[R, :], in0=zssb[R, :], scalar=gqs96[R, :], in1=Fq[R, :],
                      op0=ALU.mult, op1=ALU.mult, reads=["zssb", "Fq", "vec"], writes=["t2"])
                    A(POOL, "tensor_copy", out=q_[0:64, cs], in_=qn[0:64, :], reads=["qn"], writes=[qk])
                    A(POOL, "tensor_tensor", out=t1[R, :], in0=qn[R, :], in1=COS[R, cs], op=ALU.mult,
                      reads=["qn", "COS"], writes=["t1"])
                    A(POOL, "tensor_tensor", out=t2[R, :], in0=t2[R, :], in1=SINS[R, cs], op=ALU.mult,
                      reads=["t2", "SINS"], writes=["t2"])
                    A(POOL, "tensor_tensor", out=q_[R, cs], in0=t1[R, :], in1=t2[R, :], op=ALU.add,
                      reads=["t1", "t2", qk], writes=[qk])
                return [st0, st1, st2, st3]

            NG = NQP * TT
            groups = [(h, qp, kt) for h in range(NH) for qp in range(NQP) for kt in range(TT)]
            gap = max(4, (NG - 1) // NBLK)
            assert NBLK * 4 <= NG, (NBLK, NG)

            def emit_qk(i):
                h, qp, kt = groups[i]
                q_ = QT[h % 2]
                k_ = KT[h % 2]
                sb_ = i % 2
                sbk = [0, 2][sb_]
                for j in ([0] * FILL_QK + list(range(NJ))):
                    q0 = qp * QW + j * 512
                    A(PE, "matmul", bank(sbk + j), lhsT=k_[0:96, kt * 128:(kt + 1) * 128], rhs=q_[0:96, q0:q0 + 512],
                      start=True, stop=True, reads=[("QT", h % 2), ("KT", h % 2)], writes=["ps%d" % (sbk + j)])

            p0_list = [st for blk in range(NBLK) for st in produce_stages(0, blk)]
            for tt in range(TT):
                for k_ in range(len(p0_list)):
                    if (k_ * TT) // len(p0_list) == tt:
                        p0_list[k_]()
                ba, bb = (tt % 3) * 2, (tt % 3) * 2 + 1
                ksq = ksqs[tt % 2]
                A(PE, "matmul", bank(ba), lhsT=kvn[:, tt * 128:(tt + 1) * 128], rhs=wukv[:, 0:512], start=True, stop=True,
                  reads=[("kvn", tt // 4), "wukv"], writes=["ps%d" % ba])
                A(PE, "matmul", bank(bb), lhsT=kvn[:, tt * 128:(tt + 1) * 128], rhs=wukv[:, 512:1024], start=True, stop=True,
                  reads=[("kvn", tt // 4), "wukv"], writes=["ps%d" % bb])
                vsrc = bank(ba).rearrange("p (h d) -> p h d", h=NH)
                A(ACT, "activation", out=Vaug[:, tt, 0:NH:2, 0:64], in_=vsrc[:, 0:NH:2, :], func=AF.Copy,
                  reads=["ps%d" % ba], writes=[("V", tt, 0)])
                A(ACT, "activation", out=Vaug[:, tt, 1:NH:2, 64:128], in_=vsrc[:, 1:NH:2, :], func=AF.Copy,
                  reads=["ps%d" % ba], writes=[("V", tt, 1)])
                A(ACT, "activation", out=Vaug[:, tt, 0:NH:2, 64:128], in_=vsrc[:, 0:NH:2, :], func=AF.Identity,
                  bias=ones_f[:, 0:1], scale=0.0, reads=["ps%d" % ba, "ones_f"], writes=[("V", tt, 0)])
                A(ACT, "activation", out=Vaug[:, tt, 1:NH:2, 0:64], in_=vsrc[:, 1:NH:2, :], func=AF.Identity,
                  bias=ones_f[:, 0:1], scale=0.0, reads=["ps%d" % ba, "ones_f"], writes=[("V", tt, 1)])
                A(ACT, "activation", out=ksq, in_=bank(bb), func=AF.Square, reads=["ps%d" % bb], writes=[("ksq", tt % 2)])
                A(DVE, "tensor_reduce", out=kss[:, tt, :], in_=ksq.rearrange("p (h d) -> p h d", h=NH),
                                                        axis=AX.X, op=ALU.add, reads=[("ksq", tt % 2)], writes=[("kss", tt)])
            A(DVE, "tensor_tensor", out=kss, in0=kss, in1=krstat.unsqueeze(2).to_broadcast([128, TT, NH]), op=ALU.add,
              reads=[("kss", tt) for tt in range(TT)] + ["krstat"], writes=["kss_all"])
            A(ACT, "activation", out=kss, in_=kss, func=AF.Ln, bias=epsc, scale=1.0 / QKH, reads=["kss_all", "colc"], writes=["kss_ln"])
            A(ACT, "activation", out=sck, in_=kss, func=AF.Exp, bias=lnsc, scale=-0.5, reads=["kss_ln", "colc"], writes=["sck"])

            emit_qk(0)
            if len(groups) > 1:
                emit_qk(1)
            pending = []
            for i, (h, qp, kt) in enumerate(groups):
                sb_ = i % 2
                sbk = [0, 2][sb_]
                pb_ = i % 3
                A(ACT, "activation", out=PT[pb_], in_=ps[:, sbk * 512:sbk * 512 + QW], func=AF.Exp, bias=negB,
                  scale=sck[:, kt, h:h + 1], reads=["ps%d" % (sbk + j_) for j_ in range(NJ)] + ["sck", "negB"], writes=[("PT", pb_)])
                if i + 2 < len(groups):
                    emit_qk(i + 2)
                for j in range(NJ):
                    A(PE, "matmul", bank(4 + j), lhsT=Vaug[:, kt, h, :], rhs=PT[pb_][:, j * 512:(j + 1) * 512],
                      start=(kt == 0), stop=(kt == TT - 1), reads=[("PT", pb_), ("V", kt, h % 2)], writes=["ps%d" % (4 + j)])
                if kt == TT - 1:
                    while pending:
                        pending.pop(0)[1]()
                    for j in range(NJ):
                        A(DVE, "tensor_copy", out=Osb[:, j * 512:(j + 1) * 512], in_=bank(4 + j), reads=["ps%d" % (4 + j)],
                          writes=[("Osb", j)])
                    cq0 = qp * QW
                    nr = slice(0, 64) if h % 2 == 0 else slice(64, 128)
                    dr = slice(64, 128) if h % 2 == 0 else slice(0, 64)
                    NCH = QW // 256

                    def mk_rec(c, nr=nr, dr=dr):
                        def f():
                            cc = slice(c * 256, (c + 1) * 256)
                            A(DVE, "reciprocal", out=rec[nr, cc], in_=Osb[dr, cc], reads=[("Osb", c // 2)], writes=[("rec", c)])
                        return f

                    def mk_tg(c, nr=nr, h=h, cq0=cq0):
                        def f():
                            cc = slice(c * 256, (c + 1) * 256)
                            A(POOL, "tensor_tensor", out=tg_[nr, cc], in0=rec[nr, cc], in1=sga[nr, h // 2, cq0 + c * 256:cq0 + (c + 1) * 256],
                              op=ALU.mult, reads=[("rec", c)] + [("sga", h // 2, bl) for bl in range(NBLK)], writes=[("tg", c)])
                        return f

                    def mk_fin(c, nr=nr, h=h, qp=qp, cq0=cq0):
                        def f():
                            cc = slice(c * 256, (c + 1) * 256)
                            A(DVE, "tensor_tensor", out=yaT[nr, h // 2, cq0 + c * 256:cq0 + (c + 1) * 256], in0=Osb[nr, cc], in1=tg_[nr, cc],
                              op=ALU.mult, reads=[("Osb", c // 2), ("tg", c)], writes=[("yaT", h, qp, c)])
                        return f
                    offs = [3, 4, 5, 10, 11, 12]
                    for c in range(NCH + 1):
                        fs = []
                        if c < NCH:
                            fs += [mk_rec(c), mk_tg(c)]
                        if c >= 1:
                            fs.append(mk_fin(c - 1))
                        for f in fs:
                            pending.append((i + offs[min(c, len(offs) - 1)], f))
                while pending and pending[0][0] <= i:
                    pending.pop(0)[1]()
                gi = qp * TT + kt
                if h + 1 < NH:
                    for blk in range(NBLK):
                        for si in range(4):
                            so = (0, 3, 5, 6)[si] if gap >= 7 else si
                            if min(blk * gap + so, NG - 3) == gi:
                                produce_stages(h + 1, blk)[si]()

            while pending:
                pending.pop(0)[1]()
            ckpt("ATT")
            Sd.barrier(lambda e: e.memset(small[:, 8:9], 0.0))
            for tt in range(TT):
                xb = xo[tt % 4]
                tb = to[tt % 2]
                A(SP, "dma_start", out=xb, in_=x_d[b, tt * 128:(tt + 1) * 128, :],
                  writes=[("xo", tt % 4)], dma=True)
                for n in range(2):
                    bk = (tt % 2) * 2 + n
                    for c in range(KC):
                        src = ypT[:, c, tt * 128:(tt + 1) * 128] if c < 4 else yaT[:, c - 4, tt * 128:(tt + 1) * 128]
                        rk = [("ypT", c, tt // 4)] if c < 4 else [("yaT", 2 * (c - 4) + hh_, tt * 128 // QW, (tt * 128 % QW) // 256) for hh_ in range(2)]
                        A(PE, "matmul", bank(bk), lhsT=src, rhs=Wo[:, c, n * 512:(n + 1) * 512],
                                                                          start=(c == 0), stop=(c == KC - 1),
                          reads=rk + [("Wo", c)], writes=["ps%d" % bk])
                    A(DVE, "tensor_tensor", out=tb[:, n * 512:(n + 1) * 512], in0=bank(bk),
                                                                        in1=gate_bc[b][:, n * 512:(n + 1) * 512], op=ALU.mult,
                      reads=["ps%d" % bk, ("gate_bc", b, 4), ("gate_bc", b, 5)], writes=[("to", tt % 2, n)])
                A(POOL, "tensor_tensor", out=tb, in0=tb, in1=xb, op=ALU.add,
                  reads=[("to", tt % 2, 0), ("to", tt % 2, 1), ("xo", tt % 4)], writes=[("to", tt % 2, 0), ("to", tt % 2, 1)])
                A(POOL, "dma_start", out=out_d[b, tt * 128:(tt + 1) * 128, :], in_=tb,
                  reads=[("to", tt % 2, 0), ("to", tt % 2, 1)], dma=True)

            Sd.barrier(lambda e: e.memset(small[:, 8:9], 0.0))

        run = Sd.emit(sems, dsems)
        with nc.Block() as block:
            @block.sync
            def _(e):
                run(SP, e)

            @block.tensor
            def _(e):
                run(PE, e)

            @block.scalar
            def _(e):
                run(ACT, e)

            @block.vector
            def _(e):
                run(DVE, e)

            @block.gpsimd
            def _(e):
                run(POOL, e)
    return nc


def make_in_maps(inputs, S, NB, n_cores):
    hc = host_consts(S)
    maps = []
    for core in range(n_cores):
        m = host_layout(inputs, S, NB, core)
        m.update(hc)
        maps.append(m)
    return maps


def kernel(**inputs):
    NB = BATCH // N_CORES
    nc = build(SEQ, NB)
    in_maps = make_in_maps(inputs, SEQ, NB, N_CORES)
    res = run_bass_kernel_spmd(nc, in_maps, core_ids=list(range(N_CORES)))
    out = np.concatenate([np.asarray(r["out"]) for r in res.results], axis=0)
    return out.astype(np.float32)
```

```python
import math
from contextlib import ExitStack

import numpy as np
import ml_dtypes

import concourse.bass as bass
import concourse.mybir as mybir
from concourse.bass_utils import run_bass_kernel_spmd

F32 = mybir.dt.float32
BF16 = mybir.dt.bfloat16
I32 = mybir.dt.int32
U8 = mybir.dt.uint8
AF = mybir.ActivationFunctionType
ALU = mybir.AluOpType
AX = mybir.AxisListType

PE, ACT, DVE, POOL, SP = "pe", "act", "dve", "pool", "sp"

D = 1024
KC = 8
NH = 8
QKH = 96
EPS = 1e-6
N_CORES = 8
BATCH = 16
SEQ = 2048
NCOL = 2112
POOL_W = (2, 4, 8, 16)
FILL_QK = 0


class Ins:
    __slots__ = ("eng", "fn", "deps", "needs_inc", "count", "is_dma", "dsem", "dval", "dprev", "pos")

    def __init__(self, eng, fn, is_dma):
        self.eng = eng
        self.fn = fn
        self.deps = []
        self.needs_inc = False
        self.count = 0
        self.is_dma = is_dma
        self.dsem = None
        self.dval = 0
        self.dprev = 0
        self.pos = 0


class Sched:
    def __init__(self):
        self.streams = {PE: [], ACT: [], DVE: [], POOL: [], SP: []}
        self.last_writer = {}
        self.readers = {}
        self.n = 0
        self.cur_barrier = None
        self.dmas_since_barrier = []

    def add(self, eng, _opname, *args, reads=(), writes=(), dma=False, **kw):
        if not getattr(self, "enabled", True):
            return None
        fn = (lambda e: getattr(e, _opname)(*args, **kw))
        pk = [k for k in reads if isinstance(k, str) and k.startswith("ps")]
        if pk:
            writes = list(writes) + [k for k in pk if k not in writes]
        ins = Ins(eng, fn, dma)
        ins.pos = self.n
        self.n += 1
        deps = {}
        for k in reads:
            w = self.last_writer.get(k)
            if w is not None:
                deps[id(w)] = w
        for k in writes:
            w = self.last_writer.get(k)
            if w is not None:
                deps[id(w)] = w
            for r in self.readers.get(k, ()):
                deps[id(r)] = r
        for k in reads:
            self.readers.setdefault(k, []).append(ins)
        for k in writes:
            self.last_writer[k] = ins
            self.readers[k] = []
        if self.cur_barrier is not None:
            deps[id(self.cur_barrier)] = self.cur_barrier
        for d in deps.values():
            if d is ins:
                continue
            if (not d.is_dma) and (not dma) and d.eng == eng and eng == PE:
                continue
            ins.deps.append(d)
        self.streams[eng].append(ins)
        if dma:
            self.dmas_since_barrier.append(ins)
        return ins

    def barrier(self, fn):
        if not getattr(self, "enabled", True):
            return None
        ins = Ins(DVE, fn, False)
        ins.pos = self.n
        self.n += 1
        for e, st in self.streams.items():
            for prev in reversed(st):
                if not prev.is_dma:
                    ins.deps.append(prev)
                    break
        ins.deps.extend(self.dmas_since_barrier)
        if self.cur_barrier is not None:
            ins.deps.append(self.cur_barrier)
        self.dmas_since_barrier = []
        self.streams[DVE].append(ins)
        self.cur_barrier = ins
        self.last_writer = {}
        self.readers = {}
        return ins

    def emit(self, sems, dma_sems):
        for st in self.streams.values():
            for ins in st:
                for d in ins.deps:
                    d.needs_inc = True
        for e, st in self.streams.items():
            c = 0
            for ins in st:
                if ins.is_dma:
                    continue
                if ins.needs_inc:
                    c += 1
                    ins.count = c
        alld = [i for st in self.streams.values() for i in st if i.is_dma]
        alld.sort(key=lambda i: i.pos)
        cum = [0] * len(dma_sems)
        n_sw = 4
        n_hw = len(dma_sems) - n_sw
        jc = {True: 0, False: 0}
        for ins in alld:
            sw = ins.eng == POOL
            if sw:
                s = n_hw + jc[True] % n_sw
            else:
                s = jc[False] % n_hw
            jc[sw] += 1
            ins.dsem = s
            ins.dprev = cum[s]
            cum[s] += 16
            ins.dval = cum[s]

        def run_stream(e, eng):
            waited = {}
            for ins in self.streams[e]:
                reqs = {}
                for d in ins.deps:
                    if d.is_dma:
                        key = ("d", d.dsem)
                        v = d.dval
                    else:
                        key = ("e", d.eng)
                        v = d.count
                    if v > reqs.get(key, 0):
                        reqs[key] = v
                if ins.is_dma and ins.dprev > 0:
                    key = ("d", ins.dsem)
                    if ins.dprev > reqs.get(key, 0):
                        reqs[key] = ins.dprev
                for key, v in reqs.items():
                    if waited.get(key, 0) >= v:
                        continue
                    waited[key] = v
                    sem = dma_sems[key[1]] if key[0] == "d" else sems[key[1]]
                    eng.wait_ge(sem, v)
                bi = ins.fn(eng)
                if ins.is_dma:
                    bi.then_inc(dma_sems[ins.dsem], 16)
                elif ins.needs_inc:
                    bi.then_inc(sems[e], 1)
            last = {}
            for ins in self.streams[e]:
                if ins.is_dma:
                    last[ins.dsem] = max(last.get(ins.dsem, 0), ins.dval)
            for s, v in last.items():
                if waited.get(("d", s), 0) < v:
                    eng.wait_ge(dma_sems[s], v)

        return run_stream


class Region:
    def __init__(self, base_ap, nbytes):
        self.base = base_ap
        self.nbytes = nbytes
        self.off = 0

    def reset(self, off=0):
        self.off = off

    def alloc(self, shape_free, dt):
        esz = {F32: 4, BF16: 2, I32: 4}[dt]
        n = 1
        for s in shape_free:
            n *= s
        nb = (n * esz + 31) // 32 * 32
        assert self.off + nb <= self.nbytes, (self.off, nb, self.nbytes)
        v = self.base[:, self.off:self.off + n * esz].bitcast(dt)
        self.off += nb
        if len(shape_free) == 2:
            v = v.rearrange("p (a b) -> p a b", a=shape_free[0])
        elif len(shape_free) == 3:
            v = v.rearrange("p (a b c) -> p a b c", a=shape_free[0], b=shape_free[1])
        return v


def band_consts(S):
    out = {}
    t = np.arange(S)
    for gi, w in enumerate(POOL_W):
        lo = np.clip(t - w // 2, 0, S)
        hi = np.clip(t + (w - w // 2), 0, S)
        B = np.zeros((S, S), np.float32)
        for tt in range(S):
            B[lo[tt]:hi[tt], tt] = 1.0 / float(hi[tt] - lo[tt])
            B[tt, tt] -= 1.0
        out[gi] = dict(
            first=B[0:128, 0:128], mid=B[128:256, 128:256], last=B[S - 128:S, S - 128:S],
            left=B[0:128, 128:136], right=B[256:384, 248:256])
    return out


def host_consts(S):
    bands = band_consts(S)
    bt = np.zeros((128, 4, 400), np.float32)
    for g in range(4):
        bt[:, g, 0:128] = bands[g]["first"]
        bt[:, g, 128:256] = bands[g]["mid"]
        bt[:, g, 256:384] = bands[g]["last"]
        bt[:, g, 384:392] = bands[g]["left"]
        bt[:, g, 392:400] = bands[g]["right"]
    inv = (10000.0 ** (-np.arange(0, 32, 2, dtype=np.float32) / 32.0)).astype(np.float32)
    col = np.zeros((128, 6), np.float32)
    for i in range(32):
        col[64 + i, 0] = inv[i % 16]
        col[64 + i, 1] = -1.0 if i < 16 else 1.0
    for p in range(128):
        col[p, 4] = inv[(p % 32) % 16]
        col[p, 5] = -1.0 if (p % 32) < 16 else 1.0
    col[:, 2] = EPS
    col[:, 3] = -0.5 * math.log(96.0)
    return dict(
        ident=np.eye(128, dtype=np.float32).astype(ml_dtypes.bfloat16),
        bands=bt.astype(ml_dtypes.bfloat16),
        colc=col,
    )


def host_layout(inp, S, NB, core):
    f = lambda a: np.ascontiguousarray(np.asarray(a), dtype=np.float32)
    b0 = core * NB
    x = f(inp["x"])[b0:b0 + NB]
    c = f(inp["c"])[b0:b0 + NB]
    pos = np.ascontiguousarray(np.asarray(inp["positions"]), dtype=np.int32)[b0:b0 + NB]
    w_in = f(inp["w_in"])[0]
    cols = np.concatenate([
        np.arange(0, 1408),
        np.arange(512, 576), np.arange(1408, 1424), np.arange(1424, 1440),
        np.arange(512, 576), np.arange(1424, 1440), np.arange(1408, 1424),
        np.arange(1440, 1952)])
    assert cols.size == NCOL
    w_uq = f(inp["w_uq"])[0]
    uq_cols = []
    for h in range(NH):
        uq_cols.append(np.arange(h * 96, h * 96 + 96))
    for h in range(NH):
        uq_cols.append(np.concatenate([np.arange(h * 96, h * 96 + 64), np.arange(h * 96 + 80, h * 96 + 96),
                                       np.arange(h * 96 + 64, h * 96 + 80)]))
    uq_cols = np.concatenate(uq_cols)
    w_ukv = f(inp["w_ukv"])[0]
    kv_cols = np.concatenate([np.concatenate([np.arange(h * 128 + 64, h * 128 + 128) for h in range(NH)]),
                              np.concatenate([np.arange(h * 128, h * 128 + 64) for h in range(NH)])])
    gq = f(inp["g_qnorm"])[0]
    gk = f(inp["g_knorm"])[0]
    vec = np.zeros((128, 64), np.float32)
    vec[:, 0:8] = f(inp["norm_g"])[0].reshape(8, 128).T
    vec[:, 8:32] = f(inp["ada_b"])[0].reshape(24, 128).T
    vec[:, 32:36] = f(inp["pool_scale"])[0].reshape(4, 128).T
    vec[:, 36:38] = f(inp["g_q_lat"])[0].reshape(2, 128).T
    vec[:, 38] = f(inp["g_kv_lat"])[0]
    vec[0:96, 39] = gq
    vec[0:64, 40] = gq[0:64]
    vec[64:80, 40] = gq[80:96]
    vec[80:96, 40] = gq[64:80]
    vec[64:96, 41] = gk[64:96]
    vec[64:80, 42] = gk[80:96]
    vec[80:96, 42] = gk[64:80]
    rows = np.zeros((1, 192), np.float32)
    rows[0, 0:96] = gq
    rows[0, 96:192] = gk
    return dict(
        x=x, cT=np.ascontiguousarray(c.T.reshape(KC, 128, NB).transpose(1, 0, 2)), pos=pos,
        ada_w=f(inp["ada_w"])[0], w_in=np.ascontiguousarray(w_in[:, cols]),
        w_uq=np.ascontiguousarray(w_uq[:, uq_cols]), w_ukv=np.ascontiguousarray(w_ukv[:, kv_cols]),
        w_out=f(inp["w_out"])[0], pool_w=np.ascontiguousarray(f(inp["pool_w"])[0].transpose(1, 0, 2)),
        vec=vec, rows=rows, gkn=np.ascontiguousarray(np.broadcast_to(gk[None, 0:64], (128, 64))),
        adab_gate=f(inp["ada_b"])[0][None, 2048:3072].copy(),
    )


def build(S=SEQ, NB=2):
    TT = S // 128
    NBLK = S // 512
    QW = min(1024, S)
    NQP = S // QW
    NJ = QW // 512
    nc = bass.Bass("TRN2", target_bir_lowering=False)

    def din(name, shape, dt=F32):
        return nc.dram_tensor(name, list(shape), dt, kind="ExternalInput").ap()

    x_d = din("x", [NB, S, D])
    cT_d = din("cT", [128, KC, NB])
    pos_d = din("pos", [NB, S], I32)
    adaw_d = din("ada_w", [D, 3 * D])
    win_d = din("w_in", [D, NCOL])
    wuq_d = din("w_uq", [256, 1536])
    wukv_d = din("w_ukv", [128, 1024])
    wout_d = din("w_out", [D, D])
    poolw_d = din("pool_w", [128, 4, 128])
    vec_d = din("vec", [128, 64])
    rows_d = din("rows", [1, 192])
    gkn_d = din("gkn", [128, 64])
    adabg_d = din("adab_gate", [1, 1024])
    ident_d = din("ident", [128, 128], BF16)
    bands_d = din("bands", [128, 4, 400], BF16)
    colc_d = din("colc", [128, 6])
    out_d = nc.dram_tensor("out", [NB, S, D], F32, kind="ExternalOutput").ap()
    wins_d = nc.dram_tensor("win_s", [128, KC * NCOL], BF16).ap()
    wouts_d = nc.dram_tensor("wout_s", [128, KC * D], BF16).ap()
    tabs_d = nc.dram_tensor("tab_s", [NB, 2, 32, S], BF16).ap()

    Sd = Sched()
    A = Sd.add
    import os as _os
    _stop = _os.environ.get("KSTOP", "")

    def ckpt(name):
        if name == _stop:
            Sd.enabled = False
    es = ExitStack()
    with es:
        XB = 122 * 1024
        PB = 84 * 1024
        Xt = es.enter_context(nc.sbuf_tensor("Xreg", [128, XB], U8))
        Pt = es.enter_context(nc.sbuf_tensor("Preg", [128, PB], U8))
        ps = es.enter_context(nc.psum_tensor("ps", [128, 4096], F32))
        sems = {e: es.enter_context(nc.semaphore("s_" + e)) for e in (PE, ACT, DVE, POOL, SP)}
        dsems = [es.enter_context(nc.semaphore("dq%d" % i)) for i in range(24)]
        X = Region(Xt[:], XB)
        P = Region(Pt[:], PB)

        def bank(i, n=512, off=0):
            return ps[:, i * 512 + off:i * 512 + off + n]

        wuq = P.alloc([2, 1536], BF16)
        wukv = P.alloc([1024], BF16)
        wukvg = P.alloc([512], BF16)
        poolw = P.alloc([4, 128], BF16)
        ident = P.alloc([128], BF16)
        bands = P.alloc([4, 400], BF16)
        ones_bf = P.alloc([128], BF16)
        ones_f = P.alloc([128], F32)
        vec = P.alloc([64], F32)
        colc = P.alloc([6], F32)
        rows = P.alloc([192], F32)
        negB = P.alloc([1], F32)
        cact = P.alloc([KC, NB], F32)
        mod = P.alloc([24, NB], F32)
        gs = P.alloc([KC, NB], F32)
        COS = P.alloc([S], BF16)
        SINS = P.alloc([S], BF16)
        ypT = P.alloc([4, S], BF16)
        sga = P.alloc([4, S], BF16)
        qln = P.alloc([2, S], BF16)
        kvn = P.alloc([S], BF16)
        KR = P.alloc([S], BF16)
        krstat = P.alloc([TT], F32)
        kss = P.alloc([TT, NH], F32)
        sck = P.alloc([TT, NH], F32)
        gate_bc = [P.alloc([1024], F32) for _ in range(NB)]
        rstat = P.alloc([TT, 2], F32)
        small = P.alloc([16], F32)
        P_end = P.off

        norm_g = vec[:, 0:8]
        adab = vec[:, 8:32]
        pscale = vec[:, 32:36]
        gql = vec[:, 36:38]
        gkvl = vec[:, 38:39]
        gq96 = vec[:, 39:40]
        gqs96 = vec[:, 40:41]
        gkr = vec[:, 41:42]
        gkrs = vec[:, 42:43]
        invf = colc[:, 0:1]
        sgn = colc[:, 1:2]
        epsc = colc[:, 2:3]
        lnsc = colc[:, 3:4]
        invf4 = colc[:, 4:5]
        sgn4 = colc[:, 5:6]

        rr = [0]

        def evac_eng():
            rr[0] += 1
            return ACT if rr[0] % 2 else DVE

        def copy_on(eng_name, out, in_, reads, writes):
            if eng_name == ACT:
                A(ACT, "activation", out=out, in_=in_, func=AF.Copy, reads=reads, writes=writes)
            else:
                A(eng_name, "tensor_copy", out=out, in_=in_, reads=reads, writes=writes)

        X.reset()
        A(SP, "dma_start", out=vec, in_=vec_d, writes=["vec"], dma=True)
        A(SP, "dma_start", out=colc, in_=colc_d, writes=["colc"], dma=True)
        A(SP, "dma_start", out=rows[0:1, :], in_=rows_d, writes=["rows"], dma=True)
        A(SP, "dma_start", out=ident, in_=ident_d, writes=["ident"], dma=True)
        A(SP, "dma_start", out=bands, in_=bands_d, writes=["bands"], dma=True)
        A(SP, "dma_start", out=cact, in_=cT_d, writes=["cact"], dma=True)
        NQ4 = S // 512
        posi_all = X.alloc([NB, 512], I32)
        for b in range(NB):
            for q4 in range(NQ4):
                A(SP, "dma_start", out=posi_all[32 * q4:32 * q4 + 32, b, :],
                  in_=pos_d[b:b + 1, q4 * 512:(q4 + 1) * 512].partition_broadcast(32), writes=[("posi", b)], dma=True)
        A(POOL, "memset", ones_bf, 1.0, writes=["ones_bf"])
        A(POOL, "memset", ones_f, 1.0, writes=["ones_f"])
        A(ACT, "activation", out=cact, in_=cact, func=AF.Silu, reads=["cact"], writes=["cact"])

        A(DVE, "tensor_reduce", out=small[0:1, 0:2], in_=rows[0:1, :].rearrange("p (a b) -> p a b", a=2),
                                         axis=AX.X, op=ALU.max, apply_absolute_value=True,
          reads=["rows"], writes=["small"])
        A(DVE, "scalar_tensor_tensor", out=small[0:1, 2:3], in0=small[0:1, 0:1], scalar=-math.sqrt(96.0),
                                                in1=small[0:1, 1:2], op0=ALU.mult, op1=ALU.mult,
          reads=["small"], writes=["small2"])
        A(PE, "matmul", bank(7, 1), lhsT=ones_f[0:1, :], rhs=small[0:1, 2:3], start=True, stop=True,
          reads=["small2", "ones_f"], writes=["ps7"])
        A(DVE, "tensor_copy", out=negB, in_=bank(7, 1), reads=["ps7"], writes=["negB"])

        ckpt("s1")
        stg_uq = X.alloc([2, 1536], F32)
        stg_kv = X.alloc([1024], F32)
        stg_pw = X.alloc([4, 128], F32)
        gkn = X.alloc([64], F32)
        A(SP, "dma_start", out=stg_uq, in_=wuq_d.rearrange("(k p) c -> p k c", p=128), writes=["stg_uq"], dma=True)
        A(SP, "dma_start", out=stg_kv, in_=wukv_d, writes=["stg_kv"], dma=True)
        A(SP, "dma_start", out=stg_pw, in_=poolw_d, writes=["stg_pw"], dma=True)
        A(SP, "dma_start", out=gkn, in_=gkn_d, writes=["gkn"], dma=True)
        A(DVE, "tensor_copy", out=wuq, in_=stg_uq, reads=["stg_uq"], writes=["wuq"])
        A(DVE, "tensor_copy", out=wukv, in_=stg_kv, reads=["stg_kv"], writes=["wukv"])
        A(DVE, "tensor_copy", out=poolw, in_=stg_pw, reads=["stg_pw"], writes=["poolw"])
        A(DVE, "tensor_tensor", out=wukvg.rearrange("p (h d) -> p h d", h=NH),
                                         in0=stg_kv[:, 512:1024].rearrange("p (h d) -> p h d", h=NH),
                                         in1=gkn.unsqueeze(1).to_broadcast([128, NH, 64]), op=ALU.mult,
          reads=["stg_kv", "gkn"], writes=["wukvg"])

        ckpt("s2")
        CW = 264
        NSTG = 2
        stg_w = [X.alloc([KC, CW], F32) for _ in range(NSTG)]
        stg_wb = [X.alloc([KC, CW], BF16) for _ in range(NSTG)]
        wins_v = wins_d.rearrange("p (k c) -> p k c", k=KC)
        wouts_v = wouts_d.rearrange("p (k c) -> p k c", k=KC)
        jobs = [(win_d, wins_v, ci * CW, CW) for ci in range(NCOL // CW)]
        jobs += [(wout_d, wouts_v, ci * 256, 256) for ci in range(4)]
        for j, (src, dst, c0, cw) in enumerate(jobs):
            sf = stg_w[j % NSTG]
            sbf = stg_wb[j % NSTG]
            A(SP, "dma_start", out=sf[:, :, 0:cw], in_=src[:, c0:c0 + cw].rearrange("(k p) c -> p k c", p=128),
              writes=[("stg_w", j % NSTG)], dma=True)
            copy_on(POOL, sbf[:, :, 0:cw], sf[:, :, 0:cw], [("stg_w", j % NSTG)], [("stg_wb", j % NSTG)])
            A(POOL, "dma_start", out=dst[:, :, c0:c0 + cw], in_=sbf[:, :, 0:cw],
              reads=[("stg_wb", j % NSTG)], dma=True)

        late_dve = []
        stg_ada = [X.alloc([KC, 512], F32) for _ in range(2)]
        cbc = [X.alloc([KC, 128], F32) for _ in range(NB)]
        adabg = X.alloc([1024], F32)
        A(SP, "dma_start", out=adabg, in_=adabg_d.partition_broadcast(128), writes=["adabg"], dma=True)
        for b in range(NB):
            A(DVE, "tensor_copy", out=cbc[b], in_=cact[:, :, b:b + 1].to_broadcast([128, KC, 128]),
              reads=["cact"], writes=[("cbc", b)])
        for ec in range(6):
            st = stg_ada[ec % 2]
            A(ACT, "dma_start", out=st, in_=adaw_d[:, ec * 512:(ec + 1) * 512].rearrange("(k p) c -> p k c", p=128),
              writes=[("stg_ada", ec % 2)], dma=True)
            for mt in range(4):
                m = ec * 4 + mt
                for kc in range(KC):
                    A(PE, "matmul",
                        bank(0, NB, m * NB), lhsT=st[:, kc, mt * 128:(mt + 1) * 128], rhs=cact[:, kc, :],
                        start=(kc == 0), stop=(kc == KC - 1),
                      reads=[("stg_ada", ec % 2), "cact"], writes=["ps0"])
            if ec >= 4:
                for b in range(NB):
                    bk = 1 + b * 2 + (ec - 4)
                    for kc in range(KC):
                        A(PE, "matmul",
                            bank(bk), lhsT=cbc[b][:, kc, :], rhs=st[:, kc, :], start=(kc == 0), stop=(kc == KC - 1),
                          reads=[("stg_ada", ec % 2), ("cbc", b)], writes=["ps%d" % bk])
                    late_dve.append((lambda b=b, bk=bk, ec=ec: A(DVE, "tensor_tensor",
                        out=gate_bc[b][:, (ec - 4) * 512:(ec - 3) * 512], in0=bank(bk),
                        in1=adabg[:, (ec - 4) * 512:(ec - 3) * 512], op=ALU.add,
                      reads=["ps%d" % bk, "adabg"], writes=[("gate_bc", b, ec)])))
        late_dve.append(lambda: A(DVE, "tensor_tensor", out=mod, in0=bank(0, 24 * NB).rearrange("p (m b) -> p m b", b=NB),
                                         in1=adab.unsqueeze(2).to_broadcast([128, 24, NB]), op=ALU.add,
          reads=["ps0", "vec"], writes=["mod"]))
        late_dve.append(lambda: A(DVE, "scalar_tensor_tensor", out=gs, in0=mod[:, 8:16, :], scalar=1.0,
                                                in1=norm_g.unsqueeze(2).to_broadcast([128, KC, NB]),
                                                op0=ALU.add, op1=ALU.mult,
          reads=["mod", "vec"], writes=["gs"]))

        R = slice(64, 96)
        PQ = slice(0, 32 * NQ4)
        tmpc = [(X.alloc([512], F32), X.alloc([512], F32), X.alloc([512], I32), X.alloc([512], BF16)) for _ in range(2)]
        for b in reversed(range(NB)):
            posi = posi_all[PQ, b, :]
            chains = []
            for ci, (which, shift) in enumerate((("cos", math.pi / 2), ("sin", 0.0))):
                ang, angk, angi, tout = tmpc[ci]
                ang, angk, angi, tout = ang[PQ, :], angk[PQ, :], angi[PQ, :], tout[PQ, :]
                ka, kk_, ki, ko = ("ang", ci), ("angk", ci), ("angi", ci), ("tout", ci)
                ops = [
                    (DVE, "tensor_copy", dict(out=ang, in_=posi), [("posi", b)], [ka]),
                    (DVE, "tensor_scalar", dict(out=ang, in0=ang, scalar1=invf4[PQ, :], scalar2=shift,
                                                op0=ALU.mult, op1=ALU.add), [ka, "colc"], [ka]),
                    (DVE, "tensor_scalar", dict(out=angi, in0=ang, scalar1=1.0 / (2 * math.pi), scalar2=None,
                                                op0=ALU.mult), [ka], [ki]),
                    (DVE, "tensor_copy", dict(out=angk, in_=angi), [ki], [kk_]),
                    (DVE, "scalar_tensor_tensor", dict(out=ang, in0=angk, scalar=-2 * math.pi, in1=ang,
                                                       op0=ALU.mult, op1=ALU.add), [kk_, ka], [ka]),
                    (DVE, "tensor_scalar", dict(out=angk, in0=ang, scalar1=math.pi, scalar2=None, op0=ALU.is_gt), [ka], [kk_]),
                    (DVE, "scalar_tensor_tensor", dict(out=ang, in0=angk, scalar=-2 * math.pi, in1=ang,
                                                       op0=ALU.mult, op1=ALU.add), [kk_, ka], [ka]),
                    (DVE, "tensor_scalar", dict(out=angk, in0=ang, scalar1=-math.pi, scalar2=None, op0=ALU.is_lt), [ka], [kk_]),
                    (DVE, "scalar_tensor_tensor", dict(out=ang, in0=angk, scalar=2 * math.pi, in1=ang,
                                                       op0=ALU.mult, op1=ALU.add), [kk_, ka], [ka]),
                ]
                if which == "cos":
                    ops.append((ACT, "activation", dict(out=tout, in_=ang, func=AF.Sin), [ka], [ko]))
                else:
                    ops.append((ACT, "activation", dict(out=ang, in_=ang, func=AF.Sin), [ka], [ka]))
                    ops.append((DVE, "tensor_scalar", dict(out=tout, in0=ang, scalar1=sgn4[PQ, :], scalar2=None,
                                                           op0=ALU.mult), [ka, "colc"], [ko]))
                chains.append(ops)
            for k_ in range(max(len(c) for c in chains)):
                for c in chains:
                    if k_ < len(c):
                        eng_, op_, kw_, rd_, wr_ = c[k_]
                        A(eng_, op_, reads=rd_, writes=wr_, **kw_)
            for ci, dst in enumerate((COS, SINS)):
                tout = tmpc[ci][3]
                for q4 in range(NQ4):
                    if b > 0:
                        A(SP, "dma_start", out=tabs_d[b, ci, :, q4 * 512:(q4 + 1) * 512], in_=tout[32 * q4:32 * q4 + 32, :],
                          reads=[("tout", ci)], dma=True)
                    else:
                        A(SP, "dma_start", out=dst[R, q4 * 512:(q4 + 1) * 512], in_=tout[32 * q4:32 * q4 + 32, :],
                          reads=[("tout", ci)], writes=[("tab", ci, q4)], dma=True)

        for f_ in late_dve:
            f_()
        Sd.barrier(lambda e: e.memset(small[:, 8:9], 0.0))

        ckpt("setup")
        for b in range(NB):
            X.reset()
            Wb = X.alloc([KC, NCOL], BF16)
            xnT = X.alloc([KC, S], BF16)
            mark = X.off
            xt = [X.alloc([1024], F32) for _ in range(8)]
            junk = X.alloc([1024], BF16)
            xr = [X.alloc([1024], BF16) for _ in range(2)]
            X.reset(mark)
            u_tok = X.alloc([TT, 512], BF16)
            sgp = X.alloc([4, S], BF16)
            diffT = [X.alloc([512], BF16) for _ in range(2)]
            krg = X.alloc([512], F32)
            krsg = X.alloc([512], F32)
            krsq = X.alloc([512], BF16)
            kt1 = X.alloc([512], F32)
            qlg = X.alloc([2, 512], BF16)
            qlsq = X.alloc([2, 512], BF16)
            kvg = X.alloc([512], BF16)
            kvsq = X.alloc([512], BF16)
            lnr = X.alloc([512], F32)
            Rq = X.alloc([512], BF16)

            A(POOL, "memset", rstat, 0.0, writes=["rstat0"])

            R = slice(64, 96)
            if b > 0:
                A(SP, "dma_start", out=COS[R, :], in_=tabs_d[b, 0], writes=["COS"], dma=True)
                A(SP, "dma_start", out=SINS[R, :], in_=tabs_d[b, 1], writes=["SINS"], dma=True)
            ckpt("tables")
            _ska = set(_os.environ.get("KSKIPA", "").split(","))
            for tg in range(TT // 4):
                for j in range(4):
                    tt = tg * 4 + j
                    xb = xt[tt % 8]
                    A(SP, "dma_start", out=xb, in_=x_d[b, tt * 128:(tt + 1) * 128, :],
                      writes=[("xt", tt % 8)], dma=True)
                    if tt % 2 == 1 and tt // 2 < KC:
                        A(SP, "dma_start", out=Wb[:, tt // 2, :], in_=wins_v[:, tt // 2, :], writes=[("Wb", tt // 2)], dma=True)
                    A(ACT, "activation", out=junk, in_=xb, func=AF.Square, accum_out=rstat[:, tt, 0:1],
                      reads=[("xt", tt % 8), "rstat0"], writes=["junk", ("rs", tt)])
                g4 = slice(tg * 4, tg * 4 + 4)
                A(ACT, "activation", out=rstat[:, g4, 1:2], in_=rstat[:, g4, 0:1], func=AF.Ln, bias=epsc, scale=1.0 / D,
                  reads=[("rs", tg * 4 + j_) for j_ in range(4)] + ["colc"], writes=[("rl", tg)])
                A(ACT, "activation", out=rstat[:, g4, 0:1], in_=rstat[:, g4, 1:2], func=AF.Exp, scale=-0.5,
                  reads=[("rl", tg)], writes=[("rr", tg)])
                for j in range(4):
                    tt = tg * 4 + j
                    xb = xt[tt % 8]
                    if "xr" not in _ska: A(DVE, "tensor_scalar", out=xr[tt % 2], in0=xb, scalar1=rstat[:, tt, 0:1],
                                                                   scalar2=None, op0=ALU.mult,
                      reads=[("xt", tt % 8), ("rr", tg)], writes=[("xr", tt % 2)])
                    pst = bank(j).bitcast(BF16)
                    for kc in range(KC):
                        if "tr" not in _ska: A(PE, "transpose", out=pst[:, kc * 128:(kc + 1) * 128],
                                                                           in_=xr[tt % 2][:, kc * 128:(kc + 1) * 128],
                                                                           identity=ident,
                          reads=[("xr", tt % 2), "ident"], writes=["ps%d" % j])
                if tg == TT // 4 - 1:
                    for kc in range(TT // 2, KC):
                        A(SP, "dma_start", out=Wb[:, kc, :], in_=wins_v[:, kc, :], writes=[("Wb", kc)], dma=True)
                src_all = ps[:, 0:2048].bitcast(BF16).rearrange("p (j k t) -> p j k t", j=4, k=KC)
                for kc in range(KC):
                    dst = xnT[:, kc, tg * 512:(tg + 1) * 512].rearrange("p (j t) -> p j t", j=4)
                    A(ACT, "activation", out=dst[:, 0:2, :], in_=src_all[:, 0:2, kc, :], func=AF.Identity,
                      bias=mod[:, kc, b:b + 1], scale=gs[:, kc, b:b + 1],
                      reads=["ps0", "ps1", "gs", "mod"], writes=[("xnT", kc, tg, 0)])
                    A(DVE, "tensor_scalar", out=dst[:, 2:4, :], in0=src_all[:, 2:4, kc, :], scalar1=gs[:, kc, b:b + 1],
                      scalar2=mod[:, kc, b:b + 1], op0=ALU.mult, op1=ALU.add,
                      reads=["ps2", "ps3", "gs", "mod"], writes=[("xnT", kc, tg, 1)])

            Sd.barrier(lambda e: e.memset(small[:, 8:9], 0.0))

            ckpt("A")
            xn_keys = lambda tg: [("xnT", kc, tg, hh) for kc in range(KC) for hh in range(2)]
            for tt in range(TT):
                bk = tt % 2
                for kc in range(KC):
                    A(PE, "matmul", bank(bk), lhsT=xnT[:, kc, tt * 128:(tt + 1) * 128],
                                                                  rhs=Wb[:, kc, 0:512], start=(kc == 0), stop=(kc == KC - 1),
                      reads=xn_keys(tt // 4) + [("Wb", kc)], writes=["ps%d" % bk])
                copy_on(evac_eng(), u_tok[:, tt, :], bank(bk), ["ps%d" % bk], [("u_tok", tt)])

            ckpt("Bi")
            def fm_tile(c0, m, blk, bk):
                for kc in range(KC):
                    A(PE, "matmul", bank(bk)[0:m, :], lhsT=Wb[:, kc, c0:c0 + m],
                                                    rhs=xnT[:, kc, blk * 512:(blk + 1) * 512],
                                                    start=(kc == 0), stop=(kc == KC - 1),
                      reads=xn_keys(blk) + [("Wb", kc)], writes=["ps%d" % bk])

            for blk in range(NBLK):
                cs = slice(blk * 512, (blk + 1) * 512)
                nb_ = [2]

                def nxt():
                    nb_[0] = 2 + (nb_[0] - 1) % 6
                    return nb_[0]
                for g in range(4):
                    bk = nxt()
                    fm_tile(512 + g * 128, 128, blk, bk)
                    A(ACT, "activation", out=sgp[:, g, cs], in_=bank(bk), func=AF.Silu,
                      reads=["ps%d" % bk], writes=[("sgp", g, blk)])
                for g in range(4):
                    bk = nxt()
                    fm_tile(1600 + g * 128, 128, blk, bk)
                    A(ACT, "activation", out=sga[:, g, cs], in_=bank(bk), func=AF.Silu,
                      reads=["ps%d" % bk], writes=[("sga", g, blk)])
                for j in range(2):
                    bk = nxt()
                    fm_tile(1024 + j * 128, 128, blk, bk)
                    A(DVE, "tensor_scalar", out=qlg[:, j, :], in0=bank(bk), scalar1=gql[:, j:j + 1],
                                                                 scalar2=None, op0=ALU.mult,
                      reads=["ps%d" % bk, "vec"], writes=[("qlg", j)])
                    A(ACT, "activation", out=qlsq[:, j, :], in_=bank(bk), func=AF.Square,
                      reads=["ps%d" % bk], writes=[("qlsq", j)])
                bk = nxt()
                for j in range(2):
                    A(PE, "matmul", bank(bk), lhsT=ones_bf, rhs=qlsq[:, j, :], start=(j == 0), stop=(j == 1),
                      reads=[("qlsq", j), "ones_bf"], writes=["ps%d" % bk])
                A(ACT, "activation", out=lnr, in_=bank(bk), func=AF.Ln, bias=epsc, scale=1.0 / 256,
                  reads=["ps%d" % bk, "colc"], writes=["lnr"])
                A(ACT, "activation", out=Rq, in_=lnr, func=AF.Exp, scale=-0.5, reads=["lnr"], writes=["Rq"])
                for j in range(2):
                    A(DVE, "tensor_tensor", out=qln[:, j, cs], in0=qlg[:, j, :], in1=Rq, op=ALU.mult,
                      reads=[("qlg", j), "Rq"], writes=[("qln", j, blk)])
                bk = nxt()
                fm_tile(1280, 128, blk, bk)
                A(DVE, "tensor_scalar", out=kvg, in0=bank(bk), scalar1=gkvl, scalar2=None, op0=ALU.mult,
                  reads=["ps%d" % bk, "vec"], writes=["kvg"])
                A(ACT, "activation", out=kvsq, in_=bank(bk), func=AF.Square, reads=["ps%d" % bk], writes=["kvsq"])
                bk = nxt()
                A(PE, "matmul", bank(bk), lhsT=ones_bf, rhs=kvsq, start=True, stop=True,
                  reads=["kvsq", "ones_bf"], writes=["ps%d" % bk])
                A(ACT, "activation", out=lnr, in_=bank(bk), func=AF.Ln, bias=epsc, scale=1.0 / 128,
                  reads=["ps%d" % bk, "colc"], writes=["lnr"])
                A(ACT, "activation", out=Rq, in_=lnr, func=AF.Exp, scale=-0.5, reads=["lnr"], writes=["Rq"])
                A(DVE, "tensor_tensor", out=kvn[:, cs], in0=kvg, in1=Rq, op=ALU.mult,
                  reads=["kvg", "Rq"], writes=[("kvn", blk)])
                bk = nxt()
                fm_tile(1408, 96, blk, bk)
                A(ACT, "activation", out=krg[R, :], in_=bank(bk)[R, :], func=AF.Copy, scale=gkr[R, :],
                  reads=["ps%d" % bk, "vec"], writes=["krg"])
                A(ACT, "activation", out=krsq[R, :], in_=bank(bk)[R, :], func=AF.Square,
                  reads=["ps%d" % bk], writes=["krsq"])
                bk = nxt()
                fm_tile(1504, 96, blk, bk)
                A(ACT, "activation", out=krsg[R, :], in_=bank(bk)[R, :], func=AF.Copy, scale=gkrs[R, :],
                  reads=["ps%d" % bk, "vec"], writes=["krsg"])
                A(POOL, "tensor_tensor", out=kt1[R, :], in0=krg[R, :], in1=COS[R, cs], op=ALU.mult,
                  reads=["krg", "COS"], writes=["kt1"])
                A(POOL, "tensor_tensor", out=krsg[R, :], in0=krsg[R, :], in1=SINS[R, cs], op=ALU.mult,
                  reads=["krsg", "SINS"], writes=["krsg"])
                A(POOL, "tensor_tensor", out=KR[R, cs], in0=kt1[R, :], in1=krsg[R, :], op=ALU.add,
                  reads=["kt1", "krsg"], writes=[("KR", blk)])
                for j in range(4):
                    tt = blk * 4 + j
                    A(PE, "matmul", bank(1, 1, tt), lhsT=krsq[R, j * 128:(j + 1) * 128], rhs=ones_bf[R, 0:1],
                                                    start=True, stop=True,
                      reads=["krsq", "ones_bf"], writes=["ps1"])
            A(DVE, "tensor_copy", out=krstat, in_=bank(1, TT), reads=["ps1"], writes=["krstat"])

            ckpt("Bii")
            items = [(g, blk) for g in range(4) for blk in range(NBLK)]

            def band(idx):
                g, blk = items[idx]
                bk = 2 + idx % 2
                for j in range(4):
                    tt = blk * 4 + j
                    coff = 0 if tt == 0 else (256 if tt == TT - 1 else 128)
                    has_l = tt > 0
                    has_r = tt < TT - 1
                    A(PE, "matmul", bank(bk, 128, j * 128), lhsT=u_tok[:, tt, g * 128:(g + 1) * 128], rhs=bands[:, g, coff:coff + 128],
                      start=True, stop=not (has_l or has_r), reads=[("u_tok", tt), "bands"], writes=["ps%d" % bk])
                    if has_l:
                        A(PE, "matmul", bank(bk, 8, j * 128), lhsT=u_tok[:, tt - 1, g * 128:(g + 1) * 128], rhs=bands[:, g, 384:392],
                          start=False, stop=not has_r, reads=[("u_tok", tt - 1), "bands"], writes=["ps%d" % bk])
                    if has_r:
                        A(PE, "matmul", bank(bk, 8, j * 128 + 120), lhsT=u_tok[:, tt + 1, g * 128:(g + 1) * 128], rhs=bands[:, g, 392:400],
                          start=False, stop=True, reads=[("u_tok", tt + 1), "bands"], writes=["ps%d" % bk])

            band(0)
            for idx, (g, blk) in enumerate(items):
                cs = slice(blk * 512, (blk + 1) * 512)
                bk = 2 + idx % 2
                bk2 = 4 + idx % 2
                dT = diffT[idx % 2]
                if idx + 1 < len(items):
                    band(idx + 1)
                copy_on(evac_eng(), dT, bank(bk), ["ps%d" % bk], [("diffT", idx % 2)])
                A(PE, "matmul", bank(bk2), lhsT=poolw[:, g, :], rhs=dT, start=True, stop=True,
                  reads=[("diffT", idx % 2), "poolw"], writes=["ps%d" % bk2])
                A(DVE, "scalar_tensor_tensor", out=ypT[:, g, cs], in0=bank(bk2), scalar=pscale[:, g:g + 1], in1=sgp[:, g, cs],
                  op0=ALU.mult, op1=ALU.mult, reads=["ps%d" % bk2, "vec", ("sgp", g, blk)], writes=[("ypT", g, blk)])

            Sd.barrier(lambda e: e.memset(small[:, 8:9], 0.0))

            ckpt("Biii")
            X.reset()
            Wo = X.alloc([KC, D], BF16)
            vaug_off = X.off
            Vaug = X.alloc([TT, NH, 128], BF16)
            QT = [X.alloc([S], BF16) for _ in range(2)]
            KT = [X.alloc([S], BF16) for _ in range(2)]
            PT = [X.alloc([QW], BF16) for _ in range(3)]
            Osb = X.alloc([QW], F32)
            rec = X.alloc([QW], F32)
            tg_ = X.alloc([QW], F32)
            zsb = X.alloc([512], F32)
            zssb = X.alloc([512], F32)
            zsq = X.alloc([512], BF16)
            lnF = X.alloc([512], F32)
            Fq = X.alloc([512], F32)
            qn = X.alloc([512], F32)
            t1 = X.alloc([512], F32)
            t2 = X.alloc([512], F32)
            ksqs = [X.alloc([512], F32) for _ in range(2)]
            yaT = X.alloc([4, S], BF16)
            att_end = X.off
            X.reset(vaug_off)
            xo = [X.alloc([1024], F32) for _ in range(4)]
            to = [X.alloc([1024], F32) for _ in range(2)]
            X.reset(att_end)

            for kc in range(KC):
                A(SP, "dma_start", out=Wo[:, kc, :], in_=wouts_v[:, kc, :], reads=["wscratch"], writes=[("Wo", kc)], dma=True)

            ckpt("KV")
            def produce_stages(h, blk):
                cs = slice(blk * 512, (blk + 1) * 512)
                q_ = QT[h % 2]
                k_ = KT[h % 2]
                qk = ("QT", h % 2)
                kk = ("KT", h % 2)
                hp = h // 2
                rs = slice(0, 64) if h % 2 == 0 else slice(64, 128)

                def st0():
                    for j in range(2):
                        A(PE, "matmul", bank(6)[0:96, :], lhsT=wuq[:, j, h * 96:(h + 1) * 96], rhs=qln[:, j, cs],
                          start=(j == 0), stop=(j == 1), reads=[("qln", j, blk), "wuq"], writes=["ps6"])
                    for j in range(2):
                        A(PE, "matmul", bank(7)[0:96, :], lhsT=wuq[:, j, 768 + h * 96:768 + (h + 1) * 96], rhs=qln[:, j, cs],
                          start=(j == 0), stop=(j == 1), reads=[("qln", j, blk), "wuq"], writes=["ps7"])
                    A(DVE, "tensor_copy", out=zsb[0:96, :], in_=bank(6)[0:96, :], reads=["ps6"], writes=["zsb"])
                    A(DVE, "tensor_copy", out=zssb[R, :], in_=bank(7)[R, :], reads=["ps7"], writes=["zssb"])
                    A(DVE, "tensor_tensor", out=zsq[0:96, :], in0=zsb[0:96, :], in1=zsb[0:96, :], op=ALU.mult,
                      reads=["zsb"], writes=["zsq"])

                def st1():
                    A(PE, "matmul", bank(6)[0:96, :], lhsT=ones_bf[0:96, 0:96], rhs=zsq[0:96, :], start=True, stop=True,
                      reads=["zsq", "ones_bf"], writes=["ps6"])
                    A(PE, "matmul", bank(7), lhsT=wukvg[:, hp * 128:(hp + 1) * 128], rhs=kvn[:, cs], start=True, stop=True,
                      reads=[("kvn", blk), "wukvg"], writes=["ps7"])
                    A(DVE, "tensor_copy", out=k_[0:64, cs], in_=bank(7)[rs, :], reads=["ps7"], writes=[kk])
                    A(SP, "dma_start", out=k_[R, cs], in_=KR[R, cs], reads=[("KR", blk), kk], writes=[kk], dma=True)

                def st2():
                    A(ACT, "activation", out=lnF[0:96, :], in_=bank(6)[0:96, :], func=AF.Ln, bias=epsc[0:96, :], scale=1.0 / QKH,
                      reads=["ps6", "colc"], writes=["lnF"])
                    A(ACT, "activation", out=Fq[0:96, :], in_=lnF[0:96, :], func=AF.Exp, scale=-0.5, reads=["lnF"], writes=["Fq"])

                def st3():
                    A(DVE, "scalar_tensor_tensor", out=qn[0:96, :], in0=zsb[0:96, :], scalar=gq96[0:96, :], in1=Fq[0:96, :],
                      op0=ALU.mult, op1=ALU.mult, reads=["zsb", "Fq", "vec"], writes=["qn"])
                    A(DVE, "scalar_tensor_tensor", out=t2[R, :], in0=zssb[R, :], scalar=gqs96[R, :], in1=Fq[R, :],
                      op0=ALU.mult, op1=ALU.mult, reads=["zssb", "Fq", "vec"], writes=["t2"])
                    A(POOL, "tensor_copy", out=q_[0:64, cs], in_=qn[0:64, :], reads=["qn"], writes=[qk])
                    A(POOL, "tensor_tensor", out=t1[R, :], in0=qn[R, :], in1=COS[R, cs], op=ALU.mult,
                      reads=["qn", "COS"], writes=["t1"])
                    A(POOL, "tensor_tensor", out=t2[R, :], in0=t2[R, :], in1=SINS[R, cs], op=ALU.mult,
                      reads=["t2", "SINS"], writes=["t2"])
                    A(POOL, "tensor_tensor", out=q_[R, cs], in0=t1[R, :], in1=t2[R, :], op=ALU.add,
                      reads=["t1", "t2", qk], writes=[qk])
                return [st0, st1, st2, st3]

            NG = NQP * TT
            groups = [(h, qp, kt) for h in range(NH) for qp in range(NQP) for kt in range(TT)]
            gap = max(4, (NG - 1) // NBLK)
            assert NBLK * 4 <= NG, (NBLK, NG)

            def emit_qk(i):
                h, qp, kt = groups[i]
                q_ = QT[h % 2]
                k_ = KT[h % 2]
                sb_ = i % 2
                sbk = [0, 2][sb_]
                for j in ([0] * FILL_QK + list(range(NJ))):
                    q0 = qp * QW + j * 512
                    A(PE, "matmul", bank(sbk + j), lhsT=k_[0:96, kt * 128:(kt + 1) * 128], rhs=q_[0:96, q0:q0 + 512],
                      start=True, stop=True, reads=[("QT", h % 2), ("KT", h % 2)], writes=["ps%d" % (sbk + j)])

            p0_list = [st for blk in range(NBLK) for st in produce_stages(0, blk)]
            for tt in range(TT):
                for k_ in range(len(p0_list)):
                    if (k_ * TT) // len(p0_list) == tt:
                        p0_list[k_]()
                ba, bb = (tt % 3) * 2, (tt % 3) * 2 + 1
                ksq = ksqs[tt % 2]
                A(PE, "matmul", bank(ba), lhsT=kvn[:, tt * 128:(tt + 1) * 128], rhs=wukv[:, 0:512], start=True, stop=True,
                  reads=[("kvn", tt // 4), "wukv"], writes=["ps%d" % ba])
                A(PE, "matmul", bank(bb), lhsT=kvn[:, tt * 128:(tt + 1) * 128], rhs=wukv[:, 512:1024], start=True, stop=True,
                  reads=[("kvn", tt // 4), "wukv"], writes=["ps%d" % bb])
                vsrc = bank(ba).rearrange("p (h d) -> p h d", h=NH)
                A(ACT, "activation", out=Vaug[:, tt, 0:NH:2, 0:64], in_=vsrc[:, 0:NH:2, :], func=AF.Copy,
                  reads=["ps%d" % ba], writes=[("V", tt, 0)])
                A(ACT, "activation", out=Vaug[:, tt, 1:NH:2, 64:128], in_=vsrc[:, 1:NH:2, :], func=AF.Copy,
                  reads=["ps%d" % ba], writes=[("V", tt, 1)])
                A(DVE, "memset", Vaug[:, tt, 0:NH:2, 64:128], 1.0, writes=[("V1", tt, 0)])
                A(DVE, "memset", Vaug[:, tt, 1:NH:2, 0:64], 1.0, writes=[("V1", tt, 1)])
                A(ACT, "activation", out=ksq, in_=bank(bb), func=AF.Square, reads=["ps%d" % bb], writes=[("ksq", tt % 2)])
                A(DVE, "tensor_reduce", out=kss[:, tt, :], in_=ksq.rearrange("p (h d) -> p h d", h=NH),
                                                        axis=AX.X, op=ALU.add, reads=[("ksq", tt % 2)], writes=[("kss", tt)])
            A(DVE, "tensor_tensor", out=kss, in0=kss, in1=krstat.unsqueeze(2).to_broadcast([128, TT, NH]), op=ALU.add,
              reads=[("kss", tt) for tt in range(TT)] + ["krstat"], writes=["kss_all"])
            A(ACT, "activation", out=kss, in_=kss, func=AF.Ln, bias=epsc, scale=1.0 / QKH, reads=["kss_all", "colc"], writes=["kss_ln"])
            A(ACT, "activation", out=sck, in_=kss, func=AF.Exp, bias=lnsc, scale=-0.5, reads=["kss_ln", "colc"], writes=["sck"])

            emit_qk(0)
            if len(groups) > 1:
                emit_qk(1)
            pending = []
            for i, (h, qp, kt) in enumerate(groups):
                sb_ = i % 2
                sbk = [0, 2][sb_]
                pb_ = i % 3
                A(ACT, "activation", out=PT[pb_], in_=ps[:, sbk * 512:sbk * 512 + QW], func=AF.Exp, bias=negB,
                  scale=sck[:, kt, h:h + 1], reads=["ps%d" % (sbk + j_) for j_ in range(NJ)] + ["sck", "negB"], writes=[("PT", pb_)])
                if i + 2 < len(groups):
                    emit_qk(i + 2)
                for j in range(NJ):
                    A(PE, "matmul", bank(4 + j), lhsT=Vaug[:, kt, h, :], rhs=PT[pb_][:, j * 512:(j + 1) * 512],
                      start=(kt == 0), stop=(kt == TT - 1), reads=[("PT", pb_), ("V", kt, h % 2), ("V1", kt, h % 2)], writes=["ps%d" % (4 + j)])
                if kt == TT - 1:
                    while pending:
                        pending.pop(0)[1]()
                    for j in range(NJ):
                        A(DVE, "tensor_copy", out=Osb[:, j * 512:(j + 1) * 512], in_=bank(4 + j), reads=["ps%d" % (4 + j)],
                          writes=[("Osb", j)])
                    cq0 = qp * QW
                    nr = slice(0, 64) if h % 2 == 0 else slice(64, 128)
                    dr = slice(64, 128) if h % 2 == 0 else slice(0, 64)
                    NCH = QW // 256

                    def mk_rec(c, nr=nr, dr=dr):
                        def f():
                            cc = slice(c * 256, (c + 1) * 256)
                            A(DVE, "reciprocal", out=rec[nr, cc], in_=Osb[dr, cc], reads=[("Osb", c // 2)], writes=[("rec", c)])
                        return f

                    def mk_tg(c, nr=nr, h=h, cq0=cq0):
                        def f():
                            cc = slice(c * 256, (c + 1) * 256)
                            A(POOL, "tensor_tensor", out=tg_[nr, cc], in0=rec[nr, cc], in1=sga[nr, h // 2, cq0 + c * 256:cq0 + (c + 1) * 256],
                              op=ALU.mult, reads=[("rec", c)] + [("sga", h // 2, bl) for bl in range(NBLK)], writes=[("tg", c)])
                        return f

                    def mk_fin(c, nr=nr, h=h, qp=qp, cq0=cq0):
                        def f():
                            cc = slice(c * 256, (c + 1) * 256)
                            A(DVE, "tensor_tensor", out=yaT[nr, h // 2, cq0 + c * 256:cq0 + (c + 1) * 256], in0=Osb[nr, cc], in1=tg_[nr, cc],
                              op=ALU.mult, reads=[("Osb", c // 2), ("tg", c)], writes=[("yaT", h, qp, c)])
                        return f
                    offs = [3, 4, 5, 10, 11, 12]
                    for c in range(NCH + 1):
                        fs = []
                        if c < NCH:
                            fs += [mk_rec(c), mk_tg(c)]
                        if c >= 1:
                            fs.append(mk_fin(c - 1))
                        for f in fs:
                            pending.append((i + offs[min(c, len(offs) - 1)], f))
                while pending and pending[0][0] <= i:
                    pending.pop(0)[1]()
                gi = qp * TT + kt
                if h + 1 < NH:
                    for blk in range(NBLK):
                        for si in range(4):
                            so = (0, 4, 6, 7)[si] if gap >= 7 else si
                            if min(blk * gap + so, NG - 3) == gi:
                                produce_stages(h + 1, blk)[si]()

            while pending:
                pending.pop(0)[1]()
            ckpt("ATT")
            Sd.barrier(lambda e: e.memset(small[:, 8:9], 0.0))
            for tt in range(TT):
                xb = xo[tt % 4]
                tb = to[tt % 2]
                A(SP, "dma_start", out=xb, in_=x_d[b, tt * 128:(tt + 1) * 128, :],
                  writes=[("xo", tt % 4)], dma=True)
                for n in range(2):
                    bk = (tt % 2) * 2 + n
                    for c in range(KC):
                        src = ypT[:, c, tt * 128:(tt + 1) * 128] if c < 4 else yaT[:, c - 4, tt * 128:(tt + 1) * 128]
                        rk = [("ypT", c, tt // 4)] if c < 4 else [("yaT", 2 * (c - 4) + hh_, tt * 128 // QW, (tt * 128 % QW) // 256) for hh_ in range(2)]
                        A(PE, "matmul", bank(bk), lhsT=src, rhs=Wo[:, c, n * 512:(n + 1) * 512],
                                                                          start=(c == 0), stop=(c == KC - 1),
                          reads=rk + [("Wo", c)], writes=["ps%d" % bk])
                    A(DVE, "tensor_tensor", out=tb[:, n * 512:(n + 1) * 512], in0=bank(bk),
                                                                        in1=gate_bc[b][:, n * 512:(n + 1) * 512], op=ALU.mult,
                      reads=["ps%d" % bk, ("gate_bc", b, 4), ("gate_bc", b, 5)], writes=[("to", tt % 2, n)])
                A(POOL, "tensor_tensor", out=tb, in0=tb, in1=xb, op=ALU.add,
                  reads=[("to", tt % 2, 0), ("to", tt % 2, 1), ("xo", tt % 4)], writes=[("to", tt % 2, 0), ("to", tt % 2, 1)])
                A(POOL, "dma_start", out=out_d[b, tt * 128:(tt + 1) * 128, :], in_=tb,
                  reads=[("to", tt % 2, 0), ("to", tt % 2, 1)], dma=True)

            Sd.barrier(lambda e: e.memset(small[:, 8:9], 0.0))

        run = Sd.emit(sems, dsems)
        with nc.Block() as block:
            @block.sync
            def _(e):
                run(SP, e)

            @block.tensor
            def _(e):
                run(PE, e)

            @block.scalar
            def _(e):
                run(ACT, e)

            @block.vector
            def _(e):
                run(DVE, e)

            @block.gpsimd
            def _(e):
                run(POOL, e)
    return nc


def make_in_maps(inputs, S, NB, n_cores):
    hc = host_consts(S)
    maps = []
    for core in range(n_cores):
        m = host_layout(inputs, S, NB, core)
        m.update(hc)
        maps.append(m)
    return maps


def kernel(**inputs):
    NB = BATCH // N_CORES
    nc = build(SEQ, NB)
    in_maps = make_in_maps(inputs, SEQ, NB, N_CORES)
    res = run_bass_kernel_spmd(nc, in_maps, core_ids=list(range(N_CORES)))
    out = np.concatenate([np.asarray(r["out"]) for r in res.results], axis=0)
    return out.astype(np.float32)
```

```python
import math
from contextlib import ExitStack

import numpy as np
import ml_dtypes

import concourse.bass as bass
import concourse.mybir as mybir
from concourse.bass_utils import run_bass_kernel_spmd

F32 = mybir.dt.float32
BF16 = mybir.dt.bfloat16
I32 = mybir.dt.int32
U8 = mybir.dt.uint8
AF = mybir.ActivationFunctionType
ALU = mybir.AluOpType
AX = mybir.AxisListType

PE, ACT, DVE, POOL, SP = "pe", "act", "dve", "pool", "sp"

D = 1024
KC = 8
NH = 8
QKH = 96
EPS = 1e-6
N_CORES = 8
BATCH = 16
SEQ = 2048
NCOL = 2112
POOL_W = (2, 4, 8, 16)
FILL_QK = 0


class Ins:
    __slots__ = ("eng", "fn", "deps", "needs_inc", "count", "is_dma", "dsem", "dval", "dprev", "pos")

    def __init__(self, eng, fn, is_dma):
        self.eng = eng
        self.fn = fn
        self.deps = []
        self.needs_inc = False
        self.count = 0
        self.is_dma = is_dma
        self.dsem = None
        self.dval = 0
        self.dprev = 0
        self.pos = 0


class Sched:
    def __init__(self):
        self.streams = {PE: [], ACT: [], DVE: [], POOL: [], SP: []}
        self.last_writer = {}
        self.readers = {}
        self.n = 0
        self.cur_barrier = None
        self.dmas_since_barrier = []

    def add(self, eng, _opname, *args, reads=(), writes=(), dma=False, **kw):
        if not getattr(self, "enabled", True):
            return None
        fn = (lambda e: getattr(e, _opname)(*args, **kw))
        pk = [k for k in reads if isinstance(k, str) and k.startswith("ps")]
        if pk:
            writes = list(writes) + [k for k in pk if k not in writes]
        ins = Ins(eng, fn, dma)
        ins.pos = self.n
        self.n += 1
        deps = {}
        for k in reads:
            w = self.last_writer.get(k)
            if w is not None:
                deps[id(w)] = w
        for k in writes:
            w = self.last_writer.get(k)
            if w is not None:
                deps[id(w)] = w
            for r in self.readers.get(k, ()):
                deps[id(r)] = r
        for k in reads:
            self.readers.setdefault(k, []).append(ins)
        for k in writes:
            self.last_writer[k] = ins
            self.readers[k] = []
        if self.cur_barrier is not None:
            deps[id(self.cur_barrier)] = self.cur_barrier
        for d in deps.values():
            if d is ins:
                continue
            if (not d.is_dma) and (not dma) and d.eng == eng and eng == PE:
                continue
            ins.deps.append(d)
        self.streams[eng].append(ins)
        if dma:
            self.dmas_since_barrier.append(ins)
        return ins

    def barrier(self, fn):
        if not getattr(self, "enabled", True):
            return None
        ins = Ins(DVE, fn, False)
        ins.pos = self.n
        self.n += 1
        for e, st in self.streams.items():
            for prev in reversed(st):
                if not prev.is_dma:
                    ins.deps.append(prev)
                    break
        ins.deps.extend(self.dmas_since_barrier)
        if self.cur_barrier is not None:
            ins.deps.append(self.cur_barrier)
        self.dmas_since_barrier = []
        self.streams[DVE].append(ins)
        self.cur_barrier = ins
        self.last_writer = {}
        self.readers = {}
        return ins

    def emit(self, sems, dma_sems):
        for st in self.streams.values():
            for ins in st:
                for d in ins.deps:
                    d.needs_inc = True
        for e, st in self.streams.items():
            c = 0
            for ins in st:
                if ins.is_dma:
                    continue
                if ins.needs_inc:
                    c += 1
                    ins.count = c
        alld = [i for st in self.streams.values() for i in st if i.is_dma]
        alld.sort(key=lambda i: i.pos)
        cum = [0] * len(dma_sems)
        n_sw = 4
        n_hw = len(dma_sems) - n_sw
        jc = {True: 0, False: 0}
        for ins in alld:
            sw = ins.eng == POOL
            if sw:
                s = n_hw + jc[True] % n_sw
            else:
                s = jc[False] % n_hw
            jc[sw] += 1
            ins.dsem = s
            ins.dprev = cum[s]
            cum[s] += 16
            ins.dval = cum[s]

        def run_stream(e, eng):
            waited = {}
            for ins in self.streams[e]:
                reqs = {}
                for d in ins.deps:
                    if d.is_dma:
                        key = ("d", d.dsem)
                        v = d.dval
                    else:
                        key = ("e", d.eng)
                        v = d.count
                    if v > reqs.get(key, 0):
                        reqs[key] = v
                if ins.is_dma and ins.dprev > 0:
                    key = ("d", ins.dsem)
                    if ins.dprev > reqs.get(key, 0):
                        reqs[key] = ins.dprev
                for key, v in reqs.items():
                    if waited.get(key, 0) >= v:
                        continue
                    waited[key] = v
                    sem = dma_sems[key[1]] if key[0] == "d" else sems[key[1]]
                    eng.wait_ge(sem, v)
                bi = ins.fn(eng)
                if ins.is_dma:
                    bi.then_inc(dma_sems[ins.dsem], 16)
                elif ins.needs_inc:
                    bi.then_inc(sems[e], 1)
            last = {}
            for ins in self.streams[e]:
                if ins.is_dma:
                    last[ins.dsem] = max(last.get(ins.dsem, 0), ins.dval)
            for s, v in last.items():
                if waited.get(("d", s), 0) < v:
                    eng.wait_ge(dma_sems[s], v)

        return run_stream


class Region:
    def __init__(self, base_ap, nbytes):
        self.base = base_ap
        self.nbytes = nbytes
        self.off = 0

    def reset(self, off=0):
        self.off = off

    def alloc(self, shape_free, dt):
        esz = {F32: 4, BF16: 2, I32: 4}[dt]
        n = 1
        for s in shape_free:
            n *= s
        nb = (n * esz + 31) // 32 * 32
        assert self.off + nb <= self.nbytes, (self.off, nb, self.nbytes)
        v = self.base[:, self.off:self.off + n * esz].bitcast(dt)
        self.off += nb
        if len(shape_free) == 2:
            v = v.rearrange("p (a b) -> p a b", a=shape_free[0])
        elif len(shape_free) == 3:
            v = v.rearrange("p (a b c) -> p a b c", a=shape_free[0], b=shape_free[1])
        return v


def band_consts(S):
    out = {}
    t = np.arange(S)
    for gi, w in enumerate(POOL_W):
        lo = np.clip(t - w // 2, 0, S)
        hi = np.clip(t + (w - w // 2), 0, S)
        B = np.zeros((S, S), np.float32)
        for tt in range(S):
            B[lo[tt]:hi[tt], tt] = 1.0 / float(hi[tt] - lo[tt])
            B[tt, tt] -= 1.0
        out[gi] = dict(
            first=B[0:128, 0:128], mid=B[128:256, 128:256], last=B[S - 128:S, S - 128:S],
            left=B[0:128, 128:136], right=B[256:384, 248:256])
    return out


def host_consts(S):
    bands = band_consts(S)
    bt = np.zeros((128, 4, 400), np.float32)
    for g in range(4):
        bt[:, g, 0:128] = bands[g]["first"]
        bt[:, g, 128:256] = bands[g]["mid"]
        bt[:, g, 256:384] = bands[g]["last"]
        bt[:, g, 384:392] = bands[g]["left"]
        bt[:, g, 392:400] = bands[g]["right"]
    inv = (10000.0 ** (-np.arange(0, 32, 2, dtype=np.float32) / 32.0)).astype(np.float32)
    col = np.zeros((128, 6), np.float32)
    for i in range(32):
        col[64 + i, 0] = inv[i % 16]
        col[64 + i, 1] = -1.0 if i < 16 else 1.0
    for p in range(128):
        col[p, 4] = inv[(p % 32) % 16]
        col[p, 5] = -1.0 if (p % 32) < 16 else 1.0
    col[:, 2] = EPS
    col[:, 3] = -0.5 * math.log(96.0)
    return dict(
        ident=np.eye(128, dtype=np.float32).astype(ml_dtypes.bfloat16),
        bands=bt.astype(ml_dtypes.bfloat16),
        colc=col,
    )


def host_layout(inp, S, NB, core):
    f = lambda a: np.ascontiguousarray(np.asarray(a), dtype=np.float32)
    b0 = core * NB
    x = f(inp["x"])[b0:b0 + NB]
    c = f(inp["c"])[b0:b0 + NB]
    pos = np.ascontiguousarray(np.asarray(inp["positions"]), dtype=np.int32)[b0:b0 + NB]
    w_in = f(inp["w_in"])[0]
    cols = np.concatenate([
        np.arange(0, 1408),
        np.arange(512, 576), np.arange(1408, 1424), np.arange(1424, 1440),
        np.arange(512, 576), np.arange(1424, 1440), np.arange(1408, 1424),
        np.arange(1440, 1952)])
    assert cols.size == NCOL
    w_uq = f(inp["w_uq"])[0]
    uq_cols = []
    for h in range(NH):
        uq_cols.append(np.arange(h * 96, h * 96 + 96))
    for h in range(NH):
        uq_cols.append(np.concatenate([np.arange(h * 96, h * 96 + 64), np.arange(h * 96 + 80, h * 96 + 96),
                                       np.arange(h * 96 + 64, h * 96 + 80)]))
    uq_cols = np.concatenate(uq_cols)
    w_ukv = f(inp["w_ukv"])[0]
    kv_cols = np.concatenate([np.concatenate([np.arange(h * 128 + 64, h * 128 + 128) for h in range(NH)]),
                              np.concatenate([np.arange(h * 128, h * 128 + 64) for h in range(NH)])])
    gq = f(inp["g_qnorm"])[0]
    gk = f(inp["g_knorm"])[0]
    vec = np.zeros((128, 64), np.float32)
    vec[:, 0:8] = f(inp["norm_g"])[0].reshape(8, 128).T
    vec[:, 8:32] = f(inp["ada_b"])[0].reshape(24, 128).T
    vec[:, 32:36] = f(inp["pool_scale"])[0].reshape(4, 128).T
    vec[:, 36:38] = f(inp["g_q_lat"])[0].reshape(2, 128).T
    vec[:, 38] = f(inp["g_kv_lat"])[0]
    vec[0:96, 39] = gq
    vec[0:64, 40] = gq[0:64]
    vec[64:80, 40] = gq[80:96]
    vec[80:96, 40] = gq[64:80]
    vec[64:96, 41] = gk[64:96]
    vec[64:80, 42] = gk[80:96]
    vec[80:96, 42] = gk[64:80]
    rows = np.zeros((1, 192), np.float32)
    rows[0, 0:96] = gq
    rows[0, 96:192] = gk
    return dict(
        x=x, cT=np.ascontiguousarray(c.T.reshape(KC, 128, NB).transpose(1, 0, 2)), pos=pos,
        ada_w=f(inp["ada_w"])[0], w_in=np.ascontiguousarray(w_in[:, cols]),
        w_uq=np.ascontiguousarray(w_uq[:, uq_cols]), w_ukv=np.ascontiguousarray(w_ukv[:, kv_cols]),
        w_out=f(inp["w_out"])[0], pool_w=np.ascontiguousarray(f(inp["pool_w"])[0].transpose(1, 0, 2)),
        vec=vec, rows=rows, gkn=np.ascontiguousarray(np.broadcast_to(gk[None, 0:64], (128, 64))),
        adab_gate=f(inp["ada_b"])[0][None, 2048:3072].copy(),
    )


def build(S=SEQ, NB=2):
    TT = S // 128
    NBLK = S // 512
    QW = min(1024, S)
    NQP = S // QW
    NJ = QW // 512
    nc = bass.Bass("TRN2", target_bir_lowering=False)

    def din(name, shape, dt=F32):
        return nc.dram_tensor(name, list(shape), dt, kind="ExternalInput").ap()

    x_d = din("x", [NB, S, D])
    cT_d = din("cT", [128, KC, NB])
    pos_d = din("pos", [NB, S], I32)
    adaw_d = din("ada_w", [D, 3 * D])
    win_d = din("w_in", [D, NCOL])
    wuq_d = din("w_uq", [256, 1536])
    wukv_d = din("w_ukv", [128, 1024])
    wout_d = din("w_out", [D, D])
    poolw_d = din("pool_w", [128, 4, 128])
    vec_d = din("vec", [128, 64])
    rows_d = din("rows", [1, 192])
    gkn_d = din("gkn", [128, 64])
    adabg_d = din("adab_gate", [1, 1024])
    ident_d = din("ident", [128, 128], BF16)
    bands_d = din("bands", [128, 4, 400], BF16)
    colc_d = din("colc", [128, 6])
    out_d = nc.dram_tensor("out", [NB, S, D], F32, kind="ExternalOutput").ap()
    wins_d = nc.dram_tensor("win_s", [128, KC * NCOL], BF16).ap()
    wouts_d = nc.dram_tensor("wout_s", [128, KC * D], BF16).ap()
    tabs_d = nc.dram_tensor("tab_s", [NB, 2, 32, S], BF16).ap()

    Sd = Sched()
    A = Sd.add
    import os as _os
    _stop = _os.environ.get("KSTOP", "")

    def ckpt(name):
        if name == _stop:
            Sd.enabled = False
    es = ExitStack()
    with es:
        XB = 122 * 1024
        PB = 84 * 1024
        Xt = es.enter_context(nc.sbuf_tensor("Xreg", [128, XB], U8))
        Pt = es.enter_context(nc.sbuf_tensor("Preg", [128, PB], U8))
        ps = es.enter_context(nc.psum_tensor("ps", [128, 4096], F32))
        sems = {e: es.enter_context(nc.semaphore("s_" + e)) for e in (PE, ACT, DVE, POOL, SP)}
        dsems = [es.enter_context(nc.semaphore("dq%d" % i)) for i in range(24)]
        X = Region(Xt[:], XB)
        P = Region(Pt[:], PB)

        def bank(i, n=512, off=0):
            return ps[:, i * 512 + off:i * 512 + off + n]

        wuq = P.alloc([2, 1536], BF16)
        wukv = P.alloc([1024], BF16)
        wukvg = P.alloc([512], BF16)
        poolw = P.alloc([4, 128], BF16)
        ident = P.alloc([128], BF16)
        bands = P.alloc([4, 400], BF16)
        ones_bf = P.alloc([128], BF16)
        ones_f = P.alloc([128], F32)
        vec = P.alloc([64], F32)
        colc = P.alloc([6], F32)
        rows = P.alloc([192], F32)
        negB = P.alloc([1], F32)
        cact = P.alloc([KC, NB], F32)
        mod = P.alloc([24, NB], F32)
        gs = P.alloc([KC, NB], F32)
        COS = P.alloc([S], BF16)
        SINS = P.alloc([S], BF16)
        ypT = P.alloc([4, S], BF16)
        sga = P.alloc([4, S], BF16)
        qln = P.alloc([2, S], BF16)
        kvn = P.alloc([S], BF16)
        KR = P.alloc([S], BF16)
        krstat = P.alloc([TT], F32)
        kss = P.alloc([TT, NH], F32)
        sck = P.alloc([TT, NH], F32)
        gate_bc = [P.alloc([1024], F32) for _ in range(NB)]
        rstat = P.alloc([TT, 2], F32)
        small = P.alloc([16], F32)
        P_end = P.off

        norm_g = vec[:, 0:8]
        adab = vec[:, 8:32]
        pscale = vec[:, 32:36]
        gql = vec[:, 36:38]
        gkvl = vec[:, 38:39]
        gq96 = vec[:, 39:40]
        gqs96 = vec[:, 40:41]
        gkr = vec[:, 41:42]
        gkrs = vec[:, 42:43]
        invf = colc[:, 0:1]
        sgn = colc[:, 1:2]
        epsc = colc[:, 2:3]
        lnsc = colc[:, 3:4]
        invf4 = colc[:, 4:5]
        sgn4 = colc[:, 5:6]

        rr = [0]

        def evac_eng():
            rr[0] += 1
            return ACT if rr[0] % 2 else DVE

        def copy_on(eng_name, out, in_, reads, writes):
            if eng_name == ACT:
                A(ACT, "activation", out=out, in_=in_, func=AF.Copy, reads=reads, writes=writes)
            else:
                A(eng_name, "tensor_copy", out=out, in_=in_, reads=reads, writes=writes)

        X.reset()
        A(SP, "dma_start", out=vec, in_=vec_d, writes=["vec"], dma=True)
        A(SP, "dma_start", out=colc, in_=colc_d, writes=["colc"], dma=True)
        A(SP, "dma_start", out=rows[0:1, :], in_=rows_d, writes=["rows"], dma=True)
        A(SP, "dma_start", out=ident, in_=ident_d, writes=["ident"], dma=True)
        A(SP, "dma_start", out=bands, in_=bands_d, writes=["bands"], dma=True)
        A(SP, "dma_start", out=cact, in_=cT_d, writes=["cact"], dma=True)
        NQ4 = S // 512
        posi_all = X.alloc([NB, 512], I32)
        for b in range(NB):
            for q4 in range(NQ4):
                A(SP, "dma_start", out=posi_all[32 * q4:32 * q4 + 32, b, :],
                  in_=pos_d[b:b + 1, q4 * 512:(q4 + 1) * 512].partition_broadcast(32), writes=[("posi", b)], dma=True)
        A(POOL, "memset", ones_bf, 1.0, writes=["ones_bf"])
        A(POOL, "memset", ones_f, 1.0, writes=["ones_f"])
        A(ACT, "activation", out=cact, in_=cact, func=AF.Silu, reads=["cact"], writes=["cact"])

        A(DVE, "tensor_reduce", out=small[0:1, 0:2], in_=rows[0:1, :].rearrange("p (a b) -> p a b", a=2),
                                         axis=AX.X, op=ALU.max, apply_absolute_value=True,
          reads=["rows"], writes=["small"])
        A(DVE, "scalar_tensor_tensor", out=small[0:1, 2:3], in0=small[0:1, 0:1], scalar=-math.sqrt(96.0),
                                                in1=small[0:1, 1:2], op0=ALU.mult, op1=ALU.mult,
          reads=["small"], writes=["small2"])
        A(PE, "matmul", bank(7, 1), lhsT=ones_f[0:1, :], rhs=small[0:1, 2:3], start=True, stop=True,
          reads=["small2", "ones_f"], writes=["ps7"])
        A(DVE, "tensor_copy", out=negB, in_=bank(7, 1), reads=["ps7"], writes=["negB"])

        ckpt("s1")
        stg_uq = X.alloc([2, 1536], F32)
        stg_kv = X.alloc([1024], F32)
        stg_pw = X.alloc([4, 128], F32)
        gkn = X.alloc([64], F32)
        A(SP, "dma_start", out=stg_uq, in_=wuq_d.rearrange("(k p) c -> p k c", p=128), writes=["stg_uq"], dma=True)
        A(SP, "dma_start", out=stg_kv, in_=wukv_d, writes=["stg_kv"], dma=True)
        A(SP, "dma_start", out=stg_pw, in_=poolw_d, writes=["stg_pw"], dma=True)
        A(SP, "dma_start", out=gkn, in_=gkn_d, writes=["gkn"], dma=True)
        A(DVE, "tensor_copy", out=wuq, in_=stg_uq, reads=["stg_uq"], writes=["wuq"])
        A(DVE, "tensor_copy", out=wukv, in_=stg_kv, reads=["stg_kv"], writes=["wukv"])
        A(DVE, "tensor_copy", out=poolw, in_=stg_pw, reads=["stg_pw"], writes=["poolw"])
        A(DVE, "tensor_tensor", out=wukvg.rearrange("p (h d) -> p h d", h=NH),
                                         in0=stg_kv[:, 512:1024].rearrange("p (h d) -> p h d", h=NH),
                                         in1=gkn.unsqueeze(1).to_broadcast([128, NH, 64]), op=ALU.mult,
          reads=["stg_kv", "gkn"], writes=["wukvg"])

        ckpt("s2")
        CW = 264
        NSTG = 2
        stg_w = [X.alloc([KC, CW], F32) for _ in range(NSTG)]
        stg_wb = [X.alloc([KC, CW], BF16) for _ in range(NSTG)]
        wins_v = wins_d.rearrange("p (k c) -> p k c", k=KC)
        wouts_v = wouts_d.rearrange("p (k c) -> p k c", k=KC)
        jobs = [(win_d, wins_v, ci * CW, CW) for ci in range(NCOL // CW)]
        jobs += [(wout_d, wouts_v, ci * 256, 256) for ci in range(4)]
        for j, (src, dst, c0, cw) in enumerate(jobs):
            sf = stg_w[j % NSTG]
            sbf = stg_wb[j % NSTG]
            A(SP, "dma_start", out=sf[:, :, 0:cw], in_=src[:, c0:c0 + cw].rearrange("(k p) c -> p k c", p=128),
              writes=[("stg_w", j % NSTG)], dma=True)
            copy_on(POOL, sbf[:, :, 0:cw], sf[:, :, 0:cw], [("stg_w", j % NSTG)], [("stg_wb", j % NSTG)])
            A(POOL, "dma_start", out=dst[:, :, c0:c0 + cw], in_=sbf[:, :, 0:cw],
              reads=[("stg_wb", j % NSTG)], dma=True)

        late_dve = []
        stg_ada = [X.alloc([KC, 512], F32) for _ in range(2)]
        cbc = [X.alloc([KC, 128], F32) for _ in range(NB)]
        adabg = X.alloc([1024], F32)
        A(SP, "dma_start", out=adabg, in_=adabg_d.partition_broadcast(128), writes=["adabg"], dma=True)
        for b in range(NB):
            A(DVE, "tensor_copy", out=cbc[b], in_=cact[:, :, b:b + 1].to_broadcast([128, KC, 128]),
              reads=["cact"], writes=[("cbc", b)])
        for ec in range(6):
            st = stg_ada[ec % 2]
            A(ACT, "dma_start", out=st, in_=adaw_d[:, ec * 512:(ec + 1) * 512].rearrange("(k p) c -> p k c", p=128),
              writes=[("stg_ada", ec % 2)], dma=True)
            for mt in range(4):
                m = ec * 4 + mt
                for kc in range(KC):
                    A(PE, "matmul",
                        bank(0, NB, m * NB), lhsT=st[:, kc, mt * 128:(mt + 1) * 128], rhs=cact[:, kc, :],
                        start=(kc == 0), stop=(kc == KC - 1),
                      reads=[("stg_ada", ec % 2), "cact"], writes=["ps0"])
            if ec >= 4:
                for b in range(NB):
                    bk = 1 + b * 2 + (ec - 4)
                    for kc in range(KC):
                        A(PE, "matmul",
                            bank(bk), lhsT=cbc[b][:, kc, :], rhs=st[:, kc, :], start=(kc == 0), stop=(kc == KC - 1),
                          reads=[("stg_ada", ec % 2), ("cbc", b)], writes=["ps%d" % bk])
                    late_dve.append((lambda b=b, bk=bk, ec=ec: A(DVE, "tensor_tensor",
                        out=gate_bc[b][:, (ec - 4) * 512:(ec - 3) * 512], in0=bank(bk),
                        in1=adabg[:, (ec - 4) * 512:(ec - 3) * 512], op=ALU.add,
                      reads=["ps%d" % bk, "adabg"], writes=[("gate_bc", b, ec)])))
        late_dve.append(lambda: A(DVE, "tensor_tensor", out=mod, in0=bank(0, 24 * NB).rearrange("p (m b) -> p m b", b=NB),
                                         in1=adab.unsqueeze(2).to_broadcast([128, 24, NB]), op=ALU.add,
          reads=["ps0", "vec"], writes=["mod"]))
        late_dve.append(lambda: A(DVE, "scalar_tensor_tensor", out=gs, in0=mod[:, 8:16, :], scalar=1.0,
                                                in1=norm_g.unsqueeze(2).to_broadcast([128, KC, NB]),
                                                op0=ALU.add, op1=ALU.mult,
          reads=["mod", "vec"], writes=["gs"]))

        R = slice(64, 96)
        PQ = slice(0, 32 * NQ4)
        tmpc = [(X.alloc([512], F32), X.alloc([512], F32), X.alloc([512], I32), X.alloc([512], BF16)) for _ in range(2)]
        for b in reversed(range(NB)):
            posi = posi_all[PQ, b, :]
            chains = []
            for ci, (which, shift) in enumerate((("cos", math.pi / 2), ("sin", 0.0))):
                ang, angk, angi, tout = tmpc[ci]
                ang, angk, angi, tout = ang[PQ, :], angk[PQ, :], angi[PQ, :], tout[PQ, :]
                ka, kk_, ki, ko = ("ang", ci), ("angk", ci), ("angi", ci), ("tout", ci)
                ops = [
                    (DVE, "tensor_copy", dict(out=ang, in_=posi), [("posi", b)], [ka]),
                    (DVE, "tensor_scalar", dict(out=ang, in0=ang, scalar1=invf4[PQ, :], scalar2=shift,
                                                op0=ALU.mult, op1=ALU.add), [ka, "colc"], [ka]),
                    (DVE, "tensor_scalar", dict(out=angi, in0=ang, scalar1=1.0 / (2 * math.pi), scalar2=None,
                                                op0=ALU.mult), [ka], [ki]),
                    (DVE, "tensor_copy", dict(out=angk, in_=angi), [ki], [kk_]),
                    (DVE, "scalar_tensor_tensor", dict(out=ang, in0=angk, scalar=-2 * math.pi, in1=ang,
                                                       op0=ALU.mult, op1=ALU.add), [kk_, ka], [ka]),
                    (DVE, "tensor_scalar", dict(out=angk, in0=ang, scalar1=math.pi, scalar2=None, op0=ALU.is_gt), [ka], [kk_]),
                    (DVE, "scalar_tensor_tensor", dict(out=ang, in0=angk, scalar=-2 * math.pi, in1=ang,
                                                       op0=ALU.mult, op1=ALU.add), [kk_, ka], [ka]),
                    (DVE, "tensor_scalar", dict(out=angk, in0=ang, scalar1=-math.pi, scalar2=None, op0=ALU.is_lt), [ka], [kk_]),
                    (DVE, "scalar_tensor_tensor", dict(out=ang, in0=angk, scalar=2 * math.pi, in1=ang,
                                                       op0=ALU.mult, op1=ALU.add), [kk_, ka], [ka]),
                ]
                if which == "cos":
                    ops.append((ACT, "activation", dict(out=tout, in_=ang, func=AF.Sin), [ka], [ko]))
                else:
                    ops.append((ACT, "activation", dict(out=ang, in_=ang, func=AF.Sin), [ka], [ka]))
                    ops.append((DVE, "tensor_scalar", dict(out=tout, in0=ang, scalar1=sgn4[PQ, :], scalar2=None,
                                                           op0=ALU.mult), [ka, "colc"], [ko]))
                chains.append(ops)
            for k_ in range(max(len(c) for c in chains)):
                for c in chains:
                    if k_ < len(c):
                        eng_, op_, kw_, rd_, wr_ = c[k_]
                        A(eng_, op_, reads=rd_, writes=wr_, **kw_)
            for ci, dst in enumerate((COS, SINS)):
                tout = tmpc[ci][3]
                for q4 in range(NQ4):
                    if b > 0:
                        A(SP, "dma_start", out=tabs_d[b, ci, :, q4 * 512:(q4 + 1) * 512], in_=tout[32 * q4:32 * q4 + 32, :],
                          reads=[("tout", ci)], dma=True)
                    else:
                        A(SP, "dma_start", out=dst[R, q4 * 512:(q4 + 1) * 512], in_=tout[32 * q4:32 * q4 + 32, :],
                          reads=[("tout", ci)], writes=[("tab", ci, q4)], dma=True)

        for f_ in late_dve:
            f_()
        Sd.barrier(lambda e: e.memset(small[:, 8:9], 0.0))

        ckpt("setup")
        for b in range(NB):
            X.reset()
            Wb = X.alloc([KC, NCOL], BF16)
            xnT = X.alloc([KC, S], BF16)
            mark = X.off
            xt = [X.alloc([1024], F32) for _ in range(8)]
            junk = X.alloc([1024], BF16)
            xr = [X.alloc([1024], BF16) for _ in range(2)]
            X.reset(mark)
            u_tok = X.alloc([TT, 512], BF16)
            sgp = X.alloc([4, S], BF16)
            diffT = [X.alloc([512], BF16) for _ in range(2)]
            krg = X.alloc([512], F32)
            krsg = X.alloc([512], F32)
            krsq = X.alloc([512], BF16)
            kt1 = X.alloc([512], F32)
            qlg = X.alloc([2, 512], BF16)
            qlsq = X.alloc([2, 512], BF16)
            kvg = X.alloc([512], BF16)
            kvsq = X.alloc([512], BF16)
            lnr = X.alloc([512], F32)
            Rq = X.alloc([512], BF16)

            A(POOL, "memset", rstat, 0.0, writes=["rstat0"])

            R = slice(64, 96)
            if b > 0:
                A(SP, "dma_start", out=COS[R, :], in_=tabs_d[b, 0], writes=["COS"], dma=True)
                A(SP, "dma_start", out=SINS[R, :], in_=tabs_d[b, 1], writes=["SINS"], dma=True)
            ckpt("tables")
            _ska = set(_os.environ.get("KSKIPA", "").split(","))
            for tg in range(TT // 4):
                for j in range(4):
                    tt = tg * 4 + j
                    xb = xt[tt % 8]
                    A(SP, "dma_start", out=xb, in_=x_d[b, tt * 128:(tt + 1) * 128, :],
                      writes=[("xt", tt % 8)], dma=True)
                    if tt % 2 == 1 and tt // 2 < KC:
                        A(SP, "dma_start", out=Wb[:, tt // 2, :], in_=wins_v[:, tt // 2, :], writes=[("Wb", tt // 2)], dma=True)
                    A(ACT, "activation", out=junk, in_=xb, func=AF.Square, accum_out=rstat[:, tt, 0:1],
                      reads=[("xt", tt % 8), "rstat0"], writes=["junk", ("rs", tt)])
                g4 = slice(tg * 4, tg * 4 + 4)
                A(ACT, "activation", out=rstat[:, g4, 1:2], in_=rstat[:, g4, 0:1], func=AF.Ln, bias=epsc, scale=1.0 / D,
                  reads=[("rs", tg * 4 + j_) for j_ in range(4)] + ["colc"], writes=[("rl", tg)])
                A(ACT, "activation", out=rstat[:, g4, 0:1], in_=rstat[:, g4, 1:2], func=AF.Exp, scale=-0.5,
                  reads=[("rl", tg)], writes=[("rr", tg)])
                for j in range(4):
                    tt = tg * 4 + j
                    xb = xt[tt % 8]
                    if "xr" not in _ska: A(DVE, "tensor_scalar", out=xr[tt % 2], in0=xb, scalar1=rstat[:, tt, 0:1],
                                                                   scalar2=None, op0=ALU.mult,
                      reads=[("xt", tt % 8), ("rr", tg)], writes=[("xr", tt % 2)])
                    pst = bank(j).bitcast(BF16)
                    for kc in range(KC):
                        if "tr" not in _ska: A(PE, "transpose", out=pst[:, kc * 128:(kc + 1) * 128],
                                                                           in_=xr[tt % 2][:, kc * 128:(kc + 1) * 128],
                                                                           identity=ident,
                          reads=[("xr", tt % 2), "ident"], writes=["ps%d" % j])
                if tg == TT // 4 - 1:
                    for kc in range(TT // 2, KC):
                        A(SP, "dma_start", out=Wb[:, kc, :], in_=wins_v[:, kc, :], writes=[("Wb", kc)], dma=True)
                src_all = ps[:, 0:2048].bitcast(BF16).rearrange("p (j k t) -> p j k t", j=4, k=KC)
                for kc in range(KC):
                    dst = xnT[:, kc, tg * 512:(tg + 1) * 512].rearrange("p (j t) -> p j t", j=4)
                    A(ACT, "activation", out=dst[:, 0:2, :], in_=src_all[:, 0:2, kc, :], func=AF.Identity,
                      bias=mod[:, kc, b:b + 1], scale=gs[:, kc, b:b + 1],
                      reads=["ps0", "ps1", "gs", "mod"], writes=[("xnT", kc, tg, 0)])
                    A(DVE, "tensor_scalar", out=dst[:, 2:4, :], in0=src_all[:, 2:4, kc, :], scalar1=gs[:, kc, b:b + 1],
                      scalar2=mod[:, kc, b:b + 1], op0=ALU.mult, op1=ALU.add,
                      reads=["ps2", "ps3", "gs", "mod"], writes=[("xnT", kc, tg, 1)])

            Sd.barrier(lambda e: e.memset(small[:, 8:9], 0.0))

            ckpt("A")
            xn_keys = lambda tg: [("xnT", kc, tg, hh) for kc in range(KC) for hh in range(2)]
            for tt in range(TT):
                bk = tt % 2
                for kc in range(KC):
                    A(PE, "matmul", bank(bk), lhsT=xnT[:, kc, tt * 128:(tt + 1) * 128],
                                                                  rhs=Wb[:, kc, 0:512], start=(kc == 0), stop=(kc == KC - 1),
                      reads=xn_keys(tt // 4) + [("Wb", kc)], writes=["ps%d" % bk])
                copy_on(evac_eng(), u_tok[:, tt, :], bank(bk), ["ps%d" % bk], [("u_tok", tt)])

            ckpt("Bi")
            def fm_tile(c0, m, blk, bk):
                for kc in range(KC):
                    A(PE, "matmul", bank(bk)[0:m, :], lhsT=Wb[:, kc, c0:c0 + m],
                                                    rhs=xnT[:, kc, blk * 512:(blk + 1) * 512],
                                                    start=(kc == 0), stop=(kc == KC - 1),
                      reads=xn_keys(blk) + [("Wb", kc)], writes=["ps%d" % bk])

            for blk in range(NBLK):
                cs = slice(blk * 512, (blk + 1) * 512)
                nb_ = [2]

                def nxt():
                    nb_[0] = 2 + (nb_[0] - 1) % 6
                    return nb_[0]
                for g in range(4):
                    bk = nxt()
                    fm_tile(512 + g * 128, 128, blk, bk)
                    A(ACT, "activation", out=sgp[:, g, cs], in_=bank(bk), func=AF.Silu,
                      reads=["ps%d" % bk], writes=[("sgp", g, blk)])
                for g in range(4):
                    bk = nxt()
                    fm_tile(1600 + g * 128, 128, blk, bk)
                    A(ACT, "activation", out=sga[:, g, cs], in_=bank(bk), func=AF.Silu,
                      reads=["ps%d" % bk], writes=[("sga", g, blk)])
                for j in range(2):
                    bk = nxt()
                    fm_tile(1024 + j * 128, 128, blk, bk)
                    A(DVE, "tensor_scalar", out=qlg[:, j, :], in0=bank(bk), scalar1=gql[:, j:j + 1],
                                                                 scalar2=None, op0=ALU.mult,
                      reads=["ps%d" % bk, "vec"], writes=[("qlg", j)])
                    A(ACT, "activation", out=qlsq[:, j, :], in_=bank(bk), func=AF.Square,
                      reads=["ps%d" % bk], writes=[("qlsq", j)])
                bk = nxt()
                for j in range(2):
                    A(PE, "matmul", bank(bk), lhsT=ones_bf, rhs=qlsq[:, j, :], start=(j == 0), stop=(j == 1),
                      reads=[("qlsq", j), "ones_bf"], writes=["ps%d" % bk])
                A(ACT, "activation", out=lnr, in_=bank(bk), func=AF.Ln, bias=epsc, scale=1.0 / 256,
                  reads=["ps%d" % bk, "colc"], writes=["lnr"])
                A(ACT, "activation", out=Rq, in_=lnr, func=AF.Exp, scale=-0.5, reads=["lnr"], writes=["Rq"])
                for j in range(2):
                    A(DVE, "tensor_tensor", out=qln[:, j, cs], in0=qlg[:, j, :], in1=Rq, op=ALU.mult,
                      reads=[("qlg", j), "Rq"], writes=[("qln", j, blk)])
                bk = nxt()
                fm_tile(1280, 128, blk, bk)
                A(DVE, "tensor_scalar", out=kvg, in0=bank(bk), scalar1=gkvl, scalar2=None, op0=ALU.mult,
                  reads=["ps%d" % bk, "vec"], writes=["kvg"])
                A(ACT, "activation", out=kvsq, in_=bank(bk), func=AF.Square, reads=["ps%d" % bk], writes=["kvsq"])
                bk = nxt()
                A(PE, "matmul", bank(bk), lhsT=ones_bf, rhs=kvsq, start=True, stop=True,
                  reads=["kvsq", "ones_bf"], writes=["ps%d" % bk])
                A(ACT, "activation", out=lnr, in_=bank(bk), func=AF.Ln, bias=epsc, scale=1.0 / 128,
                  reads=["ps%d" % bk, "colc"], writes=["lnr"])
                A(ACT, "activation", out=Rq, in_=lnr, func=AF.Exp, scale=-0.5, reads=["lnr"], writes=["Rq"])
                A(DVE, "tensor_tensor", out=kvn[:, cs], in0=kvg, in1=Rq, op=ALU.mult,
                  reads=["kvg", "Rq"], writes=[("kvn", blk)])
                bk = nxt()
                fm_tile(1408, 96, blk, bk)
                A(ACT, "activation", out=krg[R, :], in_=bank(bk)[R, :], func=AF.Copy, scale=gkr[R, :],
                  reads=["ps%d" % bk, "vec"], writes=["krg"])
                A(ACT, "activation", out=krsq[R, :], in_=bank(bk)[R, :], func=AF.Square,
                  reads=["ps%d" % bk], writes=["krsq"])
                bk = nxt()
                fm_tile(1504, 96, blk, bk)
                A(ACT, "activation", out=krsg[R, :], in_=bank(bk)[R, :], func=AF.Copy, scale=gkrs[R, :],
                  reads=["ps%d" % bk, "vec"], writes=["krsg"])
                A(POOL, "tensor_tensor", out=kt1[R, :], in0=krg[R, :], in1=COS[R, cs], op=ALU.mult,
                  reads=["krg", "COS"], writes=["kt1"])
                A(POOL, "tensor_tensor", out=krsg[R, :], in0=krsg[R, :], in1=SINS[R, cs], op=ALU.mult,
                  reads=["krsg", "SINS"], writes=["krsg"])
                A(POOL, "tensor_tensor", out=KR[R, cs], in0=kt1[R, :], in1=krsg[R, :], op=ALU.add,
                  reads=["kt1", "krsg"], writes=[("KR", blk)])
                for j in range(4):
                    tt = blk * 4 + j
                    A(PE, "matmul", bank(1, 1, tt), lhsT=krsq[R, j * 128:(j + 1) * 128], rhs=ones_bf[R, 0:1],
                                                    start=True, stop=True,
                      reads=["krsq", "ones_bf"], writes=["ps1"])
            A(DVE, "tensor_copy", out=krstat, in_=bank(1, TT), reads=["ps1"], writes=["krstat"])

            ckpt("Bii")
            items = [(g, blk) for g in range(4) for blk in range(NBLK)]

            def band(idx):
                g, blk = items[idx]
                bk = 2 + idx % 2
                for j in range(4):
                    tt = blk * 4 + j
                    coff = 0 if tt == 0 else (256 if tt == TT - 1 else 128)
                    has_l = tt > 0
                    has_r = tt < TT - 1
                    A(PE, "matmul", bank(bk, 128, j * 128), lhsT=u_tok[:, tt, g * 128:(g + 1) * 128], rhs=bands[:, g, coff:coff + 128],
                      start=True, stop=not (has_l or has_r), reads=[("u_tok", tt), "bands"], writes=["ps%d" % bk])
                    if has_l:
                        A(PE, "matmul", bank(bk, 8, j * 128), lhsT=u_tok[:, tt - 1, g * 128:(g + 1) * 128], rhs=bands[:, g, 384:392],
                          start=False, stop=not has_r, reads=[("u_tok", tt - 1), "bands"], writes=["ps%d" % bk])
                    if has_r:
                        A(PE, "matmul", bank(bk, 8, j * 128 + 120), lhsT=u_tok[:, tt + 1, g * 128:(g + 1) * 128], rhs=bands[:, g, 392:400],
                          start=False, stop=True, reads=[("u_tok", tt + 1), "bands"], writes=["ps%d" % bk])

            band(0)
            for idx, (g, blk) in enumerate(items):
                cs = slice(blk * 512, (blk + 1) * 512)
                bk = 2 + idx % 2
                bk2 = 4 + idx % 2
                dT = diffT[idx % 2]
                if idx + 1 < len(items):
                    band(idx + 1)
                copy_on(evac_eng(), dT, bank(bk), ["ps%d" % bk], [("diffT", idx % 2)])
                A(PE, "matmul", bank(bk2), lhsT=poolw[:, g, :], rhs=dT, start=True, stop=True,
                  reads=[("diffT", idx % 2), "poolw"], writes=["ps%d" % bk2])
                A(DVE, "scalar_tensor_tensor", out=ypT[:, g, cs], in0=bank(bk2), scalar=pscale[:, g:g + 1], in1=sgp[:, g, cs],
                  op0=ALU.mult, op1=ALU.mult, reads=["ps%d" % bk2, "vec", ("sgp", g, blk)], writes=[("ypT", g, blk)])

            Sd.barrier(lambda e: e.memset(small[:, 8:9], 0.0))

            ckpt("Biii")
            X.reset()
            Wo = X.alloc([KC, D], BF16)
            vaug_off = X.off
            Vaug = X.alloc([TT, NH, 128], BF16)
            QT = [X.alloc([S], BF16) for _ in range(2)]
            KT = [X.alloc([S], BF16) for _ in range(2)]
            PT = [X.alloc([QW], BF16) for _ in range(3)]
            Osb = X.alloc([QW], F32)
            rec = X.alloc([QW], F32)
            tg_ = X.alloc([QW], F32)
            zsb = X.alloc([512], F32)
            zssb = X.alloc([512], F32)
            zsq = X.alloc([512], BF16)
            lnF = X.alloc([512], F32)
            Fq = X.alloc([512], F32)
            qn = X.alloc([512], F32)
            t1 = X.alloc([512], F32)
            t2 = X.alloc([512], F32)
            ksqs = [X.alloc([512], F32) for _ in range(2)]
            yaT = X.alloc([4, S], BF16)
            att_end = X.off
            X.reset(vaug_off)
            xo = [X.alloc([1024], F32) for _ in range(4)]
            to = [X.alloc([1024], F32) for _ in range(2)]
            X.reset(att_end)

            for kc in range(KC):
                A(SP, "dma_start", out=Wo[:, kc, :], in_=wouts_v[:, kc, :], reads=["wscratch"], writes=[("Wo", kc)], dma=True)

            ckpt("KV")
            def produce_stages(h, blk):
                cs = slice(blk * 512, (blk + 1) * 512)
                q_ = QT[h % 2]
                k_ = KT[h % 2]
                qk = ("QT", h % 2, blk)
                kk = ("KT", h % 2, blk)
                hp = h // 2
                rs = slice(0, 64) if h % 2 == 0 else slice(64, 128)

                def st0():
                    for j in range(2):
                        A(PE, "matmul", bank(6)[0:96, :], lhsT=wuq[:, j, h * 96:(h + 1) * 96], rhs=qln[:, j, cs],
                          start=(j == 0), stop=(j == 1), reads=[("qln", j, blk), "wuq"], writes=["ps6"])
                    for j in range(2):
                        A(PE, "matmul", bank(7)[0:96, :], lhsT=wuq[:, j, 768 + h * 96:768 + (h + 1) * 96], rhs=qln[:, j, cs],
                          start=(j == 0), stop=(j == 1), reads=[("qln", j, blk), "wuq"], writes=["ps7"])
                    A(DVE, "tensor_copy", out=zsb[0:96, :], in_=bank(6)[0:96, :], reads=["ps6"], writes=["zsb"])
                    A(DVE, "tensor_copy", out=zssb[R, :], in_=bank(7)[R, :], reads=["ps7"], writes=["zssb"])
                    A(DVE, "tensor_tensor", out=zsq[0:96, :], in0=zsb[0:96, :], in1=zsb[0:96, :], op=ALU.mult,
                      reads=["zsb"], writes=["zsq"])

                def st1():
                    A(PE, "matmul", bank(6)[0:96, :], lhsT=ones_bf[0:96, 0:96], rhs=zsq[0:96, :], start=True, stop=True,
                      reads=["zsq", "ones_bf"], writes=["ps6"])
                    A(PE, "matmul", bank(7), lhsT=wukvg[:, hp * 128:(hp + 1) * 128], rhs=kvn[:, cs], start=True, stop=True,
                      reads=[("kvn", blk), "wukvg"], writes=["ps7"])
                    A(DVE, "tensor_copy", out=k_[0:64, cs], in_=bank(7)[rs, :], reads=["ps7"], writes=[kk])
                    A(SP, "dma_start", out=k_[R, cs], in_=KR[R, cs], reads=[("KR", blk), kk], writes=[kk], dma=True)

                def st2():
                    A(ACT, "activation", out=lnF[0:96, :], in_=bank(6)[0:96, :], func=AF.Ln, bias=epsc[0:96, :], scale=1.0 / QKH,
                      reads=["ps6", "colc"], writes=["lnF"])
                    A(ACT, "activation", out=Fq[0:96, :], in_=lnF[0:96, :], func=AF.Exp, scale=-0.5, reads=["lnF"], writes=["Fq"])

                def st3():
                    A(DVE, "scalar_tensor_tensor", out=qn[0:96, :], in0=zsb[0:96, :], scalar=gq96[0:96, :], in1=Fq[0:96, :],
                      op0=ALU.mult, op1=ALU.mult, reads=["zsb", "Fq", "vec"], writes=["qn"])
                    A(DVE, "scalar_tensor_tensor", out=t2[R, :], in0=zssb[R, :], scalar=gqs96[R, :], in1=Fq[R, :],
                      op0=ALU.mult, op1=ALU.mult, reads=["zssb", "Fq", "vec"], writes=["t2"])
                    A(POOL, "tensor_copy", out=q_[0:64, cs], in_=qn[0:64, :], reads=["qn"], writes=[qk])
                    A(POOL, "tensor_tensor", out=t1[R, :], in0=qn[R, :], in1=COS[R, cs], op=ALU.mult,
                      reads=["qn", "COS"], writes=["t1"])
                    A(POOL, "tensor_tensor", out=t2[R, :], in0=t2[R, :], in1=SINS[R, cs], op=ALU.mult,
                      reads=["t2", "SINS"], writes=["t2"])
                    A(POOL, "tensor_tensor", out=q_[R, cs], in0=t1[R, :], in1=t2[R, :], op=ALU.add,
                      reads=["t1", "t2", qk], writes=[qk])
                return [st0, st1, st2, st3]

            NG = NQP * TT
            groups = [(h, qp, kt) for h in range(NH) for qp in range(NQP) for kt in range(TT)]
            gap = max(4, (NG - 1) // NBLK)
            assert NBLK * 4 <= NG, (NBLK, NG)

            def emit_qk(i):
                h, qp, kt = groups[i]
                q_ = QT[h % 2]
                k_ = KT[h % 2]
                sb_ = i % 2
                sbk = [0, 2][sb_]
                for j in ([0] * FILL_QK + list(range(NJ))):
                    q0 = qp * QW + j * 512
                    A(PE, "matmul", bank(sbk + j), lhsT=k_[0:96, kt * 128:(kt + 1) * 128], rhs=q_[0:96, q0:q0 + 512],
                      start=True, stop=True, reads=[("QT", h % 2, q0 // 512), ("KT", h % 2, kt // 4)], writes=["ps%d" % (sbk + j)])

            p0_list = [st for blk in range(NBLK) for st in produce_stages(0, blk)]
            for tt in range(TT):
                for k_ in range(len(p0_list)):
                    if (k_ * TT) // len(p0_list) == tt:
                        p0_list[k_]()
                ba, bb = (tt % 3) * 2, (tt % 3) * 2 + 1
                ksq = ksqs[tt % 2]
                A(PE, "matmul", bank(ba), lhsT=kvn[:, tt * 128:(tt + 1) * 128], rhs=wukv[:, 0:512], start=True, stop=True,
                  reads=[("kvn", tt // 4), "wukv"], writes=["ps%d" % ba])
                A(PE, "matmul", bank(bb), lhsT=kvn[:, tt * 128:(tt + 1) * 128], rhs=wukv[:, 512:1024], start=True, stop=True,
                  reads=[("kvn", tt // 4), "wukv"], writes=["ps%d" % bb])
                vsrc = bank(ba).rearrange("p (h d) -> p h d", h=NH)
                A(ACT, "activation", out=Vaug[:, tt, 0:NH:2, 0:64], in_=vsrc[:, 0:NH:2, :], func=AF.Copy,
                  reads=["ps%d" % ba], writes=[("V", tt, 0)])
                A(ACT, "activation", out=Vaug[:, tt, 1:NH:2, 64:128], in_=vsrc[:, 1:NH:2, :], func=AF.Copy,
                  reads=["ps%d" % ba], writes=[("V", tt, 1)])
                A(DVE, "memset", Vaug[:, tt, 0:NH:2, 64:128], 1.0, writes=[("V1", tt, 0)])
                A(DVE, "memset", Vaug[:, tt, 1:NH:2, 0:64], 1.0, writes=[("V1", tt, 1)])
                A(ACT, "activation", out=ksq, in_=bank(bb), func=AF.Square, reads=["ps%d" % bb], writes=[("ksq", tt % 2)])
                A(DVE, "tensor_reduce", out=kss[:, tt, :], in_=ksq.rearrange("p (h d) -> p h d", h=NH),
                                                        axis=AX.X, op=ALU.add, reads=[("ksq", tt % 2)], writes=[("kss", tt)])
            A(DVE, "tensor_tensor", out=kss, in0=kss, in1=krstat.unsqueeze(2).to_broadcast([128, TT, NH]), op=ALU.add,
              reads=[("kss", tt) for tt in range(TT)] + ["krstat"], writes=["kss_all"])
            A(ACT, "activation", out=kss, in_=kss, func=AF.Ln, bias=epsc, scale=1.0 / QKH, reads=["kss_all", "colc"], writes=["kss_ln"])
            A(ACT, "activation", out=sck, in_=kss, func=AF.Exp, bias=lnsc, scale=-0.5, reads=["kss_ln", "colc"], writes=["sck"])

            emit_qk(0)
            if len(groups) > 1:
                emit_qk(1)
            pending = []
            for i, (h, qp, kt) in enumerate(groups):
                sb_ = i % 2
                sbk = [0, 2][sb_]
                pb_ = i % 3
                A(ACT, "activation", out=PT[pb_], in_=ps[:, sbk * 512:sbk * 512 + QW], func=AF.Exp, bias=negB,
                  scale=sck[:, kt, h:h + 1], reads=["ps%d" % (sbk + j_) for j_ in range(NJ)] + ["sck", "negB"], writes=[("PT", pb_)])
                if i + 2 < len(groups):
                    emit_qk(i + 2)
                for j in range(NJ):
                    A(PE, "matmul", bank(4 + j), lhsT=Vaug[:, kt, h, :], rhs=PT[pb_][:, j * 512:(j + 1) * 512],
                      start=(kt == 0), stop=(kt == TT - 1), reads=[("PT", pb_), ("V", kt, h % 2), ("V1", kt, h % 2)], writes=["ps%d" % (4 + j)])
                if kt == TT - 1:
                    while pending:
                        pending.pop(0)[1]()
                    for j in range(NJ):
                        A(DVE, "tensor_copy", out=Osb[:, j * 512:(j + 1) * 512], in_=bank(4 + j), reads=["ps%d" % (4 + j)],
                          writes=[("Osb", j)])
                    cq0 = qp * QW
                    nr = slice(0, 64) if h % 2 == 0 else slice(64, 128)
                    dr = slice(64, 128) if h % 2 == 0 else slice(0, 64)
                    NCH = QW // 256

                    def mk_rec(c, nr=nr, dr=dr):
                        def f():
                            cc = slice(c * 256, (c + 1) * 256)
                            A(DVE, "reciprocal", out=rec[nr, cc], in_=Osb[dr, cc], reads=[("Osb", c // 2)], writes=[("rec", c)])
                        return f

                    def mk_tg(c, nr=nr, h=h, cq0=cq0):
                        def f():
                            cc = slice(c * 256, (c + 1) * 256)
                            A(POOL, "tensor_tensor", out=tg_[nr, cc], in0=rec[nr, cc], in1=sga[nr, h // 2, cq0 + c * 256:cq0 + (c + 1) * 256],
                              op=ALU.mult, reads=[("rec", c)] + [("sga", h // 2, bl) for bl in range(NBLK)], writes=[("tg", c)])
                        return f

                    def mk_fin(c, nr=nr, h=h, qp=qp, cq0=cq0):
                        def f():
                            cc = slice(c * 256, (c + 1) * 256)
                            A(DVE, "tensor_tensor", out=yaT[nr, h // 2, cq0 + c * 256:cq0 + (c + 1) * 256], in0=Osb[nr, cc], in1=tg_[nr, cc],
                              op=ALU.mult, reads=[("Osb", c // 2), ("tg", c)], writes=[("yaT", h, qp, c)])
                        return f
                    offs = [3, 4, 5, 10, 11, 12]
                    for c in range(NCH + 1):
                        fs = []
                        if c < NCH:
                            fs += [mk_rec(c), mk_tg(c)]
                        if c >= 1:
                            fs.append(mk_fin(c - 1))
                        for f in fs:
                            pending.append((i + offs[min(c, len(offs) - 1)], f))
                while pending and pending[0][0] <= i:
                    pending.pop(0)[1]()
                gi = qp * TT + kt
                if h + 1 < NH:
                    for blk in range(NBLK):
                        for si in range(4):
                            so = (0, 4, 6, 7)[si] if gap >= 7 else si
                            if min(blk * gap + so, NG - 3) == gi:
                                produce_stages(h + 1, blk)[si]()

            while pending:
                pending.pop(0)[1]()
            ckpt("ATT")
            Sd.barrier(lambda e: e.memset(small[:, 8:9], 0.0))
            for tt in range(TT):
                xb = xo[tt % 4]
                tb = to[tt % 2]
                A(SP, "dma_start", out=xb, in_=x_d[b, tt * 128:(tt + 1) * 128, :],
                  writes=[("xo", tt % 4)], dma=True)
                for n in range(2):
                    bk = (tt % 2) * 2 + n
                    for c in range(KC):
                        src = ypT[:, c, tt * 128:(tt + 1) * 128] if c < 4 else yaT[:, c - 4, tt * 128:(tt + 1) * 128]
                        rk = [("ypT", c, tt // 4)] if c < 4 else [("yaT", 2 * (c - 4) + hh_, tt * 128 // QW, (tt * 128 % QW) // 256) for hh_ in range(2)]
                        A(PE, "matmul", bank(bk), lhsT=src, rhs=Wo[:, c, n * 512:(n + 1) * 512],
                                                                          start=(c == 0), stop=(c == KC - 1),
                          reads=rk + [("Wo", c)], writes=["ps%d" % bk])
                    A(DVE, "tensor_tensor", out=tb[:, n * 512:(n + 1) * 512], in0=bank(bk),
                                                                        in1=gate_bc[b][:, n * 512:(n + 1) * 512], op=ALU.mult,
                      reads=["ps%d" % bk, ("gate_bc", b, 4), ("gate_bc", b, 5)], writes=[("to", tt % 2, n)])
                A(POOL, "tensor_tensor", out=tb, in0=tb, in1=xb, op=ALU.add,
                  reads=[("to", tt % 2, 0), ("to", tt % 2, 1), ("xo", tt % 4)], writes=[("to", tt % 2, 0), ("to", tt % 2, 1)])
                A(POOL, "dma_start", out=out_d[b, tt * 128:(tt + 1) * 128, :], in_=tb,
                  reads=[("to", tt % 2, 0), ("to", tt % 2, 1)], dma=True)

            Sd.barrier(lambda e: e.memset(small[:, 8:9], 0.0))

        run = Sd.emit(sems, dsems)
        with nc.Block() as block:
            @block.sync
            def _(e):
                run(SP, e)

            @block.tensor
            def _(e):
                run(PE, e)

            @block.scalar
            def _(e):
                run(ACT, e)

            @block.vector
            def _(e):
                run(DVE, e)

            @block.gpsimd
            def _(e):
                run(POOL, e)
    return nc


def make_in_maps(inputs, S, NB, n_cores):
    hc = host_consts(S)
    maps = []
    for core in range(n_cores):
        m = host_layout(inputs, S, NB, core)
        m.update(hc)
        maps.append(m)
    return maps


def kernel(**inputs):
    NB = BATCH // N_CORES
    nc = build(SEQ, NB)
    in_maps = make_in_maps(inputs, SEQ, NB, N_CORES)
    res = run_bass_kernel_spmd(nc, in_maps, core_ids=list(range(N_CORES)))
    out = np.concatenate([np.asarray(r["out"]) for r in res.results], axis=0)
    return out.astype(np.float32)
```
